# Optimizing a Trainium2 kernel written in Bass

```python
import math
import jax, jax.numpy as jnp
from jax import lax
import numpy as np

D_MODEL = 1024
BATCH = 32
SEQ = 256
DEPTH = 4
DEC_BATCH = 2
DEC_SEQ = 1024
PAST_LEN = 256

GRID_W = 64
N_MIXERS = 4
D_FF = 4 * D_MODEL
EPS = 1e-6
ROPE_BASE = 10000.0
CONV_W = 5
CHUNK = 64
Q_BLOCK = 128
DENSE_KEY_LIMIT = 2048

N_SSD = (DEPTH + 3) // 4
N_MLA = (DEPTH + 2) // 4
N_MLSTM = (DEPTH + 1) // 4
N_DIFF = DEPTH // 4

SSD_DI = 2 * D_MODEL
SSD_HEADDIM = 64
SSD_HEADS = SSD_DI // SSD_HEADDIM
SSD_GROUPS = 8
SSD_DSTATE = 128
SSD_CONV_CH = SSD_DI + 2 * SSD_GROUPS * SSD_DSTATE
SSD_IN = SSD_DI + SSD_CONV_CH + 2 * SSD_HEADS

MLA_HEADS = 16
MLA_Q_RANK = D_MODEL // 2
MLA_KV_RANK = D_MODEL // 4
MLA_D_NOPE = 64
MLA_D_ROPE = 32
MLA_D_V = 64
MLA_IN = MLA_Q_RANK + MLA_KV_RANK + MLA_D_ROPE

MLSTM_DI = 2 * D_MODEL
MLSTM_HEADS = 8
MLSTM_DQK = MLSTM_DI // MLSTM_HEADS // 2
MLSTM_DV = MLSTM_DI // MLSTM_HEADS
MLSTM_IN = 2 * MLSTM_DI + 4 * MLSTM_HEADS

DIFF_HEADS = 8
DIFF_D = D_MODEL // DIFF_HEADS // 2

kernel_name = 'hybrid_diffusion_ssd_mla_mlstm_diffattn_step'


def rmsnorm(x, g):
    xf = x.astype(jnp.float32)
    r = lax.rsqrt(jnp.mean(xf * xf, axis=-1, keepdims=True) + EPS)
    return (xf * r).astype(x.dtype) * g


def modulate(x, shift, scale):
    return x * (1.0 + scale) + shift


def centred_dwconv(x, w, b):
    L = x.shape[1]
    p = CONV_W // 2
    xp = jnp.pad(x, ((0, 0), (p, p), (0, 0)))
    out = b + xp[:, 0:L] * w[0]
    for k in range(1, CONV_W):
        out = out + xp[:, k:k + L] * w[k]
    return out


def axial_rope(x, pos_r, pos_c):
    d = x.shape[-1]
    half = d // 2
    nf = half // 2
    inv = ROPE_BASE ** (-jnp.arange(nf, dtype=jnp.float32) / nf)
    ang = jnp.concatenate([pos_r[:, None] * inv, pos_c[:, None] * inv], axis=-1)
    cos = jnp.cos(ang)[None, :, None, :].astype(x.dtype)
    sin = jnp.sin(ang)[None, :, None, :].astype(x.dtype)
    x1, x2 = x[..., :half], x[..., half:]
    return jnp.concatenate([x1 * cos - x2 * sin, x1 * sin + x2 * cos], axis=-1)


def attend(q, k, v, scale):
    def block(qb):
        s = jnp.einsum('bqhd,bkhd->bhqk', qb, k, preferred_element_type=jnp.float32) * scale
        p = jax.nn.softmax(s, axis=-1).astype(v.dtype)
        return jnp.einsum('bhqk,bkhd->bqhd', p, v)
    if k.shape[1] < DENSE_KEY_LIMIT:
        return block(q)
    Bsz, Tq, H, dq = q.shape
    qb = jnp.moveaxis(q.reshape(Bsz, Tq // Q_BLOCK, Q_BLOCK, H, dq), 1, 0)
    out = lax.map(block, qb)
    return jnp.moveaxis(out, 0, 1).reshape(Bsz, Tq, H, v.shape[-1])


def sqrelu_mlp(h, w1, w2):
    return jnp.square(jax.nn.relu(h @ w1)) @ w2


def ssd_scan(x, dt, A, Bm, Cm, s0):
    Bsz, L, H, P = x.shape
    G, N = Bm.shape[2], Bm.shape[3]
    R = H // G
    nc = L // CHUNK
    f32 = jnp.float32

    def chunks(a):
        return jnp.moveaxis(a.reshape((Bsz, nc, CHUNK) + a.shape[2:]), 1, 0)

    dt32 = dt.astype(f32)
    xdt = chunks((x.astype(f32) * dt32[..., None]).reshape(Bsz, L, G, R, P))
    a = chunks((dt32 * A).reshape(Bsz, L, G, R))
    Bc = chunks(Bm.astype(f32))
    Cc = chunks(Cm.astype(f32))
    mask = jnp.tril(jnp.ones((CHUNK, CHUNK), dtype=bool))[None, :, :, None, None]

    def step(S, inp):
        xdt_c, a_c, B_c, C_c = inp
        cum = jnp.cumsum(a_c, axis=1)
        decay = jnp.exp(jnp.where(mask, cum[:, :, None] - cum[:, None, :], -jnp.inf))
        cb = jnp.einsum('bign,bjgn->bijg', C_c, B_c)
        y = jnp.einsum('bijg,bijgr,bjgrp->bigrp', cb, decay, xdt_c)
        y = y + jnp.einsum('bign,bgrpn->bigrp', C_c, S) * jnp.exp(cum)[..., None]
        tot = cum[:, -1]
        S = jnp.exp(tot)[..., None, None] * S + jnp.einsum(
            'bjgn,bjgr,bjgrp->bgrpn', B_c, jnp.exp(tot[:, None] - cum), xdt_c)
        return S, y

    S, ys = lax.scan(step, s0.astype(f32).reshape(Bsz, G, R, P, N), (xdt, a, Bc, Cc))
    y = jnp.moveaxis(ys, 0, 1).reshape(Bsz, L, H, P).astype(x.dtype)
    return y, S.reshape(Bsz, H, P, N).astype(x.dtype)


def ssd_mixer(h, s0, w_in, conv_w, conv_b, dt_bias, A_log, D_skip, norm_g, w_out):
    Bsz, L, _ = h.shape
    proj = h @ w_in
    z = proj[..., :SSD_DI]
    xbc = jax.nn.silu(centred_dwconv(proj[..., SSD_DI:SSD_DI + SSD_CONV_CH], conv_w, conv_b))
    dt = jax.nn.softplus(proj[..., SSD_DI + SSD_CONV_CH:].reshape(Bsz, L, 2, SSD_HEADS) + dt_bias)
    gn = SSD_GROUPS * SSD_DSTATE
    xs = xbc[..., :SSD_DI].reshape(Bsz, L, SSD_HEADS, SSD_HEADDIM)
    Bm = xbc[..., SSD_DI:SSD_DI + gn].reshape(Bsz, L, SSD_GROUPS, SSD_DSTATE)
    Cm = xbc[..., SSD_DI + gn:].reshape(Bsz, L, SSD_GROUPS, SSD_DSTATE)
    A = -jnp.exp(A_log.astype(jnp.float32))
    rev = lambda t: jnp.flip(t, axis=1)
    yf, sf = ssd_scan(xs, dt[:, :, 0], A[0], Bm, Cm, s0[:, 0])
    yb, sb = ssd_scan(rev(xs), rev(dt[:, :, 1]), A[1], rev(Bm), rev(Cm), s0[:, 1])
    y = yf + rev(yb) + D_skip[:, None] * xs
    y = rmsnorm(y.reshape(Bsz, L, SSD_DI) * jax.nn.silu(z), norm_g)
    return y @ w_out, jnp.stack([sf, sb], axis=1)


def mla_mixer(h, ctx_ckv, ctx_kr, pos, w_in, q_norm_g, kv_norm_g, w_uq, w_ukv, w_o):
    Bsz, L, _ = h.shape
    proj = h @ w_in
    cq = rmsnorm(proj[..., :MLA_Q_RANK], q_norm_g)
    ckv = rmsnorm(proj[..., MLA_Q_RANK:MLA_Q_RANK + MLA_KV_RANK], kv_norm_g)
    kr = proj[..., MLA_Q_RANK + MLA_KV_RANK:]
    q = (cq @ w_uq).reshape(Bsz, L, MLA_HEADS, MLA_D_NOPE + MLA_D_ROPE)
    q_nope, q_rope = q[..., :MLA_D_NOPE], q[..., MLA_D_NOPE:]
    if pos is None:
        keys_ckv, keys_kr = ckv, kr
    else:
        q_rope = axial_rope(q_rope, *pos)
        keys_ckv = jnp.concatenate([ctx_ckv, ckv], axis=1)
        keys_kr = jnp.concatenate([ctx_kr, axial_rope(kr[:, :, None, :], *pos)[:, :, 0]], axis=1)
    Tk = keys_ckv.shape[1]
    kv = (keys_ckv @ w_ukv).reshape(Bsz, Tk, MLA_HEADS, MLA_D_NOPE + MLA_D_V)
    k = jnp.concatenate([kv[..., :MLA_D_NOPE],
                         jnp.broadcast_to(keys_kr[:, :, None, :], (Bsz, Tk, MLA_HEADS, MLA_D_ROPE))], axis=-1)
    o = attend(jnp.concatenate([q_nope, q_rope], axis=-1), k, kv[..., MLA_D_NOPE:],
               (MLA_D_NOPE + MLA_D_ROPE) ** -0.5)
    return o.reshape(Bsz, L, MLA_HEADS * MLA_D_V) @ w_o, (ckv, kr)


def mlstm_scan(q, k, v, logi, logf, C0, n0, m0):
    Bsz, L, H, dk = q.shape
    dv = v.shape[-1]
    nc = L // CHUNK
    f32 = jnp.float32

    def chunks(a):
        return jnp.moveaxis(a.astype(f32).reshape((Bsz, nc, CHUNK) + a.shape[2:]), 1, 0)

    mask = jnp.tril(jnp.ones((CHUNK, CHUNK), dtype=bool))[None, :, :, None]

    def step(carry, inp):
        C, n, m = carry
        q_c, k_c, v_c, i_c, f_c = inp
        b = jnp.cumsum(f_c, axis=1)
        dlog = jnp.where(mask, b[:, :, None, :] - b[:, None, :, :] + i_c[:, None, :, :], -jnp.inf)
        inter = b + m[:, None, :]
        m_comb = jnp.maximum(inter, jnp.max(dlog, axis=2))
        w = jnp.exp(dlog - m_comb[:, :, None, :])
        iw = jnp.exp(inter - m_comb)
        s = jnp.einsum('bihd,bjhd->bijh', q_c, k_c) * w
        num = jnp.einsum('bijh,bjhv->bihv', s, v_c) + iw[..., None] * jnp.einsum('bihd,bhdv->bihv', q_c, C)
        den = jnp.sum(s, axis=2) + iw * jnp.einsum('bihd,bhd->bih', q_c, n)
        h_c = num / jnp.maximum(jnp.abs(den), jnp.exp(-m_comb))[..., None]
        bQ = b[:, -1]
        wlog = bQ[:, None] - b + i_c
        m_new = jnp.maximum(bQ + m, jnp.max(wlog, axis=1))
        sw = jnp.exp(wlog - m_new[:, None])
        cw = jnp.exp(bQ + m - m_new)
        C = cw[..., None, None] * C + jnp.einsum('bjh,bjhd,bjhv->bhdv', sw, k_c, v_c)
        n = cw[..., None] * n + jnp.einsum('bjh,bjhd->bhd', sw, k_c)
        return (C, n, m_new), h_c

    (C, n, m), hs = lax.scan(step, (C0.astype(f32), n0.astype(f32), m0.astype(f32)),
                             (chunks(q), chunks(k), chunks(v), chunks(logi), chunks(logf)))
    hs = jnp.moveaxis(hs, 0, 1).reshape(Bsz, L, H, dv).astype(q.dtype)
    return hs, (C.astype(q.dtype), n.astype(q.dtype), m.astype(q.dtype))


def mlstm_mixer(h, C0, n0, m0, w_up, conv_w, conv_b, gate_b, w_q, w_k, w_v, skip, norm_g, w_down):
    Bsz, L, _ = h.shape
    proj = h @ w_up
    xm = proj[..., :MLSTM_DI]
    z = proj[..., MLSTM_DI:2 * MLSTM_DI]
    gates = proj[..., 2 * MLSTM_DI:].reshape(Bsz, L, 2, 2, MLSTM_HEADS) + gate_b
    xc = jax.nn.silu(centred_dwconv(xm, conv_w, conv_b))
    q = (xc @ w_q).reshape(Bsz, L, MLSTM_HEADS, MLSTM_DQK)
    k = (xc @ w_k).reshape(Bsz, L, MLSTM_HEADS, MLSTM_DQK) * (MLSTM_DQK ** -0.5)
    v = (xm @ w_v).reshape(Bsz, L, MLSTM_HEADS, MLSTM_DV)
    logi = gates[:, :, :, 0]
    logf = jax.nn.log_sigmoid(gates[:, :, :, 1])
    rev = lambda t: jnp.flip(t, axis=1)
    hf, sf = mlstm_scan(q, k, v, logi[:, :, 0], logf[:, :, 0], C0[:, 0], n0[:, 0], m0[:, 0])
    hb, sb = mlstm_scan(rev(q), rev(k), rev(v), rev(logi[:, :, 1]), rev(logf[:, :, 1]),
                        C0[:, 1], n0[:, 1], m0[:, 1])
    hsum = hf + rev(hb)
    hn = rmsnorm(hsum, norm_g.reshape(MLSTM_HEADS, MLSTM_DV)).reshape(Bsz, L, MLSTM_DI)
    y = (hn + skip * xc) * jax.nn.silu(z)
    states = (jnp.stack([sf[0], sb[0]], axis=1), jnp.stack([sf[1], sb[1]], axis=1),
              jnp.stack([sf[2], sb[2]], axis=1))
    return y @ w_down, states


def diff_mixer(h, ctx_k, ctx_v, pos, lam_init, w_qkv, lq1, lk1, lq2, lk2, subln_g, w_o):
    Bsz, L, _ = h.shape
    proj = h @ w_qkv
    q = proj[..., :D_MODEL].reshape(Bsz, L, 2 * DIFF_HEADS, DIFF_D)
    k = proj[..., D_MODEL:2 * D_MODEL].reshape(Bsz, L, 2 * DIFF_HEADS, DIFF_D)
    v = proj[..., 2 * D_MODEL:].reshape(Bsz, L, DIFF_HEADS, 2 * DIFF_D)
    k_own = k.reshape(Bsz, L, DIFF_HEADS, 2 * DIFF_D)
    if pos is None:
        keys, vals = k, v
    else:
        q = axial_rope(q, *pos)
        keys = jnp.concatenate([ctx_k.reshape(Bsz, ctx_k.shape[1], 2 * DIFF_HEADS, DIFF_D),
                                axial_rope(k, *pos)], axis=1)
        vals = jnp.concatenate([ctx_v, v], axis=1)
    Tk = keys.shape[1]
    q = q.reshape(Bsz, L, DIFF_HEADS, 2, DIFF_D)
    keys = keys.reshape(Bsz, Tk, DIFF_HEADS, 2, DIFF_D)
    lam = jnp.exp(jnp.sum(lq1 * lk1)) - jnp.exp(jnp.sum(lq2 * lk2)) + lam_init
    scale = DIFF_D ** -0.5
    o = attend(q[:, :, :, 0], keys[:, :, :, 0], vals, scale) - lam * attend(q[:, :, :, 1], keys[:, :, :, 1], vals, scale)
    o = rmsnorm(o, subln_g) * (1.0 - lam_init)
    return o.reshape(Bsz, L, D_MODEL) @ w_o, (k_own, v)


def setup_inputs(seed: int = 0) -> dict:
    key = jax.random.key(seed)
    keys = iter(jax.random.split(key, 96))
    f32 = jnp.float32
    D = D_MODEL

    def nrm(shape, scale=1.0):
        return jax.random.normal(next(keys), shape, f32) * scale

    def gain(shape):
        return 1.0 + nrm(shape, 0.02)

    dt0 = jnp.exp(jax.random.uniform(next(keys), (N_SSD, 2, SSD_HEADS), f32, math.log(1e-3), math.log(1e-1)))
    dt_bias = dt0 + jnp.log(-jnp.expm1(-dt0))
    a_log = jnp.log(jax.random.uniform(next(keys), (N_SSD, 2, SSD_HEADS), f32, 1.0, 16.0))
    f_bias = jnp.linspace(3.0, 6.0, MLSTM_HEADS, dtype=f32)
    gate_b = jnp.stack([nrm((N_MLSTM, 2, MLSTM_HEADS), 0.1),
                        f_bias + nrm((N_MLSTM, 2, MLSTM_HEADS), 0.1)], axis=2)
    return {
        'x_prompt': nrm((BATCH, SEQ, D)),
        'x_sample': nrm((DEC_BATCH, DEC_SEQ, D)),
        'state_ssd': nrm((DEC_BATCH, N_SSD, 2, SSD_HEADS, SSD_HEADDIM, SSD_DSTATE), 0.1),
        'cache_mla_ckv': nrm((DEC_BATCH, N_MLA, PAST_LEN, MLA_KV_RANK)),
        'cache_mla_krope': nrm((DEC_BATCH, N_MLA, PAST_LEN, MLA_D_ROPE)),
        'state_mlstm_C': nrm((DEC_BATCH, N_MLSTM, 2, MLSTM_HEADS, MLSTM_DQK, MLSTM_DV), 0.1),
        'state_mlstm_n': nrm((DEC_BATCH, N_MLSTM, 2, MLSTM_HEADS, MLSTM_DQK), 0.1),
        'state_mlstm_m': nrm((DEC_BATCH, N_MLSTM, 2, MLSTM_HEADS)),
        'cache_diff_k': nrm((DEC_BATCH, N_DIFF, PAST_LEN, DIFF_HEADS, 2 * DIFF_D)),
        'cache_diff_v': nrm((DEC_BATCH, N_DIFF, PAST_LEN, DIFF_HEADS, 2 * DIFF_D)),
        'c': nrm((DEC_BATCH, D)),
        'c_ctx': nrm((D,)),
        'norm1_g': gain((DEPTH, D)),
        'norm2_g': gain((DEPTH, D)),
        'ada_w': nrm((DEPTH, D, 6 * D), 0.5 * D ** -0.5),
        'ada_b': nrm((DEPTH, 6 * D), 0.02),
        'mlp_w1': nrm((DEPTH, D, D_FF), D ** -0.5),
        'mlp_w2': nrm((DEPTH, D_FF, D), D_FF ** -0.5),
        'final_g': gain((D,)),
        'ssd_w_in': nrm((N_SSD, D, SSD_IN), D ** -0.5),
        'ssd_conv_w': nrm((N_SSD, CONV_W, SSD_CONV_CH), CONV_W ** -0.5),
        'ssd_conv_b': nrm((N_SSD, SSD_CONV_CH), 0.02),
        'ssd_dt_bias': dt_bias,
        'ssd_A_log': a_log,
        'ssd_D': 1.0 + nrm((N_SSD, SSD_HEADS), 0.1),
        'ssd_norm_g': gain((N_SSD, SSD_DI)),
        'ssd_w_out': nrm((N_SSD, SSD_DI, D), SSD_DI ** -0.5),
        'mla_w_in': nrm((N_MLA, D, MLA_IN), D ** -0.5),
        'mla_q_norm_g': gain((N_MLA, MLA_Q_RANK)),
        'mla_kv_norm_g': gain((N_MLA, MLA_KV_RANK)),
        'mla_w_uq': nrm((N_MLA, MLA_Q_RANK, MLA_HEADS * (MLA_D_NOPE + MLA_D_ROPE)), MLA_Q_RANK ** -0.5),
        'mla_w_ukv': nrm((N_MLA, MLA_KV_RANK, MLA_HEADS * (MLA_D_NOPE + MLA_D_V)), MLA_KV_RANK ** -0.5),
        'mla_w_o': nrm((N_MLA, MLA_HEADS * MLA_D_V, D), (MLA_HEADS * MLA_D_V) ** -0.5),
        'mlstm_w_up': nrm((N_MLSTM, D, MLSTM_IN), D ** -0.5),
        'mlstm_conv_w': nrm((N_MLSTM, CONV_W, MLSTM_DI), CONV_W ** -0.5),
        'mlstm_conv_b': nrm((N_MLSTM, MLSTM_DI), 0.02),
        'mlstm_gate_b': gate_b,
        'mlstm_w_q': nrm((N_MLSTM, MLSTM_DI, MLSTM_HEADS * MLSTM_DQK), MLSTM_DI ** -0.5),
        'mlstm_w_k': nrm((N_MLSTM, MLSTM_DI, MLSTM_HEADS * MLSTM_DQK), MLSTM_DI ** -0.5),
        'mlstm_w_v': nrm((N_MLSTM, MLSTM_DI, MLSTM_DI), MLSTM_DI ** -0.5),
        'mlstm_skip': 1.0 + nrm((N_MLSTM, MLSTM_DI), 0.1),
        'mlstm_norm_g': gain((N_MLSTM, MLSTM_DI)),
        'mlstm_w_down': nrm((N_MLSTM, MLSTM_DI, D), MLSTM_DI ** -0.5),
        'diff_w_qkv': nrm((N_DIFF, D, 3 * D), D ** -0.5),
        'diff_lq1': nrm((N_DIFF, DIFF_D), 0.1),
        'diff_lk1': nrm((N_DIFF, DIFF_D), 0.1),
        'diff_lq2': nrm((N_DIFF, DIFF_D), 0.1),
        'diff_lk2': nrm((N_DIFF, DIFF_D), 0.1),
        'diff_subln_g': gain((N_DIFF, 2 * DIFF_D)),
        'diff_w_o': nrm((N_DIFF, D, D), D ** -0.5),
    }


def reference(x_prompt, x_sample, state_ssd, cache_mla_ckv, cache_mla_krope, state_mlstm_C, state_mlstm_n,
              state_mlstm_m, cache_diff_k, cache_diff_v, c, c_ctx, norm1_g, norm2_g, ada_w, ada_b, mlp_w1, mlp_w2,
              final_g, ssd_w_in, ssd_conv_w, ssd_conv_b, ssd_dt_bias, ssd_A_log, ssd_D, ssd_norm_g, ssd_w_out,
              mla_w_in, mla_q_norm_g, mla_kv_norm_g, mla_w_uq, mla_w_ukv, mla_w_o, mlstm_w_up, mlstm_conv_w,
              mlstm_conv_b, mlstm_gate_b, mlstm_w_q, mlstm_w_k, mlstm_w_v, mlstm_skip, mlstm_norm_g, mlstm_w_down,
              diff_w_qkv, diff_lq1, diff_lk1, diff_lq2, diff_lk2, diff_subln_g, diff_w_o):
    rows = x_sample.shape[1] // GRID_W
    pos = (jnp.repeat(jnp.arange(rows, dtype=jnp.float32), GRID_W),
           jnp.tile(jnp.arange(GRID_W, dtype=jnp.float32), rows))
    bp = x_prompt.shape[0]
    ada_p = jax.nn.silu(c_ctx)
    ada_s = jax.nn.silu(c)
    xp, xs = x_prompt, x_sample
    new_ssd, new_ckv, new_kr, new_C, new_n, new_m, new_dk, new_dv = [], [], [], [], [], [], [], []
    for i in range(DEPTH):
        kind, j = i % N_MIXERS, i // N_MIXERS
        mp = jnp.split(ada_p @ ada_w[i] + ada_b[i], 6, axis=-1)
        ms = jnp.split((ada_s @ ada_w[i] + ada_b[i])[:, None, :], 6, axis=-1)
        hp = modulate(rmsnorm(xp, norm1_g[i]), mp[0], mp[1])
        hs = modulate(rmsnorm(xs, norm1_g[i]), ms[0], ms[1])
        if kind == 0:
            prm = (ssd_w_in[j], ssd_conv_w[j], ssd_conv_b[j], ssd_dt_bias[j], ssd_A_log[j], ssd_D[j],
                   ssd_norm_g[j], ssd_w_out[j])
            s0 = jnp.zeros((bp, 2, SSD_HEADS, SSD_HEADDIM, SSD_DSTATE), hp.dtype)
            op, st = ssd_mixer(hp, s0, *prm)
            new_ssd.append(st)
            os_, _ = ssd_mixer(hs, state_ssd[:, j], *prm)
        elif kind == 1:
            prm = (mla_w_in[j], mla_q_norm_g[j], mla_kv_norm_g[j], mla_w_uq[j], mla_w_ukv[j], mla_w_o[j])
            op, (ckv, kr) = mla_mixer(hp, None, None, None, *prm)
            new_ckv.append(ckv)
            new_kr.append(kr)
            os_, _ = mla_mixer(hs, cache_mla_ckv[:, j], cache_mla_krope[:, j], pos, *prm)
        elif kind == 2:
            prm = (mlstm_w_up[j], mlstm_conv_w[j], mlstm_conv_b[j], mlstm_gate_b[j], mlstm_w_q[j], mlstm_w_k[j],
                   mlstm_w_v[j], mlstm_skip[j], mlstm_norm_g[j], mlstm_w_down[j])
            C0 = jnp.zeros((bp, 2, MLSTM_HEADS, MLSTM_DQK, MLSTM_DV), hp.dtype)
            n0 = jnp.zeros((bp, 2, MLSTM_HEADS, MLSTM_DQK), hp.dtype)
            m0 = jnp.zeros((bp, 2, MLSTM_HEADS), hp.dtype)
            op, (Cn, nn_, mn) = mlstm_mixer(hp, C0, n0, m0, *prm)
            new_C.append(Cn)
            new_n.append(nn_)
            new_m.append(mn)
            os_, _ = mlstm_mixer(hs, state_mlstm_C[:, j], state_mlstm_n[:, j], state_mlstm_m[:, j], *prm)
        else:
            lam_init = 0.8 - 0.6 * math.exp(-0.3 * i)
            prm = (lam_init, diff_w_qkv[j], diff_lq1[j], diff_lk1[j], diff_lq2[j], diff_lk2[j], diff_subln_g[j],
                   diff_w_o[j])
            op, (kk, vv) = diff_mixer(hp, None, None, None, *prm)
            new_dk.append(kk)
            new_dv.append(vv)
            os_, _ = diff_mixer(hs, cache_diff_k[:, j], cache_diff_v[:, j], pos, *prm)
        xp = xp + mp[2] * op
        xs = xs + ms[2] * os_
        hp = modulate(rmsnorm(xp, norm2_g[i]), mp[3], mp[4])
        hs = modulate(rmsnorm(xs, norm2_g[i]), ms[3], ms[4])
        xp = xp + mp[5] * sqrelu_mlp(hp, mlp_w1[i], mlp_w2[i])
        xs = xs + ms[5] * sqrelu_mlp(hs, mlp_w1[i], mlp_w2[i])
    y_prompt = rmsnorm(xp, final_g)
    y_sample = rmsnorm(xs, final_g)
    return (y_prompt, y_sample, jnp.stack(new_ssd, axis=1), jnp.stack(new_ckv, axis=1), jnp.stack(new_kr, axis=1),
            jnp.stack(new_C, axis=1), jnp.stack(new_n, axis=1), jnp.stack(new_m, axis=1),
            jnp.stack(new_dk, axis=1), jnp.stack(new_dv, axis=1))
```

```python
import math
from contextlib import ExitStack

import numpy as np
import concourse.bass as bass
import concourse.mybir as mybir
from concourse.bass_utils import run_bass_kernel_spmd

F32 = mybir.dt.float32
BF16 = mybir.dt.bfloat16
ALU = mybir.AluOpType
AF = mybir.ActivationFunctionType
AX = mybir.AxisListType

ENGS = ("pe", "act", "dve", "pool", "sp")

D = 1024
NPS = 4
LP = 256
LS = 1024
TU = 1024
TOK = 2048
DFF = 4096
EPS = 1e-6
N_CORES = 8
SSD_BIG = 20000.0


class Res:
    __slots__ = ("name", "lw", "rd", "excl")

    def __init__(self, name="", excl=False):
        self.name = name
        self.lw = None
        self.rd = {}
        self.excl = excl


class Op:
    __slots__ = ("fn", "waits", "sig", "lane", "lane_k")

    def __init__(self, fn):
        self.fn = fn
        self.waits = []
        self.sig = False
        self.lane = None
        self.lane_k = 0


class Rec:
    def __init__(self, nc, n_lanes_sp=8, n_lanes_pool=4, n_lanes_act=2):
        self.nc = nc
        self.ops = {e: [] for e in ENGS}
        self.waited = {e: {} for e in ENGS}
        self.lane_cnt = {}
        self.lanes = {"sp": [("dma", "sp", i) for i in range(n_lanes_sp)],
                      "pool": [("dma", "pool", i) for i in range(n_lanes_pool)],
                      "act": [("dma", "act", i) for i in range(n_lanes_act)]}
        self.lane_rr = {"sp": 0, "pool": 0, "act": 0}
        for q in self.lanes:
            for l in self.lanes[q]:
                self.lane_cnt[l] = 0

    def emit(self, eng, fn, reads=(), writes=(), dma=False):
        deps = {}

        def add(tok):
            if tok is None:
                return
            key, idx = tok
            if deps.get(key, -1) < idx:
                deps[key] = idx

        for r in reads:
            add(r.lw)
            if r.excl:
                for k, i in r.rd.items():
                    if k != ("eng", eng):
                        add((k, i))
        for w in writes:
            add(w.lw)
            for k, i in w.rd.items():
                add((k, i))
        op = Op(fn)
        if dma:
            lanes = self.lanes[eng]
            lane = lanes[self.lane_rr[eng] % len(lanes)]
            self.lane_rr[eng] += 1
            k = self.lane_cnt[lane]
            if k > 0:
                add((lane, k))
            self.lane_cnt[lane] = k + 1
            op.lane = lane
            op.lane_k = k + 1
            tok = (lane, k + 1)
        else:
            tok = (("eng", eng), len(self.ops[eng]))
        wd = self.waited[eng]
        for key, idx in deps.items():
            if key == ("eng", "pe") and eng == "pe" and not dma:
                continue
            if wd.get(key, -1) >= idx:
                continue
            wd[key] = idx
            op.waits.append((key, idx))
            if key[0] == "eng":
                self.ops[key[1]][idx].sig = True
        for r in reads:
            if r.rd.get(tok[0], -1) < tok[1]:
                r.rd[tok[0]] = tok[1]
        for w in writes:
            w.lw = tok
            w.rd = {}
        self.ops[eng].append(op)
        return tok

    def op(self, eng, method, reads, writes, **kw):
        self.emit(eng, lambda e: getattr(e, method)(**kw), reads, writes)

    def mm(self, out, lhsT, rhs, start, stop, reads, writes):
        self.emit("pe", lambda e: e.matmul(out, lhsT, rhs, start=start, stop=stop), reads, writes)

    def tr(self, out, in_, ident, reads, writes):
        self.emit("pe", lambda e: e.transpose(out, in_, ident), reads, writes)

    def dma(self, q, out, in_, reads, writes, **kw):
        self.emit(q, lambda e: e.dma_start(out=out, in_=in_, **kw), reads, writes, dma=True)

    def replay(self, stack):
        nc = self.nc
        sems = {}
        for e in ENGS:
            sems[("eng", e)] = stack.enter_context(nc.semaphore("s_" + e))
        for q in self.lanes:
            for l in self.lanes[q]:
                if self.lane_cnt[l] > 0:
                    sems[l] = stack.enter_context(nc.semaphore("l_%s%d" % (l[1], l[2])))
        rank = {}
        for e in ENGS:
            c = 0
            r = []
            for op in self.ops[e]:
                if op.sig:
                    c += 1
                r.append(c)
            rank[e] = r

        def val(key, idx):
            if key[0] == "eng":
                return rank[key[1]][idx]
            return 16 * idx

        block = stack.enter_context(nc.Block())
        ops = self.ops
        lanes = self.lanes
        lane_cnt = self.lane_cnt

        def run(eng_name, eng):
            for op in ops[eng_name]:
                for key, idx in op.waits:
                    eng.wait_ge(sems[key], val(key, idx))
                inst = op.fn(eng)
                if op.lane is not None:
                    inst.then_inc(sems[op.lane], 16)
                elif op.sig:
                    inst.then_inc(sems[("eng", eng_name)], 1)
            if eng_name in lanes:
                for l in lanes[eng_name]:
                    if lane_cnt[l] > 0:
                        eng.wait_ge(sems[l], 16 * lane_cnt[l])

        @block.tensor
        def _(e):
            run("pe", e)

        @block.scalar
        def _(e):
            run("act", e)

        @block.vector
        def _(e):
            run("dve", e)

        @block.gpsimd
        def _(e):
            run("pool", e)

        @block.sync
        def _(e):
            run("sp", e)

    def stats(self):
        return {e: (len(self.ops[e]), sum(1 for o in self.ops[e] if o.sig),
                    sum(len(o.waits) for o in self.ops[e])) for e in ENGS}


IN_SPECS = [
    ("xp", [NPS, LP, D]), ("xs", [LS, D]),
    ("st_ssd", [2, 32, 64, 128]), ("c_ckv", [256, 256]), ("c_kr", [256, 32]),
    ("st_C", [2, 8, 128, 256]), ("st_n", [2, 8, 128]), ("st_m", [2, 8]),
    ("c_dk", [256, 8, 128]), ("c_dv", [256, 8, 128]),
    ("c_s", [D]), ("c_ctx", [D]),
    ("norm1_g", [4, D]), ("norm2_g", [4, D]), ("ada_w", [4, D, 6 * D]), ("ada_b", [4, 6 * D]),
    ("mlp_w1", [4, D, DFF]), ("mlp_w2", [4, DFF, D]), ("final_g", [D]),
    ("ssd_w_in", [1, D, 6208]), ("ssd_conv_w", [1, 5, 4096]), ("ssd_conv_b", [1, 4096]),
    ("ssd_dt_bias", [1, 2, 32]), ("ssd_A_log", [1, 2, 32]), ("ssd_D", [1, 32]),
    ("ssd_norm_g", [1, 2048]), ("ssd_w_out", [1, 2048, D]),
    ("mla_w_in", [1, D, 800]), ("mla_q_norm_g", [1, 512]), ("mla_kv_norm_g", [1, 256]),
    ("mla_w_uq", [1, 512, 1536]), ("mla_w_ukv", [1, 256, 2048]), ("mla_w_o", [1, 1024, D]),
    ("mlstm_w_up", [1, D, 4128]), ("mlstm_conv_w", [1, 5, 2048]), ("mlstm_conv_b", [1, 2048]),
    ("mlstm_gate_b", [1, 2, 2, 8]), ("mlstm_w_q", [1, 2048, 1024]), ("mlstm_w_k", [1, 2048, 1024]),
    ("mlstm_w_v", [1, 2048, 2048]), ("mlstm_skip", [1, 2048]), ("mlstm_norm_g", [1, 2048]),
    ("mlstm_w_down", [1, 2048, D]),
    ("diff_w_qkv", [1, D, 3 * D]), ("diff_lq1", [1, 64]), ("diff_lk1", [1, 64]),
    ("diff_lq2", [1, 64]), ("diff_lk2", [1, 64]), ("diff_subln_g", [1, 128]), ("diff_w_o", [1, D, D]),
    ("rope_cs", [2, 32, LS]), ("rope128", [2, 128, LS]),
]
OUT_SPECS = [
    ("yp", [NPS, LP, D]), ("ys", [LS, D]),
    ("o_ssd", [NPS, 2, 32, 64, 128]), ("o_ckv", [NPS, LP, 256]), ("o_kr", [NPS, LP, 32]),
    ("o_C", [NPS, 2, 8, 128, 256]), ("o_n", [NPS, 2, 8, 128]), ("o_m", [NPS, 2, 8]),
    ("o_dk", [NPS, LP, 8, 128]), ("o_dv", [NPS, LP, 8, 128]),
]

ARENA_BYTES = 104 * 1024


class Builder:
    def __init__(self, depth=4, mixers=(0, 1, 2, 3), dbg=()):
        self.depth = depth
        self.mixers = set(mixers)
        self.dbg = dbg
        self.nc = bass.Bass("TRN2", target_bir_lowering=False)
        self.R = Rec(self.nc)
        self.dram = {}
        self.dbg_out = {}

    def sb(self, name, shape, dt):
        return self.st.enter_context(self.nc.sbuf_tensor(name, shape, dt))

    def av(self, off, shape, dt):
        esz = 2 if dt == BF16 else 4
        n = 1
        for s in shape[1:]:
            n *= s
        nb = n * esz
        assert off % 4 == 0 and off + nb <= ARENA_BYTES, (off, nb)
        ap = self.arena[0:shape[0], off // 2: (off + nb) // 2]
        if dt != BF16:
            ap = ap.bitcast(dt)
        if len(shape) == 2:
            return ap
        names = " ".join("d%d" % i for i in range(1, len(shape)))
        kw = {"d%d" % i: shape[i] for i in range(1, len(shape))}
        return ap.rearrange("p (%s) -> p %s" % (names, names), **kw)

    def bank(self, hold=False):
        while True:
            b = self.bank_i % 8
            self.bank_i += 1
            if b not in self.held:
                break
        if hold:
            self.held.add(b)
        return self.ps[b], self.ps_res[b]

    def release(self, ps):
        for b in range(8):
            if self.ps[b] is ps:
                self.held.discard(b)

    def stage(self, new_res):
        R = self.R
        allr = list(self.arena_live) + list(new_res)
        d = self.dummy
        R.emit("dve", lambda e: e.memset(d[:, 0:1], 0.0), [], allr + [self.r_dummy])
        self.arena_live = list(new_res)

    def evac_engine(self):
        self.ev_i += 1
        return "act" if self.ev_i % 2 == 0 else "dve"

    def copy(self, eng, out, in_, reads, writes):
        if eng == "act":
            self.R.op("act", "activation", reads, writes, out=out, in_=in_, func=AF.Identity)
        else:
            self.R.op(eng, "tensor_copy", reads, writes, out=out, in_=in_)

    def wload(self, src_ap, kc_n, ncols):
        assert kc_n * ncols <= 8192
        i = self.wb_i % len(self.wb)
        self.wb_i += 1
        view = self.wb[i][:, 0:kc_n * ncols].rearrange("p (kc n) -> p kc n", kc=kc_n)
        self.R.dma("pool", view, src_ap.rearrange("(kc p) n -> p kc n", p=128), [], [self.wb_res[i]])
        return view, self.wb_res[i]

    def debug_dump(self, name, ap, res, shape, dt=F32):
        if name not in self.dbg:
            return
        t = self.nc.dram_tensor("dbg_" + name, list(shape), dt, kind="ExternalOutput").ap()
        self.dbg_out[name] = list(shape)
        self.R.dma("sp", t, ap, res, [])

    def build(self):
        nc = self.nc
        R = self.R
        for name, shape in IN_SPECS:
            self.dram[name] = nc.dram_tensor(name, list(shape), F32, kind="ExternalInput").ap()
        for name, shape in OUT_SPECS:
            self.dram[name] = nc.dram_tensor(name, list(shape), F32, kind="ExternalOutput").ap()
        with ExitStack() as st:
            self.st = st
            self.xT = self.sb("xT", [128, 8, TOK], F32)
            self.r_xT = [[Res("xT%d_%d" % (dc, tt)) for tt in range(4)] for dc in range(8)]
            self.wb = [self.sb("wb%d" % i, [128, 8192], BF16) for i in range(2)]
            self.wb_res = [Res("wb%d" % i) for i in range(2)]
            self.wb_i = 0
            self.arena = self.sb("arena", [128, ARENA_BYTES // 2], BF16)
            self.arena_live = []
            self.identf = self.sb("identf", [128, 128], F32)
            self.identb = self.sb("identb", [128, 128], BF16)
            self.onesb = self.sb("onesb", [128, 128], BF16)
            self.onesf = self.sb("onesf", [128, 128], F32)
            self.prm = self.sb("prm", [128, 640], F32)
            self.mod = self.sb("mod", [128, 48, 2], F32)
            self.scT = self.sb("scT", [128, 8, 2], BF16)
            self.dummy = self.sb("sdummy", [128, 2], F32)
            self.mle = self.sb("mle", [128, 128], F32)
            self.mge = self.sb("mge", [128, 128], F32)
            self.sm64 = self.sb("sm64", [64, 4], F32)
            self.dbc = self.sb("dbc", [128, 32], F32)
            self.r_const = Res("const")
            self.r_prm = Res("prm")
            self.r_mod = Res("mod")
            self.r_scT = Res("scT")
            self.r_dummy = Res("dummy")
            self.ps = [st.enter_context(nc.psum_tensor("ps%d" % i, [128, 512], F32)) for i in range(8)]
            self.ps_res = [Res("ps%d" % i, excl=True) for i in range(8)]
            self.bank_i = 0
            self.held = set()
            self.pT_i = 0
            self.pending_fin = None
            self.ev_i = 0

            self.setup()
            self.debug_dump("xT0", self.xT[:], [r for rr in self.r_xT for r in rr], [128, 8, TOK])
            self.debug_dump("prm", self.prm[:], [self.r_prm], [128, 640])
            self.debug_dump("scT", self.scT[:], [self.r_scT], [128, 8, 2], BF16)
            for i in range(self.depth):
                self.adaln(i)
                if i == 0:
                    self.debug_dump("mod0", self.mod[:], [self.r_mod], [128, 48, 2])
                kind = i % 4
                if kind in self.mixers:
                    for u in range(2):
                        self.norm_mod(i, 0, [u * 2, u * 2 + 1], local=True)
                        [self.ssd, self.mla, self.mlstm, self.diff][kind](i, u)
                self.norm_mod(i, 1, [0, 1, 2, 3])
                if i == 0:
                    self.debug_dump("hT0", self.hT, self.r_h, [128, 8, TOK], BF16)
                self.mlp(i)
                if i == 0:
                    self.debug_dump("xT1", self.xT[:], [r for rr in self.r_xT for r in rr], [128, 8, TOK])
            self.final()
            self.stats = R.stats()
            R.replay(st)
        return nc

    def setup(self):
        R = self.R
        nc = self.nc
        dr = self.dram
        identf, identb, onesb, onesf = self.identf, self.identb, self.onesb, self.onesf
        rc = self.r_const
        R.emit("pool", lambda e: e.memset(identf[:], 0.0), [], [rc])
        R.emit("pool", lambda e: e.affine_select(out=identf[:], in_=identf[:], pattern=[[-1, 128]],
                                                 compare_op=ALU.not_equal, fill=1.0, base=0,
                                                 channel_multiplier=1), [rc], [rc])
        R.emit("dve", lambda e: e.tensor_copy(out=identb[:], in_=identf[:]), [rc], [rc])
        R.emit("dve", lambda e: e.memset(onesb[:], 1.0), [rc], [rc])
        R.emit("dve", lambda e: e.memset(onesf[:], 1.0), [rc], [rc])
        R.op("pool", "affine_select", [rc], [rc], out=self.mle[:], in_=onesf[:], pattern=[[1, 128]],
             compare_op=ALU.is_ge, fill=0.0, base=0, channel_multiplier=-1)
        R.op("pool", "affine_select", [rc], [rc], out=self.mge[:], in_=onesf[:], pattern=[[-1, 128]],
             compare_op=ALU.is_ge, fill=0.0, base=0, channel_multiplier=1)
        R.dma("sp", self.sm64[:, 0:1], dr["ssd_dt_bias"][0].rearrange("d (h o) -> (d h) o", o=1), [], [rc])
        R.dma("sp", self.sm64[:, 1:2], dr["ssd_A_log"][0].rearrange("d (h o) -> (d h) o", o=1), [], [rc])
        R.dma("sp", self.dbc[:], dr["ssd_D"][0].partition_broadcast(128), [], [rc])
        R.op("act", "activation", [rc], [rc], out=self.sm64[:, 2:3], in_=self.sm64[:, 1:2], func=AF.Exp)
        R.op("dve", "tensor_scalar", [rc], [rc], out=self.sm64[:, 2:3], in0=self.sm64[:, 2:3], scalar1=-1.0,
             scalar2=None, op0=ALU.mult)
        self.stage([])

        rows = [
            ("c_ctx", dr["c_ctx"].rearrange("(c p) -> c p", p=128)),
            ("c_s", dr["c_s"].rearrange("(c p) -> c p", p=128)),
            ("norm1_g", dr["norm1_g"].rearrange("l (c p) -> (l c) p", p=128)),
            ("norm2_g", dr["norm2_g"].rearrange("l (c p) -> (l c) p", p=128)),
            ("final_g", dr["final_g"].rearrange("(c p) -> c p", p=128)),
            ("ada_b", dr["ada_b"].rearrange("l (c p) -> (l c) p", p=128)),
            ("ssd_conv_w", dr["ssd_conv_w"][0].rearrange("k (c p) -> (k c) p", p=128)),
            ("ssd_conv_b", dr["ssd_conv_b"][0].rearrange("(c p) -> c p", p=128)),
            ("ssd_norm_g", dr["ssd_norm_g"][0].rearrange("(c p) -> c p", p=128)),
            ("mla_q_norm_g", dr["mla_q_norm_g"][0].rearrange("(c p) -> c p", p=128)),
            ("mla_kv_norm_g", dr["mla_kv_norm_g"][0].rearrange("(c p) -> c p", p=128)),
            ("mlstm_conv_w", dr["mlstm_conv_w"][0].rearrange("k (c p) -> (k c) p", p=128)),
            ("mlstm_conv_b", dr["mlstm_conv_b"][0].rearrange("(c p) -> c p", p=128)),
            ("mlstm_skip", dr["mlstm_skip"][0].rearrange("(c p) -> c p", p=128)),
            ("mlstm_norm_g", dr["mlstm_norm_g"][0].rearrange("(c p) -> c p", p=128)),
            ("diff_subln_g", dr["diff_subln_g"][0].rearrange("(c p) -> c p", p=128)),
        ]
        self.pcol = {}
        off = 0
        for name, ap in rows:
            self.pcol[name] = off
            off += ap.shape[0]
        total = off
        assert total <= 640
        ntile = (total + 127) // 128
        stg = [self.av(i * 512, [128, 128], F32) for i in range(ntile)]
        r_stg = [Res("stg%d" % i) for i in range(ntile)]
        self.stage(r_stg)
        for i in range(ntile):
            R.emit("dve", lambda e, i=i: e.memset(stg[i], 0.0), [], [r_stg[i]])
        for name, ap in rows:
            o = self.pcol[name]
            n = ap.shape[0]
            s = 0
            while s < n:
                t = (o + s) // 128
                p0 = (o + s) % 128
                m = min(n - s, 128 - p0)
                R.dma("sp", stg[t][p0:p0 + m, :], ap[s:s + m, :], [], [r_stg[t]])
                s += m
        for i in range(ntile):
            ps, rps = self.bank()
            R.tr(ps[:, 0:128], stg[i], identf[:], [r_stg[i], rc], [rps])
            w = min(128, total - i * 128)
            R.emit("dve", lambda e, i=i, ps=ps, w=w: e.tensor_copy(out=self.prm[:, i * 128:i * 128 + w],
                                                                  in_=ps[:, 0:w]), [rps], [self.r_prm])
        pc = self.pcol
        for j, nm in enumerate(["c_ctx", "c_s"]):
            R.emit("act", lambda e, j=j, nm=nm: e.activation(out=self.scT[:, :, j],
                                                            in_=self.prm[:, pc[nm]:pc[nm] + 8], func=AF.Silu),
                   [self.r_prm], [self.r_scT])

        nst = 4
        xst = [self.av(4096 + i * 4096, [128, D], F32) for i in range(nst)]
        r_xst = [Res("xst%d" % i) for i in range(nst)]
        self.stage(r_xst)
        for tt in range(16):
            if tt < 8:
                src = dr["xp"][tt // 2, (tt % 2) * 128:(tt % 2 + 1) * 128, :]
            else:
                src = dr["xs"][(tt - 8) * 128:(tt - 7) * 128, :]
            s = tt % nst
            R.dma("sp", xst[s], src, [], [r_xst[s]])
            for half in range(2):
                ps, rps = self.bank()
                for j in range(4):
                    dc = half * 4 + j
                    R.tr(ps[:, j * 128:(j + 1) * 128], xst[s][:, dc * 128:(dc + 1) * 128], identf[:],
                         [r_xst[s], rc], [rps])
                self.copy(self.evac_engine(), self.xT[:, half * 4:(half + 1) * 4, tt * 128:(tt + 1) * 128],
                          ps[:].rearrange("p (j t) -> p j t", j=4), [rps],
                          [self.r_xT[half * 4 + j][tt // 4] for j in range(4)])

    def adaln(self, i):
        R = self.R
        dr = self.dram
        pc = self.pcol
        ps, rps = self.bank(hold=True)
        for blk in range(6):
            wv, rw = self.wload(dr["ada_w"][i][:, blk * 1024:(blk + 1) * 1024], 8, 1024)
            for f in range(8):
                fc = blk * 8 + f
                for kc in range(8):
                    R.mm(ps[:, fc * 2:fc * 2 + 2], wv[:, kc, f * 128:(f + 1) * 128], self.scT[:, kc, :],
                         kc == 0, kc == 7, [rw, self.r_scT], [rps])
        mod = self.mod
        ab = self.prm[:, pc["ada_b"] + i * 48: pc["ada_b"] + (i + 1) * 48]
        R.emit("dve", lambda e: e.tensor_tensor(out=mod[:], in0=ps[:, 0:96].rearrange("p (c u) -> p c u", u=2),
                                                in1=ab.unsqueeze(2).to_broadcast([128, 48, 2]), op=ALU.add),
               [rps, self.r_prm], [self.r_mod])
        self.release(ps)
        for (sc0, gname) in ((8, "norm1_g"), (32, "norm2_g")):
            g = self.prm[:, pc[gname] + i * 8: pc[gname] + (i + 1) * 8]
            R.emit("dve", lambda e, sc0=sc0: e.tensor_scalar(out=mod[:, sc0:sc0 + 8, :], in0=mod[:, sc0:sc0 + 8, :],
                                                             scalar1=1.0, scalar2=None, op0=ALU.add),
                   [self.r_mod], [self.r_mod])
            R.emit("dve", lambda e, sc0=sc0, g=g: e.tensor_tensor(out=mod[:, sc0:sc0 + 8, :],
                                                                  in0=mod[:, sc0:sc0 + 8, :],
                                                                  in1=g.unsqueeze(2).to_broadcast([128, 8, 2]),
                                                                  op=ALU.mult),
                   [self.r_mod, self.r_prm], [self.r_mod])

    def rstd_tile(self, tt, sq, r_sq, lnt, rstd, r_tmp):
        R = self.R
        ts = slice(tt * 512, (tt + 1) * 512)
        R.op("act", "activation", [self.r_xT[dc][tt] for dc in range(8)], [r_sq],
             out=sq, in_=self.xT[:, :, ts], func=AF.Square)
        ps, rps = self.bank()
        for dc in range(8):
            R.mm(ps[:], self.onesb[:], sq[:, dc, :], dc == 0, dc == 7, [r_sq, self.r_const], [rps])
        R.op("act", "activation", [rps, self.r_const], [r_tmp],
             out=lnt, in_=ps[:], func=AF.Ln, scale=1.0 / D, bias=EPS)
        R.op("act", "activation", [r_tmp], [r_tmp], out=rstd, in_=lnt, func=AF.Exp, scale=-0.5)

    def norm_mod(self, i, which, tts, local=False):
        R = self.R
        if local:
            hT = self.av(0, [128, 8, TU], BF16)
        else:
            hT = self.av(0, [128, 8, TOK], BF16)
        r_h = [Res("hT%d" % tt) for tt in range(4)]
        base = 65536
        sq = [self.av(base + k * 8192, [128, 8, 512], BF16) for k in range(2)]
        lnt = [self.av(base + 16384 + k * 2048, [128, 512], F32) for k in range(2)]
        rstd = [self.av(base + 20480 + k * 2048, [128, 512], F32) for k in range(2)]
        tmp = [self.av(base + 24576 + k * 2048, [128, 512], F32) for k in range(2)]
        r_sq = [Res("sq0"), Res("sq1")]
        r_tmp = [Res("nt0"), Res("nt1")]
        r_t2 = [Res("t2a"), Res("t2b")]
        self.stage(r_h + r_sq + r_tmp + r_t2)
        self.hT, self.r_h = hT, r_h
        sh0 = 0 if which == 0 else 24
        sc0 = 8 if which == 0 else 32

        def stats(j):
            k = j % 2
            self.rstd_tile(tts[j], sq[k], r_sq[k], lnt[k], rstd[k], r_tmp[k])

        stats(0)
        for j, tt in enumerate(tts):
            if j + 1 < len(tts):
                stats(j + 1)
            kk = j % 2
            u = tt // 2
            ts = slice(tt * 512, (tt + 1) * 512)
            for dc in range(8):
                k = dc % 2
                R.op("dve", "tensor_tensor", [self.r_xT[dc][tt], r_tmp[kk]], [r_t2[k]],
                     out=tmp[k], in0=self.xT[:, dc, ts], in1=rstd[kk], op=ALU.mult)
                ots = slice((tt % 2) * 512, (tt % 2 + 1) * 512) if local else ts
                R.op("act", "activation", [r_t2[k], self.r_mod], [r_h[tt]],
                     out=hT[:, dc, ots], in_=tmp[k], func=AF.Identity,
                     scale=self.mod[:, sc0 + dc, u:u + 1], bias=self.mod[:, sh0 + dc, u:u + 1])

    def mlp(self, i):
        R = self.R
        dr = self.dram
        hT, r_h = self.hT, self.r_h
        hid = self.av(32768, [128, 8, TOK], BF16)
        r_hid = [[Res("hid%d_%d" % (f, tt)) for tt in range(4)] for f in range(8)]
        rl = [self.av(65536 + k * 2048, [128, 512], F32) for k in range(2)]
        r_rl = [Res("rl0"), Res("rl1")]
        self.stage(r_h + [r for rr in r_hid for r in rr] + r_rl)
        k = 0
        for fg in range(4):
            w1, rw1 = self.wload(dr["mlp_w1"][i][:, fg * 1024:(fg + 1) * 1024], 8, 1024)
            for f in range(8):
                for tt in range(4):
                    ts = slice(tt * 512, (tt + 1) * 512)
                    ps, rps = self.bank()
                    for kc in range(8):
                        R.mm(ps[:], w1[:, kc, f * 128:(f + 1) * 128], hT[:, kc, ts], kc == 0, kc == 7,
                             [rw1, r_h[tt]], [rps])
                    kk = k % 2
                    k += 1
                    R.op("act", "activation", [rps], [r_rl[kk]], out=rl[kk], in_=ps[:], func=AF.Relu)
                    R.op("dve", "tensor_tensor", [r_rl[kk]], [r_hid[f][tt]],
                         out=hid[:, f, ts], in0=rl[kk], in1=rl[kk], op=ALU.mult)
            w2, rw2 = self.wload(dr["mlp_w2"][i][fg * 1024:(fg + 1) * 1024, :], 8, 1024)
            for tt in range(4):
                u = tt // 2
                ts = slice(tt * 512, (tt + 1) * 512)
                for dc in range(8):
                    ps, rps = self.bank()
                    for f in range(8):
                        R.mm(ps[:], w2[:, f, dc * 128:(dc + 1) * 128], hid[:, f, ts], f == 0, f == 7,
                             [rw2, r_hid[f][tt]], [rps])
                    R.op("dve", "scalar_tensor_tensor", [rps, self.r_mod, self.r_xT[dc][tt]], [self.r_xT[dc][tt]],
                         out=self.xT[:, dc, ts], in0=ps[:], scalar=self.mod[:, 40 + dc, u:u + 1],
                         in1=self.xT[:, dc, ts], op0=ALU.mult, op1=ALU.add)

    def final(self):
        R = self.R
        dr = self.dram
        pc = self.pcol
        sq = self.av(65536, [128, 8, 512], BF16)
        lnt = self.av(65536 + 8192, [128, 512], F32)
        rstd = self.av(65536 + 8192 + 2048, [128, 512], F32)
        yT = self.av(0, [128, 8, 512], F32)
        yo = [self.av(16384 + k * 4096, [128, D], F32) for k in range(4)]
        r_sq, r_tmp = Res("sq"), Res("nt")
        r_yT = [Res("yT%d" % dc) for dc in range(8)]
        r_yo = [Res("yo%d" % k) for k in range(4)]
        self.stage([r_sq, r_tmp] + r_yT + r_yo)
        ko = 0
        for tt in range(4):
            ts = slice(tt * 512, (tt + 1) * 512)
            self.rstd_tile(tt, sq, r_sq, lnt, rstd, r_tmp)
            for dc in range(8):
                g = self.prm[:, pc["final_g"] + dc: pc["final_g"] + dc + 1]
                R.op("dve", "scalar_tensor_tensor", [self.r_xT[dc][tt], r_tmp, self.r_prm], [r_yT[dc]],
                     out=yT[:, dc, :], in0=self.xT[:, dc, ts], scalar=g, in1=rstd, op0=ALU.mult, op1=ALU.mult)
            for n in range(4):
                k = ko % 4
                ko += 1
                for half in range(2):
                    ps, rps = self.bank()
                    for j in range(4):
                        dc = half * 4 + j
                        R.tr(ps[:, j * 128:(j + 1) * 128], yT[:, dc, n * 128:(n + 1) * 128], self.identf[:],
                             [r_yT[dc], self.r_const], [rps])
                    self.copy(self.evac_engine(), yo[k][:, half * 512:(half + 1) * 512], ps[:], [rps], [r_yo[k]])
                tok0 = tt * 512 + n * 128
                if tok0 < TU:
                    dst = dr["yp"][tok0 // LP, (tok0 % LP):(tok0 % LP) + 128, :]
                else:
                    dst = dr["ys"][tok0 - TU: tok0 - TU + 128, :]
                R.dma("sp", dst, yo[k], [r_yo[k]], [])

    class _Alloc:
        def __init__(self, b, off):
            self.b = b
            self.off = off

        def __call__(self, shape, dt):
            esz = 2 if dt == BF16 else 4
            n = 1
            for x in shape[1:]:
                n *= x
            nb = (n * esz + 3) // 4 * 4
            v = self.b.av(self.off, shape, dt)
            self.off += nb
            return v

    def wload_multi(self, parts, kc_n):
        tot = sum(p.shape[1] for p in parts)
        assert kc_n * tot <= 8192
        i = self.wb_i % len(self.wb)
        self.wb_i += 1
        view = self.wb[i][:, 0:kc_n * tot].rearrange("p (kc n) -> p kc n", kc=kc_n)
        c0 = 0
        for p in parts:
            n = p.shape[1]
            self.R.dma("pool", view[:, :, c0:c0 + n], p.rearrange("(kc p) n -> p kc n", p=128), [], [self.wb_res[i]])
            c0 += n
        return view, self.wb_res[i]

    def ssd(self, i, u):
        R = self.R
        dr = self.dram
        pc = self.pcol
        hT, r_h = self.hT, self.r_h
        r_hl = [r_h[u * 2], r_h[u * 2 + 1]]
        nseq, L = (NPS, LP) if u == 0 else (1, LS)
        cps = L // 128
        w_in = dr["ssd_w_in"][0]
        identf, identb, onesf = self.identf, self.identb, self.onesf
        rc = self.r_const
        bc = lambda ap, shape: ap.to_broadcast(shape)

        A = self._Alloc(self, 16384)
        E_tok = A([128, 8, 64], F32)
        lnEb = A([128, 8, 64], F32)
        dO = A([128, 8, 64], F32)
        wS = A([128, 8, 64], F32)
        dec = A([128, 8, 64], F32)
        ss = A([128, 8, 8], F32)
        yg = A([128, 8, 2048], BF16)
        mark = A.off
        r_tokq, r_dec, r_ss = Res("tokq"), Res("dec"), Res("ss")
        r_yg = [Res("yg%d" % c) for c in range(8)]
        persist = [r_tokq, r_dec, r_ss] + r_yg

        dtT = A([64, 1024], F32)
        xx = A([64, 1024], F32)
        t1 = A([64, 1024], F32)
        t2 = A([64, 1024], F32)
        ET = A([64, 1024], F32)
        cm = A([64, 1024], F32)
        xd = A([64, 8, 64], F32)
        r_a = Res("ssdA")
        self.stage(r_hl + persist + [r_a])
        wdt, rwdt = self.wload(w_in[:, 6144:6208], 8, 64)
        for tt in range(2):
            ps, rps = self.bank()
            for kc in range(8):
                R.mm(ps[0:64, :], wdt[:, kc, :], hT[:, kc, tt * 512:(tt + 1) * 512], kc == 0, kc == 7,
                     [rwdt, r_hl[tt]], [rps])
            R.op("dve", "tensor_scalar", [rps, rc], [r_a], out=xx[:, tt * 512:(tt + 1) * 512], in0=ps[0:64, :],
                 scalar1=self.sm64[:, 0:1], scalar2=None, op0=ALU.add)
        RA = [r_a]
        R.op("act", "activation", RA, RA, out=t1, in_=xx, func=AF.Abs)
        R.op("act", "activation", RA, RA, out=t1, in_=t1, func=AF.Exp, scale=-1.0)
        R.op("act", "activation", RA, RA, out=t1, in_=t1, func=AF.Ln, bias=1.0)
        R.op("dve", "scalar_tensor_tensor", RA, RA, out=dtT, in0=xx, scalar=0.0, in1=t1, op0=ALU.max, op1=ALU.add)
        R.op("act", "activation", RA, RA, out=t2, in_=dtT, func=AF.Ln)
        R.op("dve", "tensor_scalar", RA + [rc], RA, out=xx, in0=dtT, scalar1=self.sm64[:, 2:3], scalar2=None,
             op0=ALU.mult)
        cm3 = cm.rearrange("p (c t) -> p c t", t=128)
        R.op("dve", "memset", [], RA, ap=cm, constant=1.0)
        R.op("dve", "memset", RA, RA, ap=cm3[:, :, 0:1], constant=0.0)
        R.op("dve", "tensor_tensor_scan", RA, RA, out=t1, data0=cm, data1=xx, initial=0.0, op0=ALU.mult, op1=ALU.add)
        cum3 = t1.rearrange("p (c t) -> p c t", t=128)
        tot = cum3[:, :, 127:128]
        ET3 = ET.rearrange("p (c t) -> p c t", t=128)
        xx3 = xx.rearrange("p (c t) -> p c t", t=128)
        R.op("dve", "tensor_copy", RA, RA, out=ET[0:32, :], in_=t1[0:32, :])
        R.op("dve", "tensor_tensor", RA, RA, out=ET3[32:64], in0=bc(tot[32:64], [32, 8, 128]), in1=cum3[32:64],
             op=ALU.subtract)
        R.op("dve", "tensor_tensor", RA, RA, out=ET[32:64, :], in0=ET[32:64, :], in1=xx[32:64, :], op=ALU.add)
        R.op("dve", "tensor_tensor", RA + [rc], RA, out=xd, in0=bc(identf[0:64, 0:64].unsqueeze(1), [64, 8, 64]),
             in1=bc(tot, [64, 8, 64]), op=ALU.mult)
        ps, rps = self.bank()
        R.mm(ps[:, 0:512], onesf[0:64, :], xd.rearrange("p c h -> p (c h)"), True, True, RA + [rc], [rps])
        R.op("act", "activation", [rps], [r_dec], out=dec.rearrange("p c h -> p (c h)"), in_=ps[:, 0:512], func=AF.Exp)
        cmv = cm.rearrange("p (c t) -> p c t", t=128)
        R.op("dve", "tensor_tensor", RA, RA, out=cmv, in0=bc(tot, [64, 8, 128]), in1=ET3, op=ALU.subtract)
        R.op("dve", "tensor_tensor", RA, RA, out=cm, in0=cm, in1=t2, op=ALU.add)
        R.op("dve", "tensor_tensor", RA, RA, out=xx, in0=t2, in1=ET, op=ALU.subtract)
        for c in range(8):
            cs = slice(c * 128, (c + 1) * 128)
            ps, rps = self.bank()
            R.tr(ps[:, 0:64], ET[:, cs], identf[0:64, 0:64], RA + [rc], [rps])
            R.tr(ps[:, 64:128], xx[:, cs], identf[0:64, 0:64], RA + [rc], [rps])
            R.tr(ps[:, 128:192], cm[:, cs], identf[0:64, 0:64], RA + [rc], [rps])
            R.op("dve", "tensor_scalar", [rps], [r_tokq], out=E_tok[:, c, :], in0=ps[:, 0:64], scalar1=SSD_BIG, scalar2=None,
                 op0=ALU.add)
            R.op("dve", "tensor_scalar", [rps], [r_tokq], out=lnEb[:, c, :], in0=ps[:, 64:128], scalar1=-SSD_BIG,
                 scalar2=None, op0=ALU.add)
            R.op("act", "activation", [rps], [r_tokq], out=dO[:, c, :], in_=ps[:, 0:64], func=AF.Exp)
            R.op("act", "activation", [rps], [r_tokq], out=wS[:, c, :], in_=ps[:, 128:192], func=AF.Exp)
        R.op("dve", "memset", [], [r_ss], ap=ss, constant=0.0)

        A.off = mark
        pre = A([128, nseq, L + 4], F32)
        acc = A([128, 1024], F32)
        xsT = A([128, 2, 1024], BF16)
        BT = A([128, 1024], BF16)
        CT = A([128, 1024], BF16)
        xb_tok = A([128, 8, 384], BF16)
        Tbf = A([128, 8, 256], BF16)
        CBm = [A([128, 128], F32) for _ in range(2)]
        X4 = [A([128, 4, 128], F32), xsT[:, 0, :].bitcast(F32).rearrange("p (h t) -> p h t", h=4)]
        L4 = [A([128, 4, 128], F32), xsT[:, 1, :].bitcast(F32).rearrange("p (h t) -> p h t", h=4)]
        M4 = [[A([128, 4, 128], BF16) for _ in range(2)] for _ in range(2)]
        xw = [A([128, 4, 64], BF16) for _ in range(2)]
        Sst = [A([128, 4, 64], F32) for _ in range(2)]
        Sbf = A([128, 256], BF16)
        zs = [A([128, 256], BF16) for _ in range(2)]
        xD = [A([128, 4, 64], BF16) for _ in range(2)]
        tc1 = A([128, 4, 64], F32)
        tc2 = A([128, 4, 64], F32)
        ost = A([128, 2, 128], F32)
        sqj = A([128, 256], BF16)
        r_pre, r_acc = Res("pre"), Res("acc")
        r_fm = [Res("xsT0"), Res("xsT1"), Res("BT"), Res("CT")]
        r_xb = [Res("xb%d" % c) for c in range(8)]
        r_Tbf = [Res("Tbf%d" % c) for c in range(8)]
        r_CB = [Res("CB0"), Res("CB1")]
        r_X4 = [Res("X4"), r_fm[0]]
        r_L4 = [Res("L4"), r_fm[1]]
        r_zs2 = [Res("zs0"), Res("zs1")]
        r_xD2 = [Res("xD0"), Res("xD1")]
        r_M4 = [[Res("M4a0"), Res("M4b0")], [Res("M4a1"), Res("M4b1")]]
        r_xw = [Res("xwa"), Res("xwb")]
        r_S = [Res("Sf"), Res("Sb")]
        r_Sbf, r_zs, r_xD, r_tc1, r_tc2, r_ost, r_sqj = (Res("Sbf"), Res("zs"), Res("xD"), Res("tc1"), Res("tc2"),
                                                         Res("ost"), Res("sqj"))
        grp_res = ([r_pre, r_acc] + r_fm + r_xb + r_Tbf + r_CB + [r_X4[0], r_L4[0]] + r_M4[0] + r_M4[1] + r_xw + r_S + r_zs2 + r_xD2 +
                   [r_Sbf, r_zs, r_xD, r_tc1, r_tc2, r_ost, r_sqj])
        self.stage(r_hl + persist + grp_res)
        R.op("dve", "memset", [], [r_pre], ap=pre, constant=0.0)
        fm_dst = [xsT[:, 0, :], xsT[:, 1, :], BT, CT]

        def gweights(g):
            return self.wload_multi([w_in[:, g * 256:(g + 1) * 256], w_in[:, 2048 + g * 256:2048 + (g + 1) * 256],
                                     w_in[:, 4096 + g * 128:4096 + (g + 1) * 128],
                                     w_in[:, 5120 + g * 128:5120 + (g + 1) * 128]], 8)

        def state_update(c, d, g):
            h0 = d * 32 + g * 4
            k = d
            R.op("dve", "tensor_tensor", [r_xb[c], r_tokq], [r_xw[k]], out=xw[k],
                 in0=xb_tok[:, c, 0:256].rearrange("p (h q) -> p h q", h=4),
                 in1=bc(wS[:, c, h0:h0 + 4].unsqueeze(2), [128, 4, 64]), op=ALU.mult)
            ps, rps = self.bank()
            R.mm(ps[:, 0:256], xb_tok[:, c, 256:384], xw[k].rearrange("p h q -> p (h q)"), True, True,
                 [r_xb[c], r_xw[k]], [rps])
            R.op("dve", "tensor_tensor", [r_S[d], r_dec], [r_S[d]], out=Sst[d], in0=Sst[d],
                 in1=bc(dec[:, c, h0:h0 + 4].unsqueeze(2), [128, 4, 64]), op=ALU.mult)
            R.op("dve", "tensor_tensor", [r_S[d], rps], [r_S[d]], out=Sst[d],
                 in0=ps[:, 0:256].rearrange("p (h q) -> p h q", h=4), in1=Sst[d], op=ALU.add)

        def state_init(d, g):
            if u == 0:
                R.op("dve", "memset", [], [r_S[d]], ap=Sst[d], constant=0.0)
            else:
                src = dr["st_ssd"][d, g * 4:(g + 1) * 4].rearrange("(blk h2) p n -> (h2 p) blk n", blk=2)
                R.dma("sp", ost, src, [], [r_ost])
                ps, rps = self.bank()
                for blk in range(2):
                    R.tr(ps[:, blk * 128:(blk + 1) * 128], ost[:, blk, :], identf[:], [r_ost, rc], [rps])
                R.op("dve", "tensor_copy", [rps], [r_S[d]], out=Sst[d].rearrange("p h q -> p (h q)"), in_=ps[:, 0:256])

        def state_out(seq, d, g):
            ps, rps = self.bank()
            S2 = Sst[d].rearrange("p h q -> p (h q)")
            for blk in range(2):
                R.tr(ps[:, blk * 128:(blk + 1) * 128], S2[:, blk * 128:(blk + 1) * 128], identf[:], [r_S[d], rc], [rps])
            R.op("dve", "tensor_copy", [rps], [r_ost], out=ost.rearrange("p b n -> p (b n)"), in_=ps[:, 0:256])
            dst = dr["o_ssd"][seq, d, g * 4:(g + 1) * 4].rearrange("(blk h2) p n -> (h2 p) blk n", blk=2)
            R.dma("sp", dst, ost, [r_ost], [])

        wnext = gweights(0)
        for g in range(8):
            W, rW = wnext
            if g < 7:
                wnext = gweights(g + 1)
            for ci, (col0, cch) in enumerate([(256, g * 2), (384, g * 2 + 1), (512, 16 + g), (640, 24 + g)]):
                for tt in range(2):
                    ps, rps = self.bank()
                    for kc in range(8):
                        R.mm(ps[:], W[:, kc, col0:col0 + 128], hT[:, kc, tt * 512:(tt + 1) * 512], kc == 0, kc == 7,
                             [rW, r_hl[tt]], [rps])
                    if u == 0:
                        self.copy("act", pre[:, 2 * tt:2 * tt + 2, 2:2 + L],
                                  ps[:].rearrange("p (s t) -> p s t", s=2), [rps], [r_pre])
                    else:
                        self.copy("act", pre[:, 0, 2 + tt * 512:2 + (tt + 1) * 512], ps[:], [rps], [r_pre])
                acc3 = acc.rearrange("p (s t) -> p s t", s=nseq)
                cw = pc["ssd_conv_w"]
                R.op("dve", "tensor_scalar", [r_pre, self.r_prm], [r_acc], out=acc3, in0=pre[:, :, 0:L],
                     scalar1=self.prm[:, cw + cch:cw + cch + 1],
                     scalar2=self.prm[:, pc["ssd_conv_b"] + cch:pc["ssd_conv_b"] + cch + 1], op0=ALU.mult, op1=ALU.add)
                for k in range(1, 5):
                    R.op("dve", "scalar_tensor_tensor", [r_pre, self.r_prm, r_acc], [r_acc], out=acc3,
                         in0=pre[:, :, k:k + L], scalar=self.prm[:, cw + k * 32 + cch:cw + k * 32 + cch + 1],
                         in1=acc3, op0=ALU.mult, op1=ALU.add)
                R.op("act", "activation", [r_acc], [r_fm[ci]], out=fm_dst[ci], in_=acc, func=AF.Silu)
            for c in range(8):
                cs = slice(c * 128, (c + 1) * 128)
                ps, rps = self.bank()
                psb = ps[:].bitcast(BF16)
                R.tr(psb[:, 0:128], xsT[:, 0, cs], identb[:], [r_fm[0], rc], [rps])
                R.tr(psb[:, 128:256], xsT[:, 1, cs], identb[:], [r_fm[1], rc], [rps])
                R.tr(psb[:, 256:384], BT[:, cs], identb[:], [r_fm[2], rc], [rps])
                self.copy("act", xb_tok[:, c, :], psb[:, 0:384], [rps], [r_xb[c]])
            for seq in range(nseq):
                chs = list(reversed(range(seq * cps, (seq + 1) * cps)))
                upd = [c for c in chs if (u == 0 or c > seq * cps)]
                dS = {}
                hb = None
                for n_, c in enumerate(upd):
                    k = n_ % 2
                    h0 = 32 + g * 4
                    R.op("dve", "tensor_tensor", [r_xb[c], r_tokq], [r_xw[k]], out=xw[k],
                         in0=xb_tok[:, c, 0:256].rearrange("p (h q) -> p h q", h=4),
                         in1=bc(wS[:, c, h0:h0 + 4].unsqueeze(2), [128, 4, 64]), op=ALU.mult)
                    if n_ % 2 == 0:
                        hb = self.bank(hold=True)
                    ps, rps = hb
                    col = (n_ % 2) * 256
                    R.mm(ps[:, col:col + 256], xb_tok[:, c, 256:384], xw[k].rearrange("p h q -> p (h q)"), True, True,
                         [r_xb[c], r_xw[k]], [rps])
                    dS[c] = (ps, rps, col)
                state_init(1, g)
                for c in chs:
                    R.op("act", "activation", [r_S[1]], [r_Tbf[c]], out=Tbf[:, c, :],
                         in_=Sst[1].rearrange("p h q -> p (h q)"), func=AF.Identity)
                    if c in dS:
                        ps, rps, col = dS[c]
                        h0 = 32 + g * 4
                        R.op("dve", "tensor_tensor", [r_S[1], r_dec], [r_S[1]], out=Sst[1], in0=Sst[1],
                             in1=bc(dec[:, c, h0:h0 + 4].unsqueeze(2), [128, 4, 64]), op=ALU.mult)
                        R.op("dve", "tensor_tensor", [r_S[1], rps], [r_S[1]], out=Sst[1],
                             in0=ps[:, col:col + 256].rearrange("p (h q) -> p h q", h=4), in1=Sst[1], op=ALU.add)
                for c in dS:
                    self.release(dS[c][0])
                if u == 0:
                    state_out(seq, 1, g)
            def front(c):
                cs = slice(c * 128, (c + 1) * 128)
                pp = c % 2
                psz, rpsz = self.bank()
                for kc in range(8):
                    R.mm(psz[:, 0:256], hT[:, kc, cs], W[:, kc, 0:256], kc == 0, kc == 7, [rW, r_hl[c // 4]], [rpsz])
                R.op("act", "activation", [rpsz], [r_zs2[pp]], out=zs[pp], in_=psz[:, 0:256], func=AF.Tanh, scale=0.5)
                R.op("dve", "scalar_tensor_tensor", [rpsz, r_zs2[pp]], [r_zs2[pp]], out=zs[pp], in0=zs[pp], scalar=1.0,
                     in1=psz[:, 0:256], op0=ALU.add, op1=ALU.mult)
                pcb, rpcb = self.bank()
                R.mm(pcb[:, 0:128], BT[:, cs], CT[:, cs], True, True, [r_fm[2], r_fm[3]], [rpcb])
                R.op("dve", "tensor_copy", [rpcb], [r_CB[pp]], out=CBm[pp], in_=pcb[:, 0:128])
                R.op("dve", "tensor_tensor", [r_xb[c], rc], [r_xD2[pp]], out=xD[pp],
                     in0=xb_tok[:, c, 0:256].rearrange("p (h q) -> p h q", h=4),
                     in1=bc(self.dbc[:, g * 4:g * 4 + 4].unsqueeze(2), [128, 4, 64]), op=ALU.mult)
                for d in range(2):
                    h0 = d * 32 + g * 4
                    R.op("pool", "tensor_tensor", [rc, r_tokq], [r_X4[d]], out=X4[d],
                         in0=bc(identf[:].unsqueeze(1), [128, 4, 128]),
                         in1=bc(E_tok[:, c, h0:h0 + 4].unsqueeze(2), [128, 4, 128]), op=ALU.mult)
                prs = []
                for d in range(2):
                    pr, rpr = self.bank()
                    msk = self.mge if d == 0 else self.mle
                    R.mm(pr[:, 0:512], msk[:], X4[d].rearrange("p h t -> p (h t)"), True, True, [r_X4[d], rc], [rpr])
                    prs.append((pr, rpr))
                for d in range(2):
                    h0 = d * 32 + g * 4
                    pr, rpr = prs[d]
                    for hh in range(4):
                        R.op("act", "activation", [rpr, r_tokq], [r_L4[d]], out=L4[d][:, hh, :],
                             in_=pr[:, hh * 128:(hh + 1) * 128], func=AF.Exp, bias=lnEb[:, c, h0 + hh:h0 + hh + 1])

            def front2(c):
                pp = c % 2
                for d in range(2):
                    R.op("dve", "tensor_tensor", [r_L4[d], r_CB[pp]], [r_M4[pp][d]], out=M4[pp][d], in0=L4[d],
                         in1=bc(CBm[pp].unsqueeze(1), [128, 4, 128]), op=ALU.mult)

            def back(seq, c):
                cs = slice(c * 128, (c + 1) * 128)
                pp = c % 2
                if c == seq * cps:
                    state_init(0, g)
                R.op("act", "activation", [r_S[0]], [r_Sbf], out=Sbf, in_=Sst[0].rearrange("p h q -> p (h q)"),
                     func=AF.Identity)
                py, rpy = self.bank(hold=True)
                R.mm(py[:, 0:256], identb[:], xD[pp].rearrange("p h q -> p (h q)"), True, False, [r_xD2[pp], rc], [rpy])
                for d in range(2):
                    for hh in range(4):
                        R.mm(py[:, hh * 64:(hh + 1) * 64], M4[pp][d][:, hh, :], xb_tok[:, c, hh * 64:(hh + 1) * 64],
                             False, d == 1 and hh == 3, [r_M4[pp][d], r_xb[c]], [rpy])
                pz, rpz = self.bank()
                R.mm(pz[:, 0:256], CT[:, cs], Sbf, True, True, [r_fm[3], r_Sbf], [rpz])
                R.mm(pz[:, 256:512], CT[:, cs], Tbf[:, c, :], True, True, [r_fm[3], r_Tbf[c]], [rpz])
                h0 = g * 4
                R.op("dve", "tensor_tensor", [rpz, r_tokq], [r_tc1], out=tc1,
                     in0=pz[:, 0:256].rearrange("p (h q) -> p h q", h=4),
                     in1=bc(dO[:, c, h0:h0 + 4].unsqueeze(2), [128, 4, 64]), op=ALU.mult)
                R.op("dve", "tensor_tensor", [rpz, r_tokq], [r_tc2], out=tc2,
                     in0=pz[:, 256:512].rearrange("p (h q) -> p h q", h=4),
                     in1=bc(dO[:, c, 32 + h0:32 + h0 + 4].unsqueeze(2), [128, 4, 64]), op=ALU.mult)
                R.op("dve", "tensor_tensor", [r_tc1, r_tc2], [r_tc1], out=tc1, in0=tc1, in1=tc2, op=ALU.add)
                R.op("dve", "tensor_tensor", [rpy, r_tc1], [r_tc1], out=tc1,
                     in0=py[:, 0:256].rearrange("p (h q) -> p h q", h=4), in1=tc1, op=ALU.add)
                self.release(py)
                ygs = yg[:, c, g * 256:(g + 1) * 256]
                R.op("dve", "tensor_tensor", [r_tc1, r_zs2[pp]], [r_yg[c]], out=ygs, in0=tc1.rearrange("p h q -> p (h q)"),
                     in1=zs[pp], op=ALU.mult)
                R.op("act", "activation", [r_yg[c]], [r_sqj, r_ss], out=sqj, in_=ygs, func=AF.Square,
                     accum_out=ss[:, c, g:g + 1])
                if u == 0 or c < (seq + 1) * cps - 1:
                    state_update(c, 0, g)
                if u == 0 and c == (seq + 1) * cps - 1:
                    state_out(seq, 0, g)

            front(0)
            front2(0)
            for c in range(8):
                if c + 1 < 8:
                    front(c + 1)
                back(c // cps, c)
                if c + 1 < 8:
                    front2(c + 1)

        A.off = mark
        sst = A([128, 8], F32)
        ynT = A([128, 16, 512], BF16)
        r_sst, r_yn = Res("sst"), Res("ynT")
        self.stage(persist + [r_sst, r_yn])
        R.op("dve", "tensor_reduce", [r_ss], [r_sst], out=sst, in_=ss, axis=AX.X, op=ALU.add)
        R.op("act", "activation", [r_sst], [r_sst], out=sst, in_=sst, func=AF.Ln, scale=1.0 / 2048, bias=4.0 * EPS)
        R.op("act", "activation", [r_sst], [r_sst], out=sst, in_=sst, func=AF.Exp, scale=-0.5)
        for c in range(8):
            R.op("dve", "tensor_scalar", [r_yg[c], r_sst], [r_yg[c]], out=yg[:, c, :], in0=yg[:, c, :],
                 scalar1=sst[:, c:c + 1], scalar2=None, op0=ALU.mult)
        w_out = dr["ssd_w_out"][0]
        wA, rwA = self.wload(w_out[0:1024, :], 8, 1024)
        wB, rwB = self.wload(w_out[1024:2048, :], 8, 1024)
        gn = pc["ssd_norm_g"]
        for half in range(2):
            for cc in range(16):
                ps, rps = self.bank()
                psb = ps[:].bitcast(BF16)
                for q in range(4):
                    c = half * 4 + q
                    R.tr(psb[:, q * 128:(q + 1) * 128], yg[:, c, cc * 128:(cc + 1) * 128], identb[:], [r_yg[c], rc], [rps])
                R.op("act", "activation", [rps, self.r_prm], [r_yn], out=ynT[:, cc, :], in_=psb[:, 0:512], func=AF.Identity,
                     scale=self.prm[:, gn + cc:gn + cc + 1])
            tt = u * 2 + half
            ts = slice(tt * 512, (tt + 1) * 512)
            for dc in range(8):
                ps, rps = self.bank()
                for cc in range(16):
                    wv, rw = (wA, rwA) if cc < 8 else (wB, rwB)
                    R.mm(ps[:], wv[:, cc % 8, dc * 128:(dc + 1) * 128], ynT[:, cc, :], cc == 0, cc == 15, [rw, r_yn], [rps])
                R.op("dve", "scalar_tensor_tensor", [rps, self.r_mod, self.r_xT[dc][tt]], [self.r_xT[dc][tt]],
                     out=self.xT[:, dc, ts], in0=ps[:], scalar=self.mod[:, 16 + dc, u:u + 1],
                     in1=self.xT[:, dc, ts], op0=ALU.mult, op1=ALU.add)

    def attn_core(self, parts, part_res, vl, v_res, blocks, scale, negm, negm_res, pT, r_pT, fin, v_res_fn=None):
        R = self.R
        for (q0, nq, kts) in blocks:
            po, r_po = self.bank(hold=True)
            psm, r_psm = self.bank(hold=True)
            n = len(kts)

            def score(kt):
                pss, r_pss = self.bank()
                for pi, (kT, qT) in enumerate(parts):
                    R.mm(pss[:, 0:nq], kT[:, kt * 128:(kt + 1) * 128], qT[:, q0:q0 + nq], pi == 0, pi == len(parts) - 1,
                         part_res, [r_pss])
                return pss, r_pss

            nxt = score(kts[0])
            for idx, kt in enumerate(kts):
                pss, r_pss = nxt
                if idx + 1 < n:
                    nxt = score(kts[idx + 1])
                slot = self.pT_i % pT.shape[1]
                self.pT_i += 1
                R.op("act", "activation", [r_pss, negm_res], [r_pT[slot]], out=pT[:, slot, 0:nq], in_=pss[:, 0:nq],
                     func=AF.Exp, scale=scale, bias=negm)
                vr = v_res_fn(kt) if v_res_fn is not None else v_res
                R.mm(po[:, 0:nq], vl(kt), pT[:, slot, 0:nq], idx == 0, idx == n - 1, [vr, r_pT[slot]], [r_po])
                R.mm(psm[:, 0:nq], self.onesb[:], pT[:, slot, 0:nq], idx == 0, idx == n - 1, [self.r_const, r_pT[slot]],
                     [r_psm])
            self.attn_flush()
            self.pending_fin = (fin, q0, nq, po, r_po, psm, r_psm)

    def attn_flush(self):
        if getattr(self, "pending_fin", None) is not None:
            fin, q0, nq, po, r_po, psm, r_psm = self.pending_fin
            self.pending_fin = None
            fin(q0, nq, po, r_po, psm, r_psm)
            self.release(po)
            self.release(psm)

    def sq_bound(self, parts, part_res, ncols, out_col, tmp_sq, r_sq, red, r_red, p0=0):
        R = self.R
        ntile = (ncols + 511) // 512
        first = True
        for t in range(ntile):
            c0 = t * 512
            w = min(512, ncols - c0)
            ps, rps = self.bank()
            for pi, p in enumerate(parts):
                K = p.shape[0]
                R.op("act", "activation", part_res, [r_sq], out=tmp_sq[p0:p0 + K, 0:w], in_=p[:, c0:c0 + w], func=AF.Square)
                R.mm(ps[:, 0:w], self.onesb[p0:p0 + K, :], tmp_sq[p0:p0 + K, 0:w], pi == 0, pi == len(parts) - 1,
                     [r_sq, self.r_const], [rps])
            if first:
                R.op("dve", "tensor_reduce", [rps], [r_red], out=red[:, out_col:out_col + 1], in_=ps[:, 0:w], axis=AX.X,
                     op=ALU.max)
                first = False
            else:
                R.op("dve", "tensor_reduce", [rps], [r_red], out=red[:, 7:8], in_=ps[:, 0:w], axis=AX.X, op=ALU.max)
                R.op("dve", "tensor_tensor", [r_red], [r_red], out=red[:, out_col:out_col + 1],
                     in0=red[:, out_col:out_col + 1], in1=red[:, 7:8], op=ALU.max)

    def fnorm(self, src32, nch, w, g_col, tmp_sq, r_sq, rstd, r_rstd, lnt):
        R = self.R
        R.op("act", "activation", [self.r_f32], [r_sq], out=tmp_sq[:, 0:nch, 0:w], in_=src32[:, 0:nch, 0:w], func=AF.Square)
        ps, rps = self.bank()
        for m in range(nch):
            R.mm(ps[:, 0:w], self.onesb[:], tmp_sq[:, m, 0:w], m == 0, m == nch - 1, [r_sq, self.r_const], [rps])
        R.op("act", "activation", [rps], [r_rstd], out=lnt[:, 0:w], in_=ps[:, 0:w], func=AF.Ln, scale=1.0 / (nch * 128),
             bias=EPS)
        R.op("act", "activation", [r_rstd], [r_rstd], out=rstd[:, 0:w], in_=lnt[:, 0:w], func=AF.Exp, scale=-0.5)

    def mla(self, i, u):
        R = self.R
        dr = self.dram
        pc = self.pcol
        hT, r_h = self.hT, self.r_h
        r_hl = [r_h[u * 2], r_h[u * 2 + 1]]
        nseq, L = (NPS, LP) if u == 0 else (1, LS)
        koff = 0 if u == 0 else 256
        Tk = TU + koff
        identf, identb = self.identf, self.identb
        rc = self.r_const
        scale = 96.0 ** -0.5

        A = self._Alloc(self, 16384)
        cqn = A([128, 4, TU], BF16)
        ckvk = A([128, 2, 1280], BF16)
        krk = A([128, 1280], BF16)
        oT = A([128, 8, TU], BF16)
        WinS = A([128, 8, 32], BF16)
        mark = A.off
        r_cqn, r_ckvk, r_krk, r_WinS = Res("cqn"), Res("ckvk"), Res("krk"), Res("WinS")
        r_oT = [[Res("oT%d_%d" % (c, t)) for t in range(2)] for c in range(8)]
        persist = [r_cqn, r_ckvk, r_krk] + [r for rr in r_oT for r in rr]

        f32 = A([128, 4, 512], F32)
        sq = A([128, 4, 512], BF16)
        lnt = A([128, 512], F32)
        rstd = A([128, 512], F32)
        ck32 = A([128, 2, 512], F32)
        krf = A([32, 512], F32)
        kt1 = A([32, 512], F32)
        kt2 = A([32, 512], F32)
        rope = A([32, 2, TU], F32)
        ost = A([128, 256], F32)
        ost2 = A([128, 32], F32)
        cst = A([128, 2, 256], F32)
        cst2 = A([128, 2, 32], F32)
        self.r_f32 = Res("f32")
        r_sq, r_rstd, r_ck32, r_krf, r_kt, r_rope = Res("sq"), Res("rstd"), Res("ck32"), Res("krf"), Res("kt"), Res("rope")
        r_ost, r_ost2, r_cst = Res("ost"), Res("ost2"), Res("cst")
        self.stage(r_hl + persist + [r_WinS, self.r_f32, r_sq, r_rstd, r_ck32, r_krf, r_kt, r_rope, r_ost, r_ost2, r_cst])
        R.op("dve", "memset", [], [r_krk], ap=krk, constant=0.0)
        Win, rWin = self.wload(dr["mla_w_in"][0], 8, 800)
        R.op("dve", "tensor_copy", [rWin], [r_WinS], out=WinS[:, :, 0:16], in_=Win[:, :, 784:800])
        R.op("dve", "tensor_copy", [rWin], [r_WinS], out=WinS[:, :, 16:32], in_=Win[:, :, 768:784])
        if u == 1:
            R.dma("sp", rope, dr["rope_cs"].rearrange("a r t -> r a t"), [], [r_rope])
            R.dma("sp", cst, dr["c_ckv"].rearrange("(a p) f -> p a f", p=128), [], [r_cst])
            R.dma("sp", cst2, dr["c_kr"].rearrange("(a p) f -> p a f", p=128), [], [r_cst])
            for a in range(2):
                ps, rps = self.bank()
                for m in range(2):
                    R.tr(ps[:, m * 128:(m + 1) * 128], cst[:, a, m * 128:(m + 1) * 128], identf[:], [r_cst, rc], [rps])
                R.op("dve", "tensor_copy", [rps], [r_ckvk], out=ckvk[:, :, a * 128:(a + 1) * 128],
                     in_=ps[:, 0:256].rearrange("p (m t) -> p m t", m=2))
                ps, rps = self.bank()
                R.tr(ps[0:32, 0:128], cst2[:, a, :], identf[:], [r_cst, rc], [rps])
                R.op("dve", "tensor_copy", [rps], [r_krk], out=krk[0:32, a * 128:(a + 1) * 128], in_=ps[0:32, 0:128])
        gq, gkv = pc["mla_q_norm_g"], pc["mla_kv_norm_g"]
        for tt in range(2):
            ts = slice(tt * 512, (tt + 1) * 512)
            for m in range(4):
                ps, rps = self.bank()
                for kc in range(8):
                    R.mm(ps[:], Win[:, kc, m * 128:(m + 1) * 128], hT[:, kc, ts], kc == 0, kc == 7, [rWin, r_hl[tt]], [rps])
                self.copy(self.evac_engine(), f32[:, m, :], ps[:], [rps], [self.r_f32])
            self.fnorm(f32, 4, 512, gq, sq, r_sq, rstd, r_rstd, lnt)
            for m in range(4):
                R.op("dve", "scalar_tensor_tensor", [self.r_f32, r_rstd, self.r_prm], [r_cqn], out=cqn[:, m, ts],
                     in0=f32[:, m, :], scalar=self.prm[:, gq + m:gq + m + 1], in1=rstd, op0=ALU.mult, op1=ALU.mult)
            for m in range(2):
                ps, rps = self.bank()
                for kc in range(8):
                    R.mm(ps[:], Win[:, kc, 512 + m * 128:512 + (m + 1) * 128], hT[:, kc, ts], kc == 0, kc == 7,
                         [rWin, r_hl[tt]], [rps])
                self.copy(self.evac_engine(), f32[:, m, :], ps[:], [rps], [self.r_f32])
            self.fnorm(f32, 2, 512, gkv, sq, r_sq, rstd, r_rstd, lnt)
            for m in range(2):
                R.op("dve", "scalar_tensor_tensor", [self.r_f32, r_rstd, self.r_prm], [r_ck32], out=ck32[:, m, :],
                     in0=f32[:, m, :], scalar=self.prm[:, gkv + m:gkv + m + 1], in1=rstd, op0=ALU.mult, op1=ALU.mult)
            R.op("act", "activation", [r_ck32], [r_ckvk], out=ckvk[:, :, koff + tt * 512:koff + (tt + 1) * 512], in_=ck32,
                 func=AF.Identity)
            ps, rps = self.bank()
            for kc in range(8):
                R.mm(ps[0:32, :], Win[:, kc, 768:800], hT[:, kc, ts], kc == 0, kc == 7, [rWin, r_hl[tt]], [rps])
            R.op("dve", "tensor_copy", [rps], [r_krf], out=krf, in_=ps[0:32, :])
            kdst = krk[0:32, koff + tt * 512:koff + (tt + 1) * 512]
            if u == 0:
                R.op("act", "activation", [r_krf], [r_krk], out=kdst, in_=krf, func=AF.Identity)
            else:
                ps2, rps2 = self.bank()
                for kc in range(8):
                    R.mm(ps2[0:32, :], WinS[:, kc, :], hT[:, kc, ts], kc == 0, kc == 7, [r_WinS, r_hl[tt]], [rps2])
                R.op("dve", "tensor_tensor", [r_krf, r_rope], [r_kt], out=kt1, in0=krf, in1=rope[:, 0, ts], op=ALU.mult)
                R.op("dve", "tensor_tensor", [rps2, r_rope], [r_kt], out=kt2, in0=ps2[0:32, :], in1=rope[:, 1, ts],
                     op=ALU.mult)
                R.op("dve", "tensor_tensor", [r_kt], [r_krk], out=kdst, in0=kt1, in1=kt2, op=ALU.add)
            if u == 0:
                for q in range(4):
                    tok0 = tt * 512 + q * 128
                    seq, t0 = tok0 // LP, tok0 % LP
                    ps, rps = self.bank()
                    for m in range(2):
                        R.tr(ps[:, m * 128:(m + 1) * 128], ck32[:, m, q * 128:(q + 1) * 128], identf[:], [r_ck32, rc], [rps])
                    R.op("dve", "tensor_copy", [rps], [r_ost], out=ost, in_=ps[:, 0:256])
                    R.dma("sp", dr["o_ckv"][seq, t0:t0 + 128, :], ost, [r_ost], [])
                    ps, rps = self.bank()
                    R.tr(ps[:, 0:32], krf[:, q * 128:(q + 1) * 128], identf[0:32, 0:32], [r_krf, rc], [rps])
                    R.op("dve", "tensor_copy", [rps], [r_ost2], out=ost2, in_=ps[:, 0:32])
                    R.dma("sp", dr["o_kr"][seq, t0:t0 + 128, :], ost2, [r_ost2], [])

        A.off = mark
        WuqS = A([128, 4, 16, 32], BF16)
        qn_ = [A([128, TU], BF16) for _ in range(2)]
        qr_ = [A([128, TU], BF16) for _ in range(2)]
        qa_ = [A([32, TU], F32) for _ in range(2)]
        qb_ = [A([32, TU], F32) for _ in range(2)]
        kn_l = [A([128, 1280], BF16) for _ in range(2)]
        sqt_ = [A([64, 512], BF16) for _ in range(2)]
        red_ = [A([128, 8], F32) for _ in range(2)]
        negm = A([128, 2], F32)
        vpair_ = [A([128, 10, 128], BF16) for _ in range(2)]
        NS = 4
        pT = A([128, NS, 512], BF16)
        rec = A([128, 512], F32)
        rope2 = A([32, 2, TU], F32)
        r_WuqS, r_rec, r_rope2 = Res("WuqS"), Res("rec"), Res("rope2")
        r_qn_ = [Res("qn0"), Res("qn1")]
        r_qr_ = [Res("qr0"), Res("qr1")]
        r_qab_ = [Res("qab0"), Res("qab1")]
        r_kn_ = [Res("kn0"), Res("kn1")]
        r_sqt_ = [Res("sqt0"), Res("sqt1")]
        r_red_ = [Res("red0"), Res("red1")]
        r_negm_ = [Res("negm0"), Res("negm1")]
        r_vp_ = [Res("vp0"), Res("vp1")]
        r_pT = [Res("pT%d" % k) for k in range(NS)]
        self.stage(persist + [r_WuqS, r_rec, r_rope2] + r_qn_ + r_qr_ + r_qab_ + r_kn_ + r_sqt_ + r_red_ + r_negm_ + r_vp_ + r_pT)
        if u == 1:
            R.dma("sp", rope2, dr["rope_cs"].rearrange("a r t -> r a t"), [], [r_rope2])
        for k in range(2):
            R.op("dve", "memset", [], [r_qn_[k]], ap=qn_[k], constant=0.0)
            R.op("dve", "memset", [], [r_qr_[k]], ap=qr_[k], constant=0.0)
            R.op("dve", "memset", [], [r_kn_[k]], ap=kn_l[k], constant=0.0)
        Wuq, rWuq = self.wload(dr["mla_w_uq"][0], 4, 1536)
        Wukv, rWukv = self.wload(dr["mla_w_ukv"][0], 2, 2048)
        Wuq4 = Wuq.rearrange("p k (h d) -> p k h d", d=96)
        for kc in range(4):
            R.op("dve", "tensor_copy", [rWuq], [r_WuqS], out=WuqS[:, kc, :, 0:16], in_=Wuq4[:, kc, :, 80:96])
            R.op("dve", "tensor_copy", [rWuq], [r_WuqS], out=WuqS[:, kc, :, 16:32], in_=Wuq4[:, kc, :, 64:80])
        ktiles = [(0, 512), (512, 512), (1024, Tk - 1024)] if Tk > 1024 else [(0, 512), (512, 512)]
        if u == 1:
            blocks = [(0, 512, list(range(10))), (512, 512, list(range(10)))]
        else:
            blocks = [(s * 256, 256, [2 * s, 2 * s + 1]) for s in range(4)]
        Wv = Wukv.rearrange("p k (h two d) -> p k h two d", two=2, d=64)
        nkt = Tk // 128

        def prep_pair(a):
            vpair, r_vp = vpair_[a % 2], r_vp_[a % 2]
            for k0 in range(0, nkt, 4):
                ps, rps = self.bank()
                nq4 = min(4, nkt - k0)
                for q in range(nq4):
                    kt = k0 + q
                    for kc in range(2):
                        R.mm(ps[:, q * 128:(q + 1) * 128].rearrange("p (h d) -> p h d", h=2),
                             ckvk[:, kc, kt * 128:(kt + 1) * 128], Wv[:, kc, 2 * a:2 * a + 2, 1, :], kc == 0, kc == 1,
                             [r_ckvk, rWukv], [rps])
                self.copy(self.evac_engine(), vpair[:, k0:k0 + nq4, :],
                          ps[:, 0:nq4 * 128].rearrange("p (q d) -> p q d", d=128), [rps], [r_vp])

        def prep(h):
            hp = h % 2
            qn, qr, qa, qb, kn, sqt, red = qn_[hp], qr_[hp], qa_[hp], qb_[hp], kn_l[hp], sqt_[hp], red_[hp]
            r_qn, r_qr, r_qab, r_kn, r_sqt, r_red, r_negm = (r_qn_[hp], r_qr_[hp], r_qab_[hp], r_kn_[hp], r_sqt_[hp],
                                                             r_red_[hp], r_negm_[hp])
            for (c0, w) in ktiles:
                ps, rps = self.bank()
                for kc in range(2):
                    R.mm(ps[0:64, 0:w], Wukv[:, kc, h * 128:h * 128 + 64], ckvk[:, kc, c0:c0 + w], kc == 0, kc == 1,
                         [rWukv, r_ckvk], [rps])
                self.copy(self.evac_engine(), kn[0:64, c0:c0 + w], ps[0:64, 0:w], [rps], [r_kn])
            for tt in range(2):
                ts = slice(tt * 512, (tt + 1) * 512)
                ps, rps = self.bank()
                for kc in range(4):
                    R.mm(ps[0:64, :], Wuq[:, kc, h * 96:h * 96 + 64], cqn[:, kc, ts], kc == 0, kc == 3, [rWuq, r_cqn], [rps])
                self.copy(self.evac_engine(), qn[0:64, ts], ps[0:64, :], [rps], [r_qn])
                ps, rps = self.bank()
                for kc in range(4):
                    R.mm(ps[0:32, :], Wuq[:, kc, h * 96 + 64:h * 96 + 96], cqn[:, kc, ts], kc == 0, kc == 3, [rWuq, r_cqn],
                         [rps])
                if u == 0:
                    self.copy(self.evac_engine(), qr[0:32, ts], ps[0:32, :], [rps], [r_qr])
                else:
                    ps2, rps2 = self.bank()
                    for kc in range(4):
                        R.mm(ps2[0:32, :], WuqS[:, kc, h, :], cqn[:, kc, ts], kc == 0, kc == 3, [r_WuqS, r_cqn], [rps2])
                    R.op("dve", "tensor_tensor", [rps, r_rope2], [r_qab], out=qa[:, ts], in0=ps[0:32, :],
                         in1=rope2[:, 0, ts], op=ALU.mult)
                    R.op("dve", "tensor_tensor", [rps2, r_rope2], [r_qab], out=qb[:, ts], in0=ps2[0:32, :],
                         in1=rope2[:, 1, ts], op=ALU.mult)
                    R.op("dve", "tensor_tensor", [r_qab], [r_qr], out=qr[0:32, ts], in0=qa[:, ts], in1=qb[:, ts], op=ALU.add)
            self.sq_bound([qn[0:64, :], qr[0:32, :]], [r_qn, r_qr], TU, 0, sqt, r_sqt, red, r_red)
            self.sq_bound([kn[0:64, 0:Tk], krk[0:32, 0:Tk]], [r_kn, r_krk], Tk, 1, sqt, r_sqt, red, r_red)
            R.op("dve", "tensor_tensor", [r_red], [r_red], out=red[:, 2:3], in0=red[:, 0:1], in1=red[:, 1:2], op=ALU.add)
            R.op("dve", "tensor_scalar", [r_red], [r_negm], out=negm[:, hp:hp + 1], in0=red[:, 2:3],
                 scalar1=-0.5 * scale, scalar2=None, op0=ALU.mult)

        def attn(h):
            a, hp = h // 2, h % 2
            rows = slice(hp * 64, (hp + 1) * 64)
            vpair, r_vp = vpair_[a % 2], r_vp_[a % 2]
            qn, qr, kn = qn_[hp], qr_[hp], kn_l[hp]

            def fin(q0, nq, po, r_po, psm, r_psm):
                R.op("dve", "reciprocal", [r_psm], [r_rec], out=rec[rows, 0:nq], in_=psm[rows, 0:nq])
                R.op("dve", "tensor_tensor", [r_po, r_rec], [r_oT[a][q0 // 512]], out=oT[rows, a, q0:q0 + nq],
                     in0=po[rows, 0:nq], in1=rec[rows, 0:nq], op=ALU.mult)

            self.attn_core([(kn, qn), (krk, qr)], [r_kn_[hp], r_qn_[hp], r_krk, r_qr_[hp]], lambda kt: vpair[:, kt, :], r_vp,
                           blocks, scale, negm[:, hp:hp + 1], r_negm_[hp], pT, r_pT, fin)

        prep_pair(0)
        prep(0)
        for h in range(16):
            if h + 1 < 16:
                if (h + 1) % 2 == 0:
                    prep_pair((h + 1) // 2)
                prep(h + 1)
            attn(h)
        self.attn_flush()

        Wo, rWo = self.wload(dr["mla_w_o"][0], 8, 1024)
        for half in range(2):
            tt = u * 2 + half
            ts = slice(tt * 512, (tt + 1) * 512)
            ls = slice(half * 512, (half + 1) * 512)
            for dc in range(8):
                ps, rps = self.bank()
                for cc in range(8):
                    R.mm(ps[:], Wo[:, cc, dc * 128:(dc + 1) * 128], oT[:, cc, ls], cc == 0, cc == 7, [rWo, r_oT[cc][half]],
                         [rps])
                R.op("dve", "scalar_tensor_tensor", [rps, self.r_mod, self.r_xT[dc][tt]], [self.r_xT[dc][tt]],
                     out=self.xT[:, dc, ts], in0=ps[:], scalar=self.mod[:, 16 + dc, u:u + 1],
                     in1=self.xT[:, dc, ts], op0=ALU.mult, op1=ALU.add)

    def mlstm(self, i, u):
        R = self.R
        nc = self.nc
        dr = self.dram
        pc = self.pcol
        hT, r_h = self.hT, self.r_h
        r_hl = [r_h[u * 2], r_h[u * 2 + 1]]
        nseq, L = (NPS, LP) if u == 0 else (1, LS)
        cps = L // 128
        identf, identb, onesf = self.identf, self.identb, self.onesf
        rc = self.r_const
        bc = lambda ap, shape: ap.to_broadcast(shape)
        w_up = dr["mlstm_w_up"][0]
        if not hasattr(self, "zsp"):
            self.zsp = nc.dram_tensor("zsp", [128, 16, TU], BF16).ap()
            self.r_zsp = Res("zsp")
        zsp, r_zsp = self.zsp, self.r_zsp

        A = self._Alloc(self, 0)
        hTv = A([128, 8, TU], BF16)
        xmT = A([128, 16, TU], BF16)
        xcT = A([128, 16, TU], BF16)
        e_tok = A([128, 8, 16], F32)
        emc_tok = A([128, 8, 16], F32)
        iwb = A([128, 8, 16], F32)
        mark = A.off
        r_xm = [Res("xm%d" % c) for c in range(16)]
        r_xc = [Res("xc%d" % c) for c in range(16)]
        r_tok = Res("mtok")
        persist = r_xm + r_xc + [r_tok]

        pre_ = [A([128, nseq, L + 4], F32) for _ in range(2)]
        acc = A([128, TU], F32)
        zst = [A([128, TU], BF16) for _ in range(2)]
        r_pre_, r_acc, r_g = [Res("pre0"), Res("pre1")], Res("acc"), Res("gates")
        r_zst = [Res("zst0"), Res("zst1")]
        self.stage(r_hl + persist + r_pre_ + [r_acc] + r_zst)
        for k in range(2):
            R.op("dve", "memset", [], [r_pre_[k]], ap=pre_[k], constant=0.0)
        cw, cb = pc["mlstm_conv_w"], pc["mlstm_conv_b"]
        for blk in range(2):
            W, rW = self.wload(w_up[:, blk * 1024:(blk + 1) * 1024], 8, 1024)
            for cl in range(8):
                cc = blk * 8 + cl
                pre, r_pre = pre_[cc % 2], r_pre_[cc % 2]
                for tt in range(2):
                    ps, rps = self.bank()
                    for kc in range(8):
                        R.mm(ps[:], W[:, kc, cl * 128:(cl + 1) * 128], hT[:, kc, tt * 512:(tt + 1) * 512], kc == 0, kc == 7,
                             [rW, r_hl[tt]], [rps])
                    if u == 0:
                        R.op("act", "activation", [rps], [r_pre], out=pre[:, 2 * tt:2 * tt + 2, 2:2 + L],
                             in_=ps[:].rearrange("p (s t) -> p s t", s=2), func=AF.Identity)
                    else:
                        R.op("act", "activation", [rps], [r_pre], out=pre[:, 0, 2 + tt * 512:2 + (tt + 1) * 512], in_=ps[:],
                             func=AF.Identity)
                    R.op("dve", "tensor_copy", [rps], [r_xm[cc]], out=xmT[:, cc, tt * 512:(tt + 1) * 512], in_=ps[:])
                acc3 = acc.rearrange("p (s t) -> p s t", s=nseq)
                R.op("dve", "tensor_scalar", [r_pre, self.r_prm], [r_acc], out=acc3, in0=pre[:, :, 0:L],
                     scalar1=self.prm[:, cw + cc:cw + cc + 1], scalar2=self.prm[:, cb + cc:cb + cc + 1], op0=ALU.mult,
                     op1=ALU.add)
                for k in range(1, 5):
                    R.op("dve", "scalar_tensor_tensor", [r_pre, self.r_prm, r_acc], [r_acc], out=acc3,
                         in0=pre[:, :, k:k + L], scalar=self.prm[:, cw + k * 16 + cc:cw + k * 16 + cc + 1], in1=acc3,
                         op0=ALU.mult, op1=ALU.add)
                R.op("act", "activation", [r_acc], [r_xc[cc]], out=xcT[:, cc, :], in_=acc, func=AF.Silu)
        for blk in range(2):
            W, rW = self.wload(w_up[:, 2048 + blk * 1024:2048 + (blk + 1) * 1024], 8, 1024)
            for cl in range(8):
                cc = blk * 8 + cl
                k = cc % 2
                for tt in range(2):
                    ps, rps = self.bank()
                    for kc in range(8):
                        R.mm(ps[:], W[:, kc, cl * 128:(cl + 1) * 128], hT[:, kc, tt * 512:(tt + 1) * 512], kc == 0, kc == 7,
                             [rW, r_hl[tt]], [rps])
                    R.op("act", "activation", [rps], [r_zst[k]], out=zst[k][:, tt * 512:(tt + 1) * 512], in_=ps[:], func=AF.Silu)
                R.dma("sp", zsp[:, cc, :], zst[k], [r_zst[k]], [r_zsp])
        A.off = mark
        Wg40 = A([128, 8, 2, 40], BF16)
        gb = A([40, 2], F32)
        GI = A([40, TU], F32)
        X = A([40, TU], F32)
        T1 = A([40, TU], F32)
        Bt = A([40, TU], F32)
        cm = Bt
        Mt = A([40, 8], F32)
        mint = A([40, 9], F32)
        amax = A([40, 8], F32)
        iw = A([40, 8], F32)
        mcur = A([40, 1], F32)
        mfin = A([40, 4], F32)
        xd = A([40, 8, 40], F32)
        self.stage(r_hl + persist + [r_g])
        RG = [r_g]
        Wg, rWg = self.wload(w_up[:, 4096:4128], 8, 32)
        R.op("dve", "memset", [], RG, ap=Wg40, constant=0.0)
        for (f, dst0, src0) in ((0, 0, 0), (0, 32, 16), (1, 0, 8), (1, 32, 24)):
            R.op("dve", "tensor_copy", [rWg] + RG, RG, out=Wg40[:, :, f, dst0:dst0 + 8], in_=Wg[:, :, src0:src0 + 8])
        R.op("dve", "memset", RG, RG, ap=gb, constant=0.0)
        gbd = dr["mlstm_gate_b"][0]
        for d in range(2):
            for f in range(2):
                R.dma("sp", gb[d * 32:d * 32 + 8, f:f + 1], gbd[d, f].rearrange("(h o) -> h o", o=1), RG, RG)
        for f, dst in ((0, GI), (1, X)):
            for tt in range(2):
                ps, rps = self.bank()
                for kc in range(8):
                    R.mm(ps[0:40, :], Wg40[:, kc, f, :], hT[:, kc, tt * 512:(tt + 1) * 512], kc == 0, kc == 7,
                         RG + [r_hl[tt]], [rps])
                R.op("dve", "tensor_scalar", [rps] + RG, RG, out=dst[:, tt * 512:(tt + 1) * 512], in0=ps[0:40, :],
                     scalar1=gb[:, f:f + 1], scalar2=None, op0=ALU.add)
        R.op("act", "activation", RG, RG, out=T1, in_=X, func=AF.Abs)
        R.op("act", "activation", RG, RG, out=T1, in_=T1, func=AF.Exp, scale=-1.0)
        R.op("act", "activation", RG, RG, out=T1, in_=T1, func=AF.Ln, bias=1.0)
        R.op("dve", "scalar_tensor_tensor", RG, RG, out=X, in0=X, scalar=0.0, in1=T1, op0=ALU.min, op1=ALU.subtract)
        cm3 = cm.rearrange("p (c t) -> p c t", t=128)
        R.op("dve", "memset", RG, RG, ap=cm, constant=1.0)
        R.op("dve", "memset", RG, RG, ap=cm3[:, :, 0:1], constant=0.0)
        R.op("dve", "tensor_tensor_scan", RG, RG, out=T1, data0=cm, data1=X, initial=0.0, op0=ALU.mult, op1=ALU.add)
        cum3 = T1.rearrange("p (c t) -> p c t", t=128)
        tot = cum3[:, :, 127:128]
        Bt3 = Bt.rearrange("p (c t) -> p c t", t=128)
        R.op("dve", "tensor_copy", RG, RG, out=Bt, in_=T1)
        R.op("dve", "tensor_tensor", RG, RG, out=Bt3[32:40], in0=bc(tot[32:40], [8, 8, 128]), in1=cum3[32:40], op=ALU.subtract)
        R.op("dve", "tensor_tensor", RG, RG, out=Bt[32:40, :], in0=Bt[32:40, :], in1=X[32:40, :], op=ALU.add)
        R.op("dve", "tensor_tensor", RG, RG, out=GI, in0=GI, in1=Bt, op=ALU.subtract)
        R.op("dve", "tensor_reduce", RG, RG, out=amax, in_=GI.rearrange("p (c t) -> p c t", t=128), axis=AX.X, op=ALU.max)
        R.op("dve", "memset", RG, RG, ap=mfin, constant=0.0)
        for d, rows in ((0, slice(0, 8)), (1, slice(32, 40))):
            for seq in range(nseq):
                if u == 0:
                    R.op("dve", "memset", RG, RG, ap=mcur[rows, :], constant=0.0)
                else:
                    R.dma("sp", mcur[rows, :], dr["st_m"][d].rearrange("(h o) -> h o", o=1), RG, RG)
                cl = list(range(seq * cps, (seq + 1) * cps))
                if d == 1:
                    cl = cl[::-1]
                for c in cl:
                    R.op("dve", "tensor_copy", RG, RG, out=mint[rows, c:c + 1], in_=mcur[rows, :])
                    R.op("dve", "tensor_tensor", RG, RG, out=Mt[rows, c:c + 1], in0=mcur[rows, :], in1=amax[rows, c:c + 1],
                         op=ALU.max)
                    R.op("dve", "tensor_tensor", RG, RG, out=mcur[rows, :], in0=Mt[rows, c:c + 1], in1=tot[rows, c, :],
                         op=ALU.add)
                R.op("dve", "tensor_copy", RG, RG, out=mfin[rows, seq:seq + 1], in_=mcur[rows, :])
                if u == 0:
                    R.dma("sp", dr["o_m"][seq, d].rearrange("(h o) -> h o", o=1), mfin[rows, seq:seq + 1], RG, [])
        R.op("dve", "memset", RG, RG, ap=amax, constant=0.0)
        R.op("dve", "tensor_tensor", RG, RG, out=iw[0:8, :], in0=mint[0:8, 0:8], in1=Mt[0:8, :], op=ALU.subtract)
        R.op("dve", "tensor_tensor", RG, RG, out=iw[32:40, :], in0=mint[32:40, 0:8], in1=Mt[32:40, :], op=ALU.subtract)
        R.op("dve", "memset", RG, RG, ap=xd, constant=0.0)
        for rows in (slice(0, 8), slice(32, 40)):
            R.op("act", "activation", RG, RG, out=iw[rows, :], in_=iw[rows, :], func=AF.Exp)
            R.op("dve", "tensor_scalar", RG, RG, out=amax[rows, :], in0=Mt[rows, :], scalar1=-1.0, scalar2=None, op0=ALU.mult)
            R.op("dve", "tensor_tensor", RG + [rc], RG, out=xd[rows], in0=bc(identf[rows, 0:40].unsqueeze(1), [8, 8, 40]),
                 in1=bc(iw[rows, :].unsqueeze(2), [8, 8, 40]), op=ALU.mult)
        ps, rps = self.bank()
        R.mm(ps[:, 0:320], onesf[0:40, :], xd.rearrange("p c h -> p (c h)"), True, True, RG + [rc], [rps])
        ps3 = ps[:, 0:320].rearrange("p (c h) -> p c h", h=40)
        R.op("dve", "tensor_copy", [rps], [r_tok], out=iwb[:, :, 0:8], in_=ps3[:, :, 0:8])
        R.op("dve", "tensor_copy", [rps], [r_tok], out=iwb[:, :, 8:16], in_=ps3[:, :, 32:40])
        for rows in (slice(0, 8), slice(32, 40)):
            for c in range(8):
                cs = slice(c * 128, (c + 1) * 128)
                R.op("act", "activation", RG, RG, out=GI[rows, cs], in_=GI[rows, cs], func=AF.Exp, bias=amax[rows, c:c + 1])
                R.op("act", "activation", RG, RG, out=Bt[rows, cs], in_=Bt[rows, cs], func=AF.Exp, scale=-1.0,
                     bias=amax[rows, c:c + 1])
        for c in range(8):
            cs = slice(c * 128, (c + 1) * 128)
            ps, rps = self.bank()
            R.tr(ps[:, 0:40], GI[:, cs], identf[0:40, 0:40], RG + [rc], [rps])
            R.tr(ps[:, 64:104], Bt[:, cs], identf[0:40, 0:40], RG + [rc], [rps])
            R.op("dve", "tensor_copy", [rps], [r_tok], out=e_tok[:, c, 0:8], in_=ps[:, 0:8])
            R.op("dve", "tensor_copy", [rps], [r_tok], out=e_tok[:, c, 8:16], in_=ps[:, 32:40])
            R.op("dve", "tensor_copy", [rps], [r_tok], out=emc_tok[:, c, 0:8], in_=ps[:, 64:72])
            R.op("dve", "tensor_copy", [rps], [r_tok], out=emc_tok[:, c, 8:16], in_=ps[:, 96:104])

        A.off = 16384 + 2 * 32768 + 3 * 512
        assert A.off == mark
        A.off = 0
        yTq = A([128, 4, TU], BF16)
        zsq = A([128, 4, TU], BF16)
        assert A.off <= 16384
        A.off = mark
        qT = A([128, TU], BF16)
        kT = A([128, TU], BF16)
        v_tok = A([128, 8, 257], BF16)
        k_tok = A([128, 8, 128], BF16)
        Cpb = A([128, 8, 257], BF16)
        _c0, _c1, _c2 = A([128, 257], F32), A([128, 257], F32), A([128, 257], F32)
        Cst2 = [[_c0, _c1], [_c2, _c1]]
        Cst = list(Cst2[0])
        Cp = [A([128, 257], BF16) for _ in range(2)]
        sT = [[A([128, 128], BF16) for _ in range(2)] for _ in range(2)]
        kw = A([128, 128], BF16)
        _hs = A([128, 256], F32)
        hs_ = [_hs, _hs]
        hn_ = [A([128, 256], BF16) for _ in range(2)]
        sml_ = [A([128, 8], F32) for _ in range(2)]
        u1 = A([128, 128], F32)
        r_yTq = [[Res("yTq%d_%d" % (c, t)) for t in range(2)] for c in range(4)]
        r_zsq, r_qT, r_kT = Res("zsq"), Res("qT"), Res("kT")
        r_vt = [Res("vt%d" % c) for c in range(8)]
        r_kt = [Res("kt%d" % c) for c in range(8)]
        r_Cpb = [Res("Cpb%d" % c) for c in range(8)]
        _r0, _r1, _r2 = Res("Cst00"), Res("Cst01"), Res("Cst10")
        r_Cst2 = [[_r0, _r1], [_r2, _r1]]
        r_Cst = list(r_Cst2[0])
        r_kw, r_u1 = Res("kw"), Res("u1")
        _rhs = Res("hs")
        r_hs_ = [_rhs, _rhs]
        r_hn_ = [Res("hn0"), Res("hn1")]
        r_sml_ = [Res("sml0"), Res("sml1")]
        r_Cp = [Res("Cp0"), Res("Cp1")]
        r_sT = [[Res("sT00"), Res("sT01")], [Res("sT10"), Res("sT11")]]
        self.stage(persist + [r for rr in r_yTq for r in rr] + [r_zsq, r_qT, r_kT] + r_vt + r_kt + r_Cpb + [_r0, _r1, _r2] +
                   r_Cp + [r_kw, r_u1, _rhs] + r_hn_ + r_sml_ + r_sT[0] + r_sT[1])
        R.op("dve", "memset", [], r_vt, ap=v_tok[:, :, 256:257], constant=1.0)
        gcol, scol = pc["mlstm_norm_g"], pc["mlstm_skip"]
        kscale = 128.0 ** -0.5

        def st_sel(seq, d):
            Cst[d] = Cst2[seq % 2][d]
            r_Cst[d] = r_Cst2[seq % 2][d]

        def st_init(d, h):
            if u == 0:
                R.op("dve", "memset", [], [r_Cst[d]], ap=Cst[d], constant=0.0)
            else:
                R.dma("sp", Cst[d][:, 0:256], dr["st_C"][d, h], [], [r_Cst[d]])
                R.dma("sp", Cst[d][:, 256:257], dr["st_n"][d, h].rearrange("(p o) -> p o", o=1), [], [r_Cst[d]])

        def st_out(seq, d, h):
            R.dma("sp", dr["o_C"][seq, d, h], Cst[d][:, 0:256], [r_Cst[d]], [])
            R.dma("sp", dr["o_n"][seq, d, h].rearrange("(p o) -> p o", o=1), Cst[d][:, 256:257], [r_Cst[d]], [])

        def st_decay(c, d, h):
            col = d * 8 + h
            R.op("dve", "tensor_scalar", [r_Cst[d], r_tok], [r_Cst[d]], out=Cst[d], in0=Cst[d],
                 scalar1=iwb[:, c, col:col + 1], scalar2=None, op0=ALU.mult)

        def st_update(c, d, h):
            col = d * 8 + h
            R.op("pool", "tensor_scalar", [r_kt[c], r_tok], [r_kw], out=kw, in0=k_tok[:, c, :],
                 scalar1=e_tok[:, c, col:col + 1], scalar2=None, op0=ALU.mult)
            ps, rps = self.bank()
            R.mm(ps[:, 0:257], kw, v_tok[:, c, :], True, True, [r_kw, r_vt[c]], [rps])
            R.op("dve", "tensor_tensor", [rps, r_Cst[d]], [r_Cst[d]], out=Cst[d], in0=ps[:, 0:257], in1=Cst[d], op=ALU.add)

        for pair in range(4):
            R.dma("sp", zsq, zsp[:, pair * 4:(pair + 1) * 4, :], [r_zsp], [r_zsq])
            for hp in range(2):
                h = pair * 2 + hp
                W, rW = self.wload_multi([dr["mlstm_w_q"][0][:, h * 128:(h + 1) * 128],
                                          dr["mlstm_w_k"][0][:, h * 128:(h + 1) * 128],
                                          dr["mlstm_w_v"][0][:, h * 256:(h + 1) * 256]], 16)
                for tt in range(2):
                    ts = slice(tt * 512, (tt + 1) * 512)
                    ps, rps = self.bank()
                    for kc in range(16):
                        R.mm(ps[:], W[:, kc, 0:128], xcT[:, kc, ts], kc == 0, kc == 15, [rW, r_xc[kc]], [rps])
                    self.copy(self.evac_engine(), qT[:, ts], ps[:], [rps], [r_qT])
                    ps, rps = self.bank()
                    for kc in range(16):
                        R.mm(ps[:], W[:, kc, 128:256], xcT[:, kc, ts], kc == 0, kc == 15, [rW, r_xc[kc]], [rps])
                    R.op("act", "activation", [rps], [r_kT], out=kT[:, ts], in_=ps[:], func=AF.Identity, scale=kscale)
                for c in range(8):
                    cs = slice(c * 128, (c + 1) * 128)
                    ps, rps = self.bank()
                    for kc in range(16):
                        R.mm(ps[:, 0:256], xmT[:, kc, cs], W[:, kc, 256:512], kc == 0, kc == 15, [rW, r_xm[kc]], [rps])
                    self.copy(self.evac_engine(), v_tok[:, c, 0:256], ps[:, 0:256], [rps], [r_vt[c]])
                ps, rps = self.bank()
                psb = ps[:].bitcast(BF16)
                for c in range(8):
                    R.tr(psb[:, c * 128:(c + 1) * 128], kT[:, c * 128:(c + 1) * 128], identb[:], [r_kT, rc], [rps])
                R.op("dve", "tensor_copy", [rps], r_kt, out=k_tok.rearrange("p c d -> p (c d)"), in_=psb[:, 0:1024])
                for seq in range(nseq):
                    st_sel(seq, 1)
                    st_init(1, h)
                    for c in reversed(range(seq * cps, (seq + 1) * cps)):
                        st_decay(c, 1, h)
                        R.op("act", "activation", [r_Cst[1]], [r_Cpb[c]], out=Cpb[:, c, :], in_=Cst[1], func=AF.Identity)
                        if u == 0 or c > seq * cps:
                            st_update(c, 1, h)
                    if u == 0:
                        st_out(seq, 1, h)
                def F(c):
                    cs = slice(c * 128, (c + 1) * 128)
                    pp = c % 2
                    pss, rpss = self.bank()
                    R.mm(pss[:, 0:128], kT[:, cs], qT[:, cs], True, True, [r_kT, r_qT], [rpss])
                    for d, msk in ((0, self.mle), (1, self.mge)):
                        col = d * 8 + h
                        R.op("dve", "scalar_tensor_tensor", [rpss, r_tok, rc], [r_sT[pp][d]], out=sT[pp][d], in0=pss[:, 0:128],
                             scalar=e_tok[:, c, col:col + 1], in1=msk[:], op0=ALU.mult, op1=ALU.mult)

                def Astep(c):
                    seq = c // cps
                    pp = c % 2
                    if c == seq * cps:
                        st_sel(seq, 0)
                        st_init(0, h)
                    st_decay(c, 0, h)
                    R.op("act", "activation", [r_Cst[0]], [r_Cp[pp]], out=Cp[pp], in_=Cst[0], func=AF.Identity)
                    if u == 0 or c < (seq + 1) * cps - 1:
                        st_update(c, 0, h)
                    if u == 0 and c == (seq + 1) * cps - 1:
                        st_out(seq, 0, h)

                def B1(c):
                    cs = slice(c * 128, (c + 1) * 128)
                    pp = c % 2
                    hs, hn, sml, sqj = hs_[pp], hn_[pp], sml_[pp], hn_[pp]
                    r_hs, r_hn, r_sml = r_hs_[pp], r_hn_[pp], r_sml_[pp]
                    pn = []
                    for d in range(2):
                        p_, rp_ = self.bank(hold=True)
                        R.mm(p_[:, 0:257], sT[pp][d], v_tok[:, c, :], True, False, [r_sT[pp][d], r_vt[c]], [rp_])
                        if d == 0:
                            R.mm(p_[:, 0:257], qT[:, cs], Cp[pp], False, True, [r_qT, r_Cp[pp]], [rp_])
                        else:
                            R.mm(p_[:, 0:257], qT[:, cs], Cpb[:, c, :], False, True, [r_qT, r_Cpb[c]], [rp_])
                        pn.append((p_, rp_))
                    for d in range(2):
                        col = d * 8 + h
                        p_, rp_ = pn[d]
                        R.op("act", "activation", [rp_], [r_sml], out=sml[:, d:d + 1], in_=p_[:, 256:257], func=AF.Abs)
                        R.op("dve", "tensor_tensor", [r_sml, r_tok], [r_sml], out=sml[:, d:d + 1], in0=sml[:, d:d + 1],
                             in1=emc_tok[:, c, col:col + 1], op=ALU.max)
                        R.op("dve", "reciprocal", [r_sml], [r_sml], out=sml[:, 2 + d:3 + d], in_=sml[:, d:d + 1])
                    R.op("dve", "tensor_scalar", [pn[0][1], r_sml], [r_hs], out=hs, in0=pn[0][0][:, 0:256],
                         scalar1=sml[:, 2:3], scalar2=None, op0=ALU.mult)
                    R.op("dve", "scalar_tensor_tensor", [pn[1][1], r_sml, r_hs], [r_hs], out=hs, in0=pn[1][0][:, 0:256],
                         scalar=sml[:, 3:4], in1=hs, op0=ALU.mult, op1=ALU.add)
                    self.release(pn[0][0])
                    self.release(pn[1][0])
                    R.op("act", "activation", [r_hs], [r_hn, r_sml], out=sqj, in_=hs, func=AF.Square, accum_out=sml[:, 4:5])
                    R.op("act", "activation", [r_sml], [r_sml], out=sml[:, 5:6], in_=sml[:, 4:5], func=AF.Ln, scale=1.0 / 256,
                         bias=EPS)
                    R.op("act", "activation", [r_sml], [r_sml], out=sml[:, 5:6], in_=sml[:, 5:6], func=AF.Exp, scale=-0.5)
                    R.op("dve", "tensor_scalar", [r_hs, r_sml], [r_hn], out=hn, in0=hs, scalar1=sml[:, 5:6], scalar2=None,
                         op0=ALU.mult)

                def B2(c):
                    pp = c % 2
                    hn, r_hn = hn_[pp], r_hn_[pp]
                    q4 = c % 4
                    if q4 == 0:
                        self.pt_cur = self.bank(hold=True)
                    pt, rpt = self.pt_cur
                    ptb = pt[:].bitcast(BF16)
                    for c2 in range(2):
                        R.tr(ptb[:, (c2 * 4 + q4) * 128:(c2 * 4 + q4 + 1) * 128], hn[:, c2 * 128:(c2 + 1) * 128], identb[:],
                             [r_hn, rc], [rpt])
                    if q4 == 3:
                        tl = c // 4
                        for c2 in range(2):
                            ch = 2 * h + c2
                            lc = 2 * hp + c2
                            for hf in range(4):
                                t2 = slice(tl * 512 + hf * 128, tl * 512 + (hf + 1) * 128)
                                R.op("dve", "tensor_scalar", [rpt, self.r_prm], [r_u1], out=u1,
                                     in0=ptb[:, c2 * 512 + hf * 128:c2 * 512 + (hf + 1) * 128],
                                     scalar1=self.prm[:, gcol + ch:gcol + ch + 1], scalar2=None, op0=ALU.mult)
                                R.op("dve", "scalar_tensor_tensor", [r_xc[ch], self.r_prm, r_u1], [r_u1], out=u1,
                                     in0=xcT[:, ch, t2], scalar=self.prm[:, scol + ch:scol + ch + 1], in1=u1, op0=ALU.mult,
                                     op1=ALU.add)
                                R.op("pool", "tensor_tensor", [r_u1, r_zsq], [r_yTq[lc][tl]], out=yTq[:, lc, t2], in0=u1,
                                     in1=zsq[:, lc, t2], op=ALU.mult)
                        self.release(pt)

                F(0)
                Astep(0)
                for c in range(8):
                    if c + 1 < 8:
                        F(c + 1)
                        Astep(c + 1)
                    B1(c)
                    if c > 0:
                        B2(c - 1)
                B2(7)
            Wd, rWd = self.wload(dr["mlstm_w_down"][0][pair * 512:(pair + 1) * 512, :], 4, 1024)
            for half in range(2):
                tt = u * 2 + half
                ts = slice(tt * 512, (tt + 1) * 512)
                ls = slice(half * 512, (half + 1) * 512)
                for dc in range(8):
                    ps, rps = self.bank()
                    for lc in range(4):
                        R.mm(ps[:], Wd[:, lc, dc * 128:(dc + 1) * 128], yTq[:, lc, ls], lc == 0, lc == 3,
                             [rWd, r_yTq[lc][half]], [rps])
                    R.op("dve", "scalar_tensor_tensor", [rps, self.r_mod, self.r_xT[dc][tt]], [self.r_xT[dc][tt]],
                         out=self.xT[:, dc, ts], in0=ps[:], scalar=self.mod[:, 16 + dc, u:u + 1],
                         in1=self.xT[:, dc, ts], op0=ALU.mult, op1=ALU.add)

    def diff(self, i, u):
        R = self.R
        dr = self.dram
        pc = self.pcol
        hT, r_h = self.hT, self.r_h
        r_hl = [r_h[u * 2], r_h[u * 2 + 1]]
        koff = 0 if u == 0 else 256
        Tk = TU + koff
        nkt = Tk // 128
        identf, identb, onesf = self.identf, self.identb, self.onesf
        rc = self.r_const
        scale = 64.0 ** -0.5
        lam_init = 0.8 - 0.6 * math.exp(-0.3 * i)
        wqkv = dr["diff_w_qkv"][0]

        A = self._Alloc(self, 16384)
        qT = A([128, 8, TU], BF16)
        kT = A([128, 8, 1280], BF16)
        v_tok = A([128, 10, 1024], BF16)
        lamt = A([128, 4], F32)
        gs = A([128, 1], F32)
        mark = A.off
        r_qT = [Res("qT%d" % m) for m in range(8)]
        r_kT = [Res("kT%d" % m) for m in range(8)]
        r_v = [Res("v%d" % k) for k in range(10)]
        r_lam = Res("lam")
        persist = r_qT + r_kT + r_v + [r_lam]

        rope = A([128, 2, TU], F32)
        qraw = A([128, 512], BF16)
        tA = A([128, 512], F32)
        tB = A([128, 512], F32)
        kvst = [A([128, 1024], F32) for _ in range(2)]
        perm = A([128, 128], BF16)
        lqk = A([64, 4], F32)
        r_rope, r_qraw, r_tA, r_tB, r_perm, r_lqk = Res("rope"), Res("qraw"), Res("tA"), Res("tB"), Res("perm"), Res("lqk")
        r_kvst = [Res("kvst0"), Res("kvst1")]
        self.stage(r_hl + persist + [r_rope, r_qraw, r_tA, r_tB, r_perm, r_lqk] + r_kvst)
        for k, nm in enumerate(["diff_lq1", "diff_lk1", "diff_lq2", "diff_lk2"]):
            R.dma("sp", lqk[:, k:k + 1], dr[nm][0].rearrange("(d o) -> d o", o=1), [], [r_lqk])
        R.op("dve", "tensor_tensor", [r_lqk], [r_lqk], out=lqk[:, 0:1], in0=lqk[:, 0:1], in1=lqk[:, 1:2], op=ALU.mult)
        R.op("dve", "tensor_tensor", [r_lqk], [r_lqk], out=lqk[:, 1:2], in0=lqk[:, 2:3], in1=lqk[:, 3:4], op=ALU.mult)
        ps, rps = self.bank()
        R.mm(ps[:, 0:2], onesf[0:64, :], lqk[:, 0:2], True, True, [r_lqk, rc], [rps])
        R.op("act", "activation", [rps], [r_lam], out=lamt[:, 0:2], in_=ps[:, 0:2], func=AF.Exp)
        R.op("dve", "tensor_tensor", [r_lam], [r_lam], out=lamt[:, 2:3], in0=lamt[:, 1:2], in1=lamt[:, 0:1], op=ALU.subtract)
        R.op("dve", "tensor_scalar", [r_lam], [r_lam], out=lamt[:, 2:3], in0=lamt[:, 2:3], scalar1=-lam_init, scalar2=None,
             op0=ALU.add)
        R.op("dve", "tensor_scalar", [self.r_prm], [r_lam], out=gs,
             in0=self.prm[:, pc["diff_subln_g"]:pc["diff_subln_g"] + 1], scalar1=1.0 - lam_init, scalar2=None,
             op0=ALU.mult)
        if u == 1:
            R.dma("sp", rope, dr["rope128"].rearrange("a r t -> r a t"), [], [r_rope])
            for (d0, s0) in ((0, 32), (32, 0), (64, 96), (96, 64)):
                R.op("dve", "tensor_copy", [rc], [r_perm], out=perm[:, d0:d0 + 32], in_=identb[:, s0:s0 + 32])
            ck = dr["c_dk"].rearrange("(a p) h d -> p a (h d)", p=128)
            cv = dr["c_dv"].rearrange("(a p) h d -> p a (h d)", p=128)
            for a in range(2):
                R.dma("sp", kvst[0], ck[:, a, :], [], [r_kvst[0]])
                for mh in range(2):
                    ps, rps = self.bank()
                    for q in range(4):
                        m = mh * 4 + q
                        R.tr(ps[:, q * 128:(q + 1) * 128], kvst[0][:, m * 128:(m + 1) * 128], identf[:], [r_kvst[0], rc],
                             [rps])
                    self.copy(self.evac_engine(), kT[:, mh * 4:(mh + 1) * 4, a * 128:(a + 1) * 128],
                              ps[:].rearrange("p (q t) -> p q t", q=4), [rps], r_kT[mh * 4:(mh + 1) * 4])
                R.dma("sp", kvst[1], cv[:, a, :], [], [r_kvst[1]])
                R.op("dve", "tensor_copy", [r_kvst[1]], [r_v[a]], out=v_tok[:, a, :], in_=kvst[1])

        def fm_proj(W, rW, dst, r_dst, col_off):
            for m in range(8):
                for tt in range(2):
                    ts = slice(tt * 512, (tt + 1) * 512)
                    ps, rps = self.bank()
                    for kc in range(8):
                        R.mm(ps[:], W[:, kc, m * 128:(m + 1) * 128], hT[:, kc, ts], kc == 0, kc == 7, [rW, r_hl[tt]], [rps])
                    od = dst[:, m, col_off + tt * 512:col_off + (tt + 1) * 512]
                    if u == 0:
                        self.copy(self.evac_engine(), od, ps[:], [rps], [r_dst[m]])
                    else:
                        R.op("act", "activation", [rps], [r_qraw], out=qraw, in_=ps[:], func=AF.Identity)
                        R.op("dve", "tensor_tensor", [rps, r_rope], [r_tA], out=tA, in0=ps[:], in1=rope[:, 0, ts], op=ALU.mult)
                        ps2, rps2 = self.bank()
                        R.mm(ps2[:], perm, qraw, True, True, [r_perm, r_qraw], [rps2])
                        R.op("dve", "tensor_tensor", [rps2, r_rope], [r_tB], out=tB, in0=ps2[:], in1=rope[:, 1, ts],
                             op=ALU.mult)
                        R.op("dve", "tensor_tensor", [r_tA, r_tB], [r_dst[m]], out=od, in0=tA, in1=tB, op=ALU.add)

        def tm_proj(W, rW, out_name, to_v):
            for c in range(8):
                cs = slice(c * 128, (c + 1) * 128)
                kk = c % 2
                for half in range(2):
                    ps, rps = self.bank()
                    for kc in range(8):
                        R.mm(ps[:], hT[:, kc, cs], W[:, kc, half * 512:(half + 1) * 512], kc == 0, kc == 7,
                             [rW, r_hl[c // 4]], [rps])
                    if to_v:
                        R.op("act", "activation", [rps], [r_v[koff // 128 + c]],
                             out=v_tok[:, koff // 128 + c, half * 512:(half + 1) * 512], in_=ps[:], func=AF.Identity)
                    if u == 0:
                        R.op("dve", "tensor_copy", [rps], [r_kvst[kk]], out=kvst[kk][:, half * 512:(half + 1) * 512], in_=ps[:])
                if u == 0:
                    seq, t0 = (c * 128) // LP, (c * 128) % LP
                    R.dma("sp", dr[out_name][seq, t0:t0 + 128].rearrange("t h d -> t (h d)"), kvst[kk], [r_kvst[kk]], [])

        Wq, rWq = self.wload(wqkv[:, 0:1024], 8, 1024)
        fm_proj(Wq, rWq, qT, r_qT, 0)
        Wk, rWk = self.wload(wqkv[:, 1024:2048], 8, 1024)
        fm_proj(Wk, rWk, kT, r_kT, koff)
        if u == 0:
            tm_proj(Wk, rWk, "o_dk", False)
        Wv, rWv = self.wload(wqkv[:, 2048:3072], 8, 1024)
        tm_proj(Wv, rWv, "o_dv", True)

        A.off = mark
        oT = self.av(0, [128, 8, TU], BF16)
        NS = 4
        pT = A([128, NS, 512], BF16)
        qz = [[A([128, TU], BF16) for _ in range(2)] for _ in range(2)]
        r_qz = [[Res("qz%d%d" % (a_, b_)) for b_ in range(2)] for a_ in range(2)]
        a1 = A([128, 512], F32)
        rec = A([128, 512], F32)
        o32 = A([128, 512], F32)
        sqo = A([128, 512], BF16)
        rstd = A([128, 512], F32)
        lnt = A([128, 512], F32)
        sqt = A([128, 512], BF16)
        red = A([128, 8], F32)
        negm = A([128, 2], F32)
        r_oT = [[Res("oT%d_%d" % (c, t)) for t in range(2)] for c in range(8)]
        r_pT = [Res("pT%d" % k) for k in range(NS)]
        r_a1, r_rec, r_o32, r_sqo, r_rstd, r_sqt, r_red, r_negm = (Res("a1"), Res("rec"), Res("o32"), Res("sqo"),
                                                                   Res("rstd"), Res("sqt"), Res("red"), Res("negm"))
        self.stage(persist + [r for rr in r_oT for r in rr] + r_pT + r_qz[0] + r_qz[1] +
                   [r_a1, r_rec, r_o32, r_sqo, r_rstd, r_sqt, r_red, r_negm])
        for a_ in range(2):
            for b_ in range(2):
                R.op("dve", "memset", [], [r_qz[a_][b_]], ap=qz[a_][b_], constant=0.0)
        if u == 1:
            blocks = [(0, 512, list(range(10))), (512, 512, list(range(10)))]
        else:
            blocks = [(s * 256, 256, [2 * s, 2 * s + 1]) for s in range(4)]
        for h in range(8):
            for br in range(2):
                rows = slice(br * 64, (br + 1) * 64)
                self.sq_bound([qT[rows, h, :]], [r_qT[h]], TU, 0, sqt, r_sqt, red, r_red, p0=br * 64)
                self.sq_bound([kT[rows, h, 0:Tk]], [r_kT[h]], Tk, 1, sqt, r_sqt, red, r_red, p0=br * 64)
                R.op("dve", "tensor_tensor", [r_red], [r_red], out=red[:, 2:3], in0=red[:, 0:1], in1=red[:, 1:2], op=ALU.add)
                R.op("dve", "tensor_scalar", [r_red], [r_negm], out=negm[:, br:br + 1], in0=red[:, 2:3],
                     scalar1=-0.5 * scale, scalar2=None, op0=ALU.mult)

            def fin1(q0, nq, po, r_po, psm, r_psm):
                R.op("dve", "reciprocal", [r_psm], [r_rec], out=rec[:, 0:nq], in_=psm[:, 0:nq])
                R.op("dve", "tensor_tensor", [r_po, r_rec], [r_a1], out=a1[:, 0:nq], in0=po[:, 0:nq], in1=rec[:, 0:nq],
                     op=ALU.mult)

            def fin2(q0, nq, po, r_po, psm, r_psm, h=h):
                R.op("dve", "reciprocal", [r_psm], [r_rec], out=rec[:, 0:nq], in_=psm[:, 0:nq])
                R.op("dve", "tensor_tensor", [r_po, r_rec], [r_o32], out=o32[:, 0:nq], in0=po[:, 0:nq], in1=rec[:, 0:nq],
                     op=ALU.mult)
                R.op("dve", "scalar_tensor_tensor", [r_o32, r_a1, r_lam], [r_o32], out=o32[:, 0:nq], in0=o32[:, 0:nq],
                     scalar=lamt[:, 2:3], in1=a1[:, 0:nq], op0=ALU.mult, op1=ALU.add)
                R.op("act", "activation", [r_o32], [r_sqo], out=sqo[:, 0:nq], in_=o32[:, 0:nq], func=AF.Square)
                ps, rps = self.bank()
                R.mm(ps[:, 0:nq], self.onesb[:], sqo[:, 0:nq], True, True, [r_sqo, rc], [rps])
                R.op("act", "activation", [rps], [r_rstd], out=lnt[:, 0:nq], in_=ps[:, 0:nq], func=AF.Ln, scale=1.0 / 128,
                     bias=EPS)
                R.op("act", "activation", [r_rstd], [r_rstd], out=rstd[:, 0:nq], in_=lnt[:, 0:nq], func=AF.Exp, scale=-0.5)
                R.op("dve", "scalar_tensor_tensor", [r_o32, r_rstd, r_lam], [r_oT[h][q0 // 512]], out=oT[:, h, q0:q0 + nq],
                     in0=o32[:, 0:nq], scalar=gs[:, 0:1], in1=rstd[:, 0:nq], op0=ALU.mult, op1=ALU.mult)

            hq = h % 2
            for br in range(2):
                rows = slice(br * 64, (br + 1) * 64)
                self.copy(self.evac_engine(), qz[hq][br][rows, :], qT[rows, h, :], [r_qT[h]], [r_qz[hq][br]])
            for blk in blocks:
                for br, fin in ((0, fin1), (1, fin2)):
                    self.attn_core([(kT[:, h, :], qz[hq][br])], [r_kT[h], r_qz[hq][br]],
                                   lambda kt, h=h: v_tok[:, kt, h * 128:(h + 1) * 128], None, [blk], scale,
                                   negm[:, br:br + 1], r_negm, pT, r_pT, fin, v_res_fn=lambda kt: r_v[kt])

        self.attn_flush()

        Wo, rWo = self.wload(dr["diff_w_o"][0], 8, 1024)
        for half in range(2):
            tt = u * 2 + half
            ts = slice(tt * 512, (tt + 1) * 512)
            ls = slice(half * 512, (half + 1) * 512)
            for dc in range(8):
                ps, rps = self.bank()
                for cc in range(8):
                    R.mm(ps[:], Wo[:, cc, dc * 128:(dc + 1) * 128], oT[:, cc, ls], cc == 0, cc == 7, [rWo, r_oT[cc][half]],
                         [rps])
                R.op("dve", "scalar_tensor_tensor", [rps, self.r_mod, self.r_xT[dc][tt]], [self.r_xT[dc][tt]],
                     out=self.xT[:, dc, ts], in0=ps[:], scalar=self.mod[:, 16 + dc, u:u + 1],
                     in1=self.xT[:, dc, ts], op0=ALU.mult, op1=ALU.add)


def rope_tables(d):
    half = d // 2
    nf = half // 2
    t = np.arange(LS)
    pos_r = (t // 64).astype(np.float32)
    pos_c = (t % 64).astype(np.float32)
    inv = (10000.0 ** (-np.arange(nf, dtype=np.float32) / nf)).astype(np.float32)
    ang = np.concatenate([pos_r[:, None] * inv, pos_c[:, None] * inv], axis=-1).astype(np.float32)
    cos = np.cos(ang).astype(np.float32).T
    sin = np.sin(ang).astype(np.float32).T
    out = np.zeros((2, d, LS), np.float32)
    out[0, :half] = cos
    out[0, half:] = cos
    out[1, :half] = -sin
    out[1, half:] = sin
    return out


def make_in_maps(inputs):
    f = lambda a: np.ascontiguousarray(np.asarray(a, dtype=np.float32))
    shared = {}
    for name, shape in IN_SPECS[12:]:
        if name in ("rope_cs", "rope128"):
            continue
        shared[name] = f(inputs[name]).reshape(shape)
    shared["rope_cs"] = rope_tables(32)
    r64 = rope_tables(64)
    shared["rope128"] = np.ascontiguousarray(np.concatenate([r64, r64], axis=1))
    shared["c_ctx"] = f(inputs["c_ctx"])
    maps = []
    for c in range(N_CORES):
        s = c % 2
        m = dict(shared)
        m["xp"] = f(inputs["x_prompt"][c * NPS:(c + 1) * NPS])
        m["xs"] = f(inputs["x_sample"][s])
        m["st_ssd"] = f(inputs["state_ssd"][s, 0])
        m["c_ckv"] = f(inputs["cache_mla_ckv"][s, 0])
        m["c_kr"] = f(inputs["cache_mla_krope"][s, 0])
        m["st_C"] = f(inputs["state_mlstm_C"][s, 0])
        m["st_n"] = f(inputs["state_mlstm_n"][s, 0])
        m["st_m"] = f(inputs["state_mlstm_m"][s, 0])
        m["c_dk"] = f(inputs["cache_diff_k"][s, 0])
        m["c_dv"] = f(inputs["cache_diff_v"][s, 0])
        m["c_s"] = f(inputs["c"][s])
        maps.append(m)
    return maps


def assemble(results):
    cat = lambda k: np.concatenate([r[k] for r in results], axis=0)
    y_prompt = cat("yp")
    y_sample = np.stack([results[0]["ys"], results[1]["ys"]], axis=0)
    exp1 = lambda a: a[:, None]
    return (y_prompt, y_sample, exp1(cat("o_ssd")), exp1(cat("o_ckv")), exp1(cat("o_kr")),
            exp1(cat("o_C")), exp1(cat("o_n")), exp1(cat("o_m")), exp1(cat("o_dk")), exp1(cat("o_dv")))


_CACHE = {}


def kernel(**inputs):
    if "nc" not in _CACHE:
        b = Builder()
        _CACHE["nc"] = b.build()
    nc = _CACHE["nc"]
    maps = make_in_maps(inputs)
    res = run_bass_kernel_spmd(nc, maps, core_ids=list(range(N_CORES)))
    return assemble(res.results)
```

```python
import math
from contextlib import ExitStack

import numpy as np
import concourse.bass as bass
import concourse.mybir as mybir
from concourse.bass_utils import run_bass_kernel_spmd

F32 = mybir.dt.float32
BF16 = mybir.dt.bfloat16
ALU = mybir.AluOpType
AF = mybir.ActivationFunctionType
AX = mybir.AxisListType

ENGS = ("pe", "act", "dve", "pool", "sp")

D = 1024
NPS = 4
LP = 256
LS = 1024
TU = 1024
TOK = 2048
DFF = 4096
EPS = 1e-6
N_CORES = 8
SSD_BIG = 20000.0


class Res:
    __slots__ = ("name", "lw", "rd", "excl")

    def __init__(self, name="", excl=False):
        self.name = name
        self.lw = None
        self.rd = {}
        self.excl = excl


class Op:
    __slots__ = ("fn", "waits", "sig", "lane", "lane_k")

    def __init__(self, fn):
        self.fn = fn
        self.waits = []
        self.sig = False
        self.lane = None
        self.lane_k = 0


class Rec:
    def __init__(self, nc, n_lanes_sp=8, n_lanes_pool=4, n_lanes_act=2):
        self.nc = nc
        self.ops = {e: [] for e in ENGS}
        self.waited = {e: {} for e in ENGS}
        self.lane_cnt = {}
        self.lanes = {"sp": [("dma", "sp", i) for i in range(n_lanes_sp)],
                      "pool": [("dma", "pool", i) for i in range(n_lanes_pool)],
                      "act": [("dma", "act", i) for i in range(n_lanes_act)]}
        self.lane_rr = {"sp": 0, "pool": 0, "act": 0}
        for q in self.lanes:
            for l in self.lanes[q]:
                self.lane_cnt[l] = 0

    def emit(self, eng, fn, reads=(), writes=(), dma=False):
        deps = {}

        def add(tok):
            if tok is None:
                return
            key, idx = tok
            if deps.get(key, -1) < idx:
                deps[key] = idx

        for r in reads:
            add(r.lw)
            if r.excl:
                for k, i in r.rd.items():
                    if k != ("eng", eng):
                        add((k, i))
        for w in writes:
            add(w.lw)
            for k, i in w.rd.items():
                add((k, i))
        op = Op(fn)
        if dma:
            lanes = self.lanes[eng]
            lane = lanes[self.lane_rr[eng] % len(lanes)]
            self.lane_rr[eng] += 1
            k = self.lane_cnt[lane]
            if k > 0:
                add((lane, k))
            self.lane_cnt[lane] = k + 1
            op.lane = lane
            op.lane_k = k + 1
            tok = (lane, k + 1)
        else:
            tok = (("eng", eng), len(self.ops[eng]))
        wd = self.waited[eng]
        for key, idx in deps.items():
            if key == ("eng", "pe") and eng == "pe" and not dma:
                continue
            if wd.get(key, -1) >= idx:
                continue
            wd[key] = idx
            op.waits.append((key, idx))
            if key[0] == "eng":
                self.ops[key[1]][idx].sig = True
        for r in reads:
            if r.rd.get(tok[0], -1) < tok[1]:
                r.rd[tok[0]] = tok[1]
        for w in writes:
            w.lw = tok
            w.rd = {}
        self.ops[eng].append(op)
        return tok

    def op(self, eng, method, reads, writes, **kw):
        self.emit(eng, lambda e: getattr(e, method)(**kw), reads, writes)

    def mm(self, out, lhsT, rhs, start, stop, reads, writes):
        self.emit("pe", lambda e: e.matmul(out, lhsT, rhs, start=start, stop=stop), reads, writes)

    def tr(self, out, in_, ident, reads, writes):
        self.emit("pe", lambda e: e.transpose(out, in_, ident), reads, writes)

    def dma(self, q, out, in_, reads, writes, **kw):
        self.emit(q, lambda e: e.dma_start(out=out, in_=in_, **kw), reads, writes, dma=True)

    def replay(self, stack):
        nc = self.nc
        sems = {}
        for e in ENGS:
            sems[("eng", e)] = stack.enter_context(nc.semaphore("s_" + e))
        for q in self.lanes:
            for l in self.lanes[q]:
                if self.lane_cnt[l] > 0:
                    sems[l] = stack.enter_context(nc.semaphore("l_%s%d" % (l[1], l[2])))
        rank = {}
        for e in ENGS:
            c = 0
            r = []
            for op in self.ops[e]:
                if op.sig:
                    c += 1
                r.append(c)
            rank[e] = r

        def val(key, idx):
            if key[0] == "eng":
                return rank[key[1]][idx]
            return 16 * idx

        block = stack.enter_context(nc.Block())
        ops = self.ops
        lanes = self.lanes
        lane_cnt = self.lane_cnt

        def run(eng_name, eng):
            for op in ops[eng_name]:
                for key, idx in op.waits:
                    eng.wait_ge(sems[key], val(key, idx))
                inst = op.fn(eng)
                if op.lane is not None:
                    inst.then_inc(sems[op.lane], 16)
                elif op.sig:
                    inst.then_inc(sems[("eng", eng_name)], 1)
            if eng_name in lanes:
                for l in lanes[eng_name]:
                    if lane_cnt[l] > 0:
                        eng.wait_ge(sems[l], 16 * lane_cnt[l])

        @block.tensor
        def _(e):
            run("pe", e)

        @block.scalar
        def _(e):
            run("act", e)

        @block.vector
        def _(e):
            run("dve", e)

        @block.gpsimd
        def _(e):
            run("pool", e)

        @block.sync
        def _(e):
            run("sp", e)

    def stats(self):
        return {e: (len(self.ops[e]), sum(1 for o in self.ops[e] if o.sig),
                    sum(len(o.waits) for o in self.ops[e])) for e in ENGS}


IN_SPECS = [
    ("xp", [NPS, LP, D]), ("xs", [LS, D]),
    ("st_ssd", [2, 32, 64, 128]), ("c_ckv", [256, 256]), ("c_kr", [256, 32]),
    ("st_C", [2, 8, 128, 256]), ("st_n", [2, 8, 128]), ("st_m", [2, 8]),
    ("c_dk", [256, 8, 128]), ("c_dv", [256, 8, 128]),
    ("c_s", [D]), ("c_ctx", [D]),
    ("norm1_g", [4, D]), ("norm2_g", [4, D]), ("ada_w", [4, D, 6 * D]), ("ada_b", [4, 6 * D]),
    ("mlp_w1", [4, D, DFF]), ("mlp_w2", [4, DFF, D]), ("final_g", [D]),
    ("ssd_w_in", [1, D, 6208]), ("ssd_conv_w", [1, 5, 4096]), ("ssd_conv_b", [1, 4096]),
    ("ssd_dt_bias", [1, 2, 32]), ("ssd_A_log", [1, 2, 32]), ("ssd_D", [1, 32]),
    ("ssd_norm_g", [1, 2048]), ("ssd_w_out", [1, 2048, D]),
    ("mla_w_in", [1, D, 800]), ("mla_q_norm_g", [1, 512]), ("mla_kv_norm_g", [1, 256]),
    ("mla_w_uq", [1, 512, 1536]), ("mla_w_ukv", [1, 256, 2048]), ("mla_w_o", [1, 1024, D]),
    ("mlstm_w_up", [1, D, 4128]), ("mlstm_conv_w", [1, 5, 2048]), ("mlstm_conv_b", [1, 2048]),
    ("mlstm_gate_b", [1, 2, 2, 8]), ("mlstm_w_q", [1, 2048, 1024]), ("mlstm_w_k", [1, 2048, 1024]),
    ("mlstm_w_v", [1, 2048, 2048]), ("mlstm_skip", [1, 2048]), ("mlstm_norm_g", [1, 2048]),
    ("mlstm_w_down", [1, 2048, D]),
    ("diff_w_qkv", [1, D, 3 * D]), ("diff_lq1", [1, 64]), ("diff_lk1", [1, 64]),
    ("diff_lq2", [1, 64]), ("diff_lk2", [1, 64]), ("diff_subln_g", [1, 128]), ("diff_w_o", [1, D, D]),
    ("rope_cs", [2, 32, LS]), ("rope128", [2, 128, LS]),
]
OUT_SPECS = [
    ("yp", [NPS, LP, D]), ("ys", [LS, D]),
    ("o_ssd", [NPS, 2, 32, 64, 128]), ("o_ckv", [NPS, LP, 256]), ("o_kr", [NPS, LP, 32]),
    ("o_C", [NPS, 2, 8, 128, 256]), ("o_n", [NPS, 2, 8, 128]), ("o_m", [NPS, 2, 8]),
    ("o_dk", [NPS, LP, 8, 128]), ("o_dv", [NPS, LP, 8, 128]),
]

ARENA_BYTES = 104 * 1024


class Builder:
    def __init__(self, depth=4, mixers=(0, 1, 2, 3), dbg=()):
        self.depth = depth
        self.mixers = set(mixers)
        self.dbg = dbg
        self.nc = bass.Bass("TRN2", target_bir_lowering=False)
        self.R = Rec(self.nc)
        self.dram = {}
        self.dbg_out = {}

    def sb(self, name, shape, dt):
        return self.st.enter_context(self.nc.sbuf_tensor(name, shape, dt))

    def av(self, off, shape, dt):
        esz = 2 if dt == BF16 else 4
        n = 1
        for s in shape[1:]:
            n *= s
        nb = n * esz
        assert off % 4 == 0 and off + nb <= ARENA_BYTES, (off, nb)
        ap = self.arena[0:shape[0], off // 2: (off + nb) // 2]
        if dt != BF16:
            ap = ap.bitcast(dt)
        if len(shape) == 2:
            return ap
        names = " ".join("d%d" % i for i in range(1, len(shape)))
        kw = {"d%d" % i: shape[i] for i in range(1, len(shape))}
        return ap.rearrange("p (%s) -> p %s" % (names, names), **kw)

    def bank(self, hold=False):
        while True:
            b = self.bank_i % 8
            self.bank_i += 1
            if b not in self.held:
                break
        if hold:
            self.held.add(b)
        return self.ps[b], self.ps_res[b]

    def release(self, ps):
        for b in range(8):
            if self.ps[b] is ps:
                self.held.discard(b)

    def stage(self, new_res):
        R = self.R
        allr = list(self.arena_live) + list(new_res)
        d = self.dummy
        R.emit("dve", lambda e: e.memset(d[:, 0:1], 0.0), [], allr + [self.r_dummy])
        self.arena_live = list(new_res)

    def evac_engine(self):
        self.ev_i += 1
        return "act" if self.ev_i % 2 == 0 else "dve"

    def copy(self, eng, out, in_, reads, writes):
        if eng == "act":
            self.R.op("act", "activation", reads, writes, out=out, in_=in_, func=AF.Identity)
        else:
            self.R.op(eng, "tensor_copy", reads, writes, out=out, in_=in_)

    def wload(self, src_ap, kc_n, ncols):
        assert kc_n * ncols <= 8192
        i = self.wb_i % len(self.wb)
        self.wb_i += 1
        view = self.wb[i][:, 0:kc_n * ncols].rearrange("p (kc n) -> p kc n", kc=kc_n)
        self.R.dma("pool", view, src_ap.rearrange("(kc p) n -> p kc n", p=128), [], [self.wb_res[i]])
        return view, self.wb_res[i]

    def debug_dump(self, name, ap, res, shape, dt=F32):
        if name not in self.dbg:
            return
        t = self.nc.dram_tensor("dbg_" + name, list(shape), dt, kind="ExternalOutput").ap()
        self.dbg_out[name] = list(shape)
        self.R.dma("sp", t, ap, res, [])

    def build(self):
        nc = self.nc
        R = self.R
        for name, shape in IN_SPECS:
            self.dram[name] = nc.dram_tensor(name, list(shape), F32, kind="ExternalInput").ap()
        for name, shape in OUT_SPECS:
            self.dram[name] = nc.dram_tensor(name, list(shape), F32, kind="ExternalOutput").ap()
        with ExitStack() as st:
            self.st = st
            self.xT = self.sb("xT", [128, 8, TOK], F32)
            self.r_xT = [[Res("xT%d_%d" % (dc, tt)) for tt in range(4)] for dc in range(8)]
            self.wb = [self.sb("wb%d" % i, [128, 8192], BF16) for i in range(2)]
            self.wb_res = [Res("wb%d" % i) for i in range(2)]
            self.wb_i = 0
            self.arena = self.sb("arena", [128, ARENA_BYTES // 2], BF16)
            self.arena_live = []
            self.identf = self.sb("identf", [128, 128], F32)
            self.identb = self.sb("identb", [128, 128], BF16)
            self.onesb = self.sb("onesb", [128, 128], BF16)
            self.onesf = self.sb("onesf", [128, 128], F32)
            self.prm = self.sb("prm", [128, 640], F32)
            self.mod = self.sb("mod", [128, 48, 2], F32)
            self.scT = self.sb("scT", [128, 8, 2], BF16)
            self.dummy = self.sb("sdummy", [128, 2], F32)
            self.mle = self.sb("mle", [128, 128], F32)
            self.mge = self.sb("mge", [128, 128], F32)
            self.sm64 = self.sb("sm64", [64, 4], F32)
            self.dbc = self.sb("dbc", [128, 32], F32)
            self.r_const = Res("const")
            self.r_prm = Res("prm")
            self.r_mod = Res("mod")
            self.r_scT = Res("scT")
            self.r_dummy = Res("dummy")
            self.ps = [st.enter_context(nc.psum_tensor("ps%d" % i, [128, 512], F32)) for i in range(8)]
            self.ps_res = [Res("ps%d" % i, excl=True) for i in range(8)]
            self.bank_i = 0
            self.held = set()
            self.pT_i = 0
            self.pending_fin = None
            self.ev_i = 0

            self.setup()
            self.debug_dump("xT0", self.xT[:], [r for rr in self.r_xT for r in rr], [128, 8, TOK])
            self.debug_dump("prm", self.prm[:], [self.r_prm], [128, 640])
            self.debug_dump("scT", self.scT[:], [self.r_scT], [128, 8, 2], BF16)
            for i in range(self.depth):
                self.adaln(i)
                if i == 0:
                    self.debug_dump("mod0", self.mod[:], [self.r_mod], [128, 48, 2])
                kind = i % 4
                if kind in self.mixers:
                    for u in range(2):
                        self.norm_mod(i, 0, [u * 2, u * 2 + 1], local=True)
                        [self.ssd, self.mla, self.mlstm, self.diff][kind](i, u)
                self.norm_mod(i, 1, [0, 1, 2, 3])
                if i == 0:
                    self.debug_dump("hT0", self.hT, self.r_h, [128, 8, TOK], BF16)
                self.mlp(i)
                if i == 0:
                    self.debug_dump("xT1", self.xT[:], [r for rr in self.r_xT for r in rr], [128, 8, TOK])
            self.final()
            self.stats = R.stats()
            R.replay(st)
        return nc

    def setup(self):
        R = self.R
        nc = self.nc
        dr = self.dram
        identf, identb, onesb, onesf = self.identf, self.identb, self.onesb, self.onesf
        rc = self.r_const
        R.emit("pool", lambda e: e.memset(identf[:], 0.0), [], [rc])
        R.emit("pool", lambda e: e.affine_select(out=identf[:], in_=identf[:], pattern=[[-1, 128]],
                                                 compare_op=ALU.not_equal, fill=1.0, base=0,
                                                 channel_multiplier=1), [rc], [rc])
        R.emit("dve", lambda e: e.tensor_copy(out=identb[:], in_=identf[:]), [rc], [rc])
        R.emit("dve", lambda e: e.memset(onesb[:], 1.0), [rc], [rc])
        R.emit("dve", lambda e: e.memset(onesf[:], 1.0), [rc], [rc])
        R.op("pool", "affine_select", [rc], [rc], out=self.mle[:], in_=onesf[:], pattern=[[1, 128]],
             compare_op=ALU.is_ge, fill=0.0, base=0, channel_multiplier=-1)
        R.op("pool", "affine_select", [rc], [rc], out=self.mge[:], in_=onesf[:], pattern=[[-1, 128]],
             compare_op=ALU.is_ge, fill=0.0, base=0, channel_multiplier=1)
        R.dma("sp", self.sm64[:, 0:1], dr["ssd_dt_bias"][0].rearrange("d (h o) -> (d h) o", o=1), [], [rc])
        R.dma("sp", self.sm64[:, 1:2], dr["ssd_A_log"][0].rearrange("d (h o) -> (d h) o", o=1), [], [rc])
        R.dma("sp", self.dbc[:], dr["ssd_D"][0].partition_broadcast(128), [], [rc])
        R.op("act", "activation", [rc], [rc], out=self.sm64[:, 2:3], in_=self.sm64[:, 1:2], func=AF.Exp)
        R.op("dve", "tensor_scalar", [rc], [rc], out=self.sm64[:, 2:3], in0=self.sm64[:, 2:3], scalar1=-1.0,
             scalar2=None, op0=ALU.mult)
        self.stage([])

        rows = [
            ("c_ctx", dr["c_ctx"].rearrange("(c p) -> c p", p=128)),
            ("c_s", dr["c_s"].rearrange("(c p) -> c p", p=128)),
            ("norm1_g", dr["norm1_g"].rearrange("l (c p) -> (l c) p", p=128)),
            ("norm2_g", dr["norm2_g"].rearrange("l (c p) -> (l c) p", p=128)),
            ("final_g", dr["final_g"].rearrange("(c p) -> c p", p=128)),
            ("ada_b", dr["ada_b"].rearrange("l (c p) -> (l c) p", p=128)),
            ("ssd_conv_w", dr["ssd_conv_w"][0].rearrange("k (c p) -> (k c) p", p=128)),
            ("ssd_conv_b", dr["ssd_conv_b"][0].rearrange("(c p) -> c p", p=128)),
            ("ssd_norm_g", dr["ssd_norm_g"][0].rearrange("(c p) -> c p", p=128)),
            ("mla_q_norm_g", dr["mla_q_norm_g"][0].rearrange("(c p) -> c p", p=128)),
            ("mla_kv_norm_g", dr["mla_kv_norm_g"][0].rearrange("(c p) -> c p", p=128)),
            ("mlstm_conv_w", dr["mlstm_conv_w"][0].rearrange("k (c p) -> (k c) p", p=128)),
            ("mlstm_conv_b", dr["mlstm_conv_b"][0].rearrange("(c p) -> c p", p=128)),
            ("mlstm_skip", dr["mlstm_skip"][0].rearrange("(c p) -> c p", p=128)),
            ("mlstm_norm_g", dr["mlstm_norm_g"][0].rearrange("(c p) -> c p", p=128)),
            ("diff_subln_g", dr["diff_subln_g"][0].rearrange("(c p) -> c p", p=128)),
        ]
        self.pcol = {}
        off = 0
        for name, ap in rows:
            self.pcol[name] = off
            off += ap.shape[0]
        total = off
        assert total <= 640
        ntile = (total + 127) // 128
        stg = [self.av(i * 512, [128, 128], F32) for i in range(ntile)]
        r_stg = [Res("stg%d" % i) for i in range(ntile)]
        self.stage(r_stg)
        for i in range(ntile):
            R.emit("dve", lambda e, i=i: e.memset(stg[i], 0.0), [], [r_stg[i]])
        for name, ap in rows:
            o = self.pcol[name]
            n = ap.shape[0]
            s = 0
            while s < n:
                t = (o + s) // 128
                p0 = (o + s) % 128
                m = min(n - s, 128 - p0)
                R.dma("sp", stg[t][p0:p0 + m, :], ap[s:s + m, :], [], [r_stg[t]])
                s += m
        for i in range(ntile):
            ps, rps = self.bank()
            R.tr(ps[:, 0:128], stg[i], identf[:], [r_stg[i], rc], [rps])
            w = min(128, total - i * 128)
            R.emit("dve", lambda e, i=i, ps=ps, w=w: e.tensor_copy(out=self.prm[:, i * 128:i * 128 + w],
                                                                  in_=ps[:, 0:w]), [rps], [self.r_prm])
        pc = self.pcol
        for j, nm in enumerate(["c_ctx", "c_s"]):
            R.emit("act", lambda e, j=j, nm=nm: e.activation(out=self.scT[:, :, j],
                                                            in_=self.prm[:, pc[nm]:pc[nm] + 8], func=AF.Silu),
                   [self.r_prm], [self.r_scT])

        nst = 4
        xst = [self.av(4096 + i * 4096, [128, D], F32) for i in range(nst)]
        r_xst = [Res("xst%d" % i) for i in range(nst)]
        self.stage(r_xst)
        for tt in range(16):
            if tt < 8:
                src = dr["xp"][tt // 2, (tt % 2) * 128:(tt % 2 + 1) * 128, :]
            else:
                src = dr["xs"][(tt - 8) * 128:(tt - 7) * 128, :]
            s = tt % nst
            R.dma("sp", xst[s], src, [], [r_xst[s]])
            for half in range(2):
                ps, rps = self.bank()
                for j in range(4):
                    dc = half * 4 + j
                    R.tr(ps[:, j * 128:(j + 1) * 128], xst[s][:, dc * 128:(dc + 1) * 128], identf[:],
                         [r_xst[s], rc], [rps])
                self.copy(self.evac_engine(), self.xT[:, half * 4:(half + 1) * 4, tt * 128:(tt + 1) * 128],
                          ps[:].rearrange("p (j t) -> p j t", j=4), [rps],
                          [self.r_xT[half * 4 + j][tt // 4] for j in range(4)])

    def adaln(self, i):
        R = self.R
        dr = self.dram
        pc = self.pcol
        ps, rps = self.bank(hold=True)
        for blk in range(6):
            wv, rw = self.wload(dr["ada_w"][i][:, blk * 1024:(blk + 1) * 1024], 8, 1024)
            for f in range(8):
                fc = blk * 8 + f
                for kc in range(8):
                    R.mm(ps[:, fc * 2:fc * 2 + 2], wv[:, kc, f * 128:(f + 1) * 128], self.scT[:, kc, :],
                         kc == 0, kc == 7, [rw, self.r_scT], [rps])
        mod = self.mod
        ab = self.prm[:, pc["ada_b"] + i * 48: pc["ada_b"] + (i + 1) * 48]
        R.emit("dve", lambda e: e.tensor_tensor(out=mod[:], in0=ps[:, 0:96].rearrange("p (c u) -> p c u", u=2),
                                                in1=ab.unsqueeze(2).to_broadcast([128, 48, 2]), op=ALU.add),
               [rps, self.r_prm], [self.r_mod])
        self.release(ps)
        for (sc0, gname) in ((8, "norm1_g"), (32, "norm2_g")):
            g = self.prm[:, pc[gname] + i * 8: pc[gname] + (i + 1) * 8]
            R.emit("dve", lambda e, sc0=sc0: e.tensor_scalar(out=mod[:, sc0:sc0 + 8, :], in0=mod[:, sc0:sc0 + 8, :],
                                                             scalar1=1.0, scalar2=None, op0=ALU.add),
                   [self.r_mod], [self.r_mod])
            R.emit("dve", lambda e, sc0=sc0, g=g: e.tensor_tensor(out=mod[:, sc0:sc0 + 8, :],
                                                                  in0=mod[:, sc0:sc0 + 8, :],
                                                                  in1=g.unsqueeze(2).to_broadcast([128, 8, 2]),
                                                                  op=ALU.mult),
                   [self.r_mod, self.r_prm], [self.r_mod])

    def rstd_tile(self, tt, sq, r_sq, lnt, rstd, r_tmp):
        R = self.R
        ts = slice(tt * 512, (tt + 1) * 512)
        R.op("act", "activation", [self.r_xT[dc][tt] for dc in range(8)], [r_sq],
             out=sq, in_=self.xT[:, :, ts], func=AF.Square)
        ps, rps = self.bank()
        for dc in range(8):
            R.mm(ps[:], self.onesb[:], sq[:, dc, :], dc == 0, dc == 7, [r_sq, self.r_const], [rps])
        R.op("act", "activation", [rps, self.r_const], [r_tmp],
             out=lnt, in_=ps[:], func=AF.Ln, scale=1.0 / D, bias=EPS)
        R.op("act", "activation", [r_tmp], [r_tmp], out=rstd, in_=lnt, func=AF.Exp, scale=-0.5)

    def norm_mod(self, i, which, tts, local=False):
        R = self.R
        if local:
            hT = self.av(0, [128, 8, TU], BF16)
        else:
            hT = self.av(0, [128, 8, TOK], BF16)
        r_h = [Res("hT%d" % tt) for tt in range(4)]
        base = 65536
        sq = [self.av(base + k * 8192, [128, 8, 512], BF16) for k in range(2)]
        lnt = [self.av(base + 16384 + k * 2048, [128, 512], F32) for k in range(2)]
        rstd = [self.av(base + 20480 + k * 2048, [128, 512], F32) for k in range(2)]
        tmp = [self.av(base + 24576 + k * 2048, [128, 512], F32) for k in range(2)]
        r_sq = [Res("sq0"), Res("sq1")]
        r_tmp = [Res("nt0"), Res("nt1")]
        r_t2 = [Res("t2a"), Res("t2b")]
        self.stage(r_h + r_sq + r_tmp + r_t2)
        self.hT, self.r_h = hT, r_h
        sh0 = 0 if which == 0 else 24
        sc0 = 8 if which == 0 else 32

        def stats(j):
            k = j % 2
            self.rstd_tile(tts[j], sq[k], r_sq[k], lnt[k], rstd[k], r_tmp[k])

        stats(0)
        for j, tt in enumerate(tts):
            if j + 1 < len(tts):
                stats(j + 1)
            kk = j % 2
            u = tt // 2
            ts = slice(tt * 512, (tt + 1) * 512)
            for dc in range(8):
                k = dc % 2
                R.op("dve", "tensor_tensor", [self.r_xT[dc][tt], r_tmp[kk]], [r_t2[k]],
                     out=tmp[k], in0=self.xT[:, dc, ts], in1=rstd[kk], op=ALU.mult)
                ots = slice((tt % 2) * 512, (tt % 2 + 1) * 512) if local else ts
                R.op("act", "activation", [r_t2[k], self.r_mod], [r_h[tt]],
                     out=hT[:, dc, ots], in_=tmp[k], func=AF.Identity,
                     scale=self.mod[:, sc0 + dc, u:u + 1], bias=self.mod[:, sh0 + dc, u:u + 1])

    def mlp(self, i):
        R = self.R
        dr = self.dram
        hT, r_h = self.hT, self.r_h
        hid = self.av(32768, [128, 8, TOK], BF16)
        r_hid = [[Res("hid%d_%d" % (f, tt)) for tt in range(4)] for f in range(8)]
        rl = [self.av(65536 + k * 2048, [128, 512], F32) for k in range(2)]
        r_rl = [Res("rl0"), Res("rl1")]
        self.stage(r_h + [r for rr in r_hid for r in rr] + r_rl)
        k = 0
        for fg in range(4):
            w1, rw1 = self.wload(dr["mlp_w1"][i][:, fg * 1024:(fg + 1) * 1024], 8, 1024)
            for f in range(8):
                for tt in range(4):
                    ts = slice(tt * 512, (tt + 1) * 512)
                    ps, rps = self.bank()
                    for kc in range(8):
                        R.mm(ps[:], w1[:, kc, f * 128:(f + 1) * 128], hT[:, kc, ts], kc == 0, kc == 7,
                             [rw1, r_h[tt]], [rps])
                    kk = k % 2
                    k += 1
                    R.op("act", "activation", [rps], [r_rl[kk]], out=rl[kk], in_=ps[:], func=AF.Relu)
                    R.op("dve", "tensor_tensor", [r_rl[kk]], [r_hid[f][tt]],
                         out=hid[:, f, ts], in0=rl[kk], in1=rl[kk], op=ALU.mult)
            w2, rw2 = self.wload(dr["mlp_w2"][i][fg * 1024:(fg + 1) * 1024, :], 8, 1024)
            for tt in range(4):
                u = tt // 2
                ts = slice(tt * 512, (tt + 1) * 512)
                for dc in range(8):
                    ps, rps = self.bank()
                    for f in range(8):
                        R.mm(ps[:], w2[:, f, dc * 128:(dc + 1) * 128], hid[:, f, ts], f == 0, f == 7,
                             [rw2, r_hid[f][tt]], [rps])
                    R.op("dve", "scalar_tensor_tensor", [rps, self.r_mod, self.r_xT[dc][tt]], [self.r_xT[dc][tt]],
                         out=self.xT[:, dc, ts], in0=ps[:], scalar=self.mod[:, 40 + dc, u:u + 1],
                         in1=self.xT[:, dc, ts], op0=ALU.mult, op1=ALU.add)

    def final(self):
        R = self.R
        dr = self.dram
        pc = self.pcol
        sq = self.av(65536, [128, 8, 512], BF16)
        lnt = self.av(65536 + 8192, [128, 512], F32)
        rstd = self.av(65536 + 8192 + 2048, [128, 512], F32)
        yT = self.av(0, [128, 8, 512], F32)
        yo = [self.av(16384 + k * 4096, [128, D], F32) for k in range(4)]
        r_sq, r_tmp = Res("sq"), Res("nt")
        r_yT = [Res("yT%d" % dc) for dc in range(8)]
        r_yo = [Res("yo%d" % k) for k in range(4)]
        self.stage([r_sq, r_tmp] + r_yT + r_yo)
        ko = 0
        for tt in range(4):
            ts = slice(tt * 512, (tt + 1) * 512)
            self.rstd_tile(tt, sq, r_sq, lnt, rstd, r_tmp)
            for dc in range(8):
                g = self.prm[:, pc["final_g"] + dc: pc["final_g"] + dc + 1]
                R.op("dve", "scalar_tensor_tensor", [self.r_xT[dc][tt], r_tmp, self.r_prm], [r_yT[dc]],
                     out=yT[:, dc, :], in0=self.xT[:, dc, ts], scalar=g, in1=rstd, op0=ALU.mult, op1=ALU.mult)
            for n in range(4):
                k = ko % 4
                ko += 1
                for half in range(2):
                    ps, rps = self.bank()
                    for j in range(4):
                        dc = half * 4 + j
                        R.tr(ps[:, j * 128:(j + 1) * 128], yT[:, dc, n * 128:(n + 1) * 128], self.identf[:],
                             [r_yT[dc], self.r_const], [rps])
                    self.copy(self.evac_engine(), yo[k][:, half * 512:(half + 1) * 512], ps[:], [rps], [r_yo[k]])
                tok0 = tt * 512 + n * 128
                if tok0 < TU:
                    dst = dr["yp"][tok0 // LP, (tok0 % LP):(tok0 % LP) + 128, :]
                else:
                    dst = dr["ys"][tok0 - TU: tok0 - TU + 128, :]
                R.dma("sp", dst, yo[k], [r_yo[k]], [])

    class _Alloc:
        def __init__(self, b, off):
            self.b = b
            self.off = off

        def __call__(self, shape, dt):
            esz = 2 if dt == BF16 else 4
            n = 1
            for x in shape[1:]:
                n *= x
            nb = (n * esz + 3) // 4 * 4
            v = self.b.av(self.off, shape, dt)
            self.off += nb
            return v

    def wload_multi(self, parts, kc_n):
        tot = sum(p.shape[1] for p in parts)
        assert kc_n * tot <= 8192
        i = self.wb_i % len(self.wb)
        self.wb_i += 1
        view = self.wb[i][:, 0:kc_n * tot].rearrange("p (kc n) -> p kc n", kc=kc_n)
        c0 = 0
        for p in parts:
            n = p.shape[1]
            self.R.dma("pool", view[:, :, c0:c0 + n], p.rearrange("(kc p) n -> p kc n", p=128), [], [self.wb_res[i]])
            c0 += n
        return view, self.wb_res[i]

    def ssd(self, i, u):
        R = self.R
        dr = self.dram
        pc = self.pcol
        hT, r_h = self.hT, self.r_h
        r_hl = [r_h[u * 2], r_h[u * 2 + 1]]
        nseq, L = (NPS, LP) if u == 0 else (1, LS)
        cps = L // 128
        w_in = dr["ssd_w_in"][0]
        identf, identb, onesf = self.identf, self.identb, self.onesf
        rc = self.r_const
        bc = lambda ap, shape: ap.to_broadcast(shape)

        A = self._Alloc(self, 16384)
        E_tok = A([128, 8, 64], F32)
        lnEb = A([128, 8, 64], F32)
        dO = A([128, 8, 64], F32)
        wS = A([128, 8, 64], F32)
        dec = A([128, 8, 64], F32)
        ss = A([128, 8, 8], F32)
        yg = A([128, 8, 2048], BF16)
        mark = A.off
        r_tokq, r_dec, r_ss = Res("tokq"), Res("dec"), Res("ss")
        r_yg = [Res("yg%d" % c) for c in range(8)]
        persist = [r_tokq, r_dec, r_ss] + r_yg

        dtT = A([64, 1024], F32)
        xx = A([64, 1024], F32)
        t1 = A([64, 1024], F32)
        t2 = A([64, 1024], F32)
        ET = A([64, 1024], F32)
        cm = A([64, 1024], F32)
        xd = A([64, 8, 64], F32)
        r_a = Res("ssdA")
        self.stage(r_hl + persist + [r_a])
        wdt, rwdt = self.wload(w_in[:, 6144:6208], 8, 64)
        for tt in range(2):
            ps, rps = self.bank()
            for kc in range(8):
                R.mm(ps[0:64, :], wdt[:, kc, :], hT[:, kc, tt * 512:(tt + 1) * 512], kc == 0, kc == 7,
                     [rwdt, r_hl[tt]], [rps])
            R.op("dve", "tensor_scalar", [rps, rc], [r_a], out=xx[:, tt * 512:(tt + 1) * 512], in0=ps[0:64, :],
                 scalar1=self.sm64[:, 0:1], scalar2=None, op0=ALU.add)
        RA = [r_a]
        R.op("act", "activation", RA, RA, out=t1, in_=xx, func=AF.Abs)
        R.op("act", "activation", RA, RA, out=t1, in_=t1, func=AF.Exp, scale=-1.0)
        R.op("act", "activation", RA, RA, out=t1, in_=t1, func=AF.Ln, bias=1.0)
        R.op("dve", "scalar_tensor_tensor", RA, RA, out=dtT, in0=xx, scalar=0.0, in1=t1, op0=ALU.max, op1=ALU.add)
        R.op("act", "activation", RA, RA, out=t2, in_=dtT, func=AF.Ln)
        R.op("dve", "tensor_scalar", RA + [rc], RA, out=xx, in0=dtT, scalar1=self.sm64[:, 2:3], scalar2=None,
             op0=ALU.mult)
        cm3 = cm.rearrange("p (c t) -> p c t", t=128)
        R.op("dve", "memset", [], RA, ap=cm, constant=1.0)
        R.op("dve", "memset", RA, RA, ap=cm3[:, :, 0:1], constant=0.0)
        R.op("dve", "tensor_tensor_scan", RA, RA, out=t1, data0=cm, data1=xx, initial=0.0, op0=ALU.mult, op1=ALU.add)
        cum3 = t1.rearrange("p (c t) -> p c t", t=128)
        tot = cum3[:, :, 127:128]
        ET3 = ET.rearrange("p (c t) -> p c t", t=128)
        xx3 = xx.rearrange("p (c t) -> p c t", t=128)
        R.op("dve", "tensor_copy", RA, RA, out=ET[0:32, :], in_=t1[0:32, :])
        R.op("dve", "tensor_tensor", RA, RA, out=ET3[32:64], in0=bc(tot[32:64], [32, 8, 128]), in1=cum3[32:64],
             op=ALU.subtract)
        R.op("dve", "tensor_tensor", RA, RA, out=ET[32:64, :], in0=ET[32:64, :], in1=xx[32:64, :], op=ALU.add)
        R.op("dve", "tensor_tensor", RA + [rc], RA, out=xd, in0=bc(identf[0:64, 0:64].unsqueeze(1), [64, 8, 64]),
             in1=bc(tot, [64, 8, 64]), op=ALU.mult)
        ps, rps = self.bank()
        R.mm(ps[:, 0:512], onesf[0:64, :], xd.rearrange("p c h -> p (c h)"), True, True, RA + [rc], [rps])
        R.op("act", "activation", [rps], [r_dec], out=dec.rearrange("p c h -> p (c h)"), in_=ps[:, 0:512], func=AF.Exp)
        cmv = cm.rearrange("p (c t) -> p c t", t=128)
        R.op("dve", "tensor_tensor", RA, RA, out=cmv, in0=bc(tot, [64, 8, 128]), in1=ET3, op=ALU.subtract)
        R.op("dve", "tensor_tensor", RA, RA, out=cm, in0=cm, in1=t2, op=ALU.add)
        R.op("dve", "tensor_tensor", RA, RA, out=xx, in0=t2, in1=ET, op=ALU.subtract)
        for c in range(8):
            cs = slice(c * 128, (c + 1) * 128)
            ps, rps = self.bank()
            R.tr(ps[:, 0:64], ET[:, cs], identf[0:64, 0:64], RA + [rc], [rps])
            R.tr(ps[:, 64:128], xx[:, cs], identf[0:64, 0:64], RA + [rc], [rps])
            R.tr(ps[:, 128:192], cm[:, cs], identf[0:64, 0:64], RA + [rc], [rps])
            R.op("dve", "tensor_scalar", [rps], [r_tokq], out=E_tok[:, c, :], in0=ps[:, 0:64], scalar1=SSD_BIG, scalar2=None,
                 op0=ALU.add)
            R.op("dve", "tensor_scalar", [rps], [r_tokq], out=lnEb[:, c, :], in0=ps[:, 64:128], scalar1=-SSD_BIG,
                 scalar2=None, op0=ALU.add)
            R.op("act", "activation", [rps], [r_tokq], out=dO[:, c, :], in_=ps[:, 0:64], func=AF.Exp)
            R.op("act", "activation", [rps], [r_tokq], out=wS[:, c, :], in_=ps[:, 128:192], func=AF.Exp)
        R.op("dve", "memset", [], [r_ss], ap=ss, constant=0.0)

        A.off = mark
        pre = A([128, nseq, L + 4], F32)
        acc = A([128, 1024], F32)
        xsT = A([128, 2, 1024], BF16)
        BT = A([128, 1024], BF16)
        CT = A([128, 1024], BF16)
        xb_tok = A([128, 8, 384], BF16)
        Tbf = A([128, 8, 256], BF16)
        CBm = [A([128, 128], F32) for _ in range(2)]
        X4 = [A([128, 4, 128], F32), xsT[:, 0, :].bitcast(F32).rearrange("p (h t) -> p h t", h=4)]
        L4 = [A([128, 4, 128], F32), xsT[:, 1, :].bitcast(F32).rearrange("p (h t) -> p h t", h=4)]
        M4 = [[A([128, 4, 128], BF16) for _ in range(2)] for _ in range(2)]
        xw = [A([128, 4, 64], BF16) for _ in range(2)]
        Sst = [A([128, 4, 64], F32) for _ in range(2)]
        Sbf = A([128, 256], BF16)
        zs = [A([128, 256], BF16) for _ in range(2)]
        xD = [A([128, 4, 64], BF16) for _ in range(2)]
        tc1 = A([128, 4, 64], F32)
        tc2 = A([128, 4, 64], F32)
        ost = A([128, 2, 128], F32)
        sqj = A([128, 256], BF16)
        r_pre, r_acc = Res("pre"), Res("acc")
        r_fm = [Res("xsT0"), Res("xsT1"), Res("BT"), Res("CT")]
        r_xb = [Res("xb%d" % c) for c in range(8)]
        r_Tbf = [Res("Tbf%d" % c) for c in range(8)]
        r_CB = [Res("CB0"), Res("CB1")]
        r_X4 = [Res("X4"), r_fm[0]]
        r_L4 = [Res("L4"), r_fm[1]]
        r_zs2 = [Res("zs0"), Res("zs1")]
        r_xD2 = [Res("xD0"), Res("xD1")]
        r_M4 = [[Res("M4a0"), Res("M4b0")], [Res("M4a1"), Res("M4b1")]]
        r_xw = [Res("xwa"), Res("xwb")]
        r_S = [Res("Sf"), Res("Sb")]
        r_Sbf, r_zs, r_xD, r_tc1, r_tc2, r_ost, r_sqj = (Res("Sbf"), Res("zs"), Res("xD"), Res("tc1"), Res("tc2"),
                                                         Res("ost"), Res("sqj"))
        grp_res = ([r_pre, r_acc] + r_fm + r_xb + r_Tbf + r_CB + [r_X4[0], r_L4[0]] + r_M4[0] + r_M4[1] + r_xw + r_S + r_zs2 + r_xD2 +
                   [r_Sbf, r_zs, r_xD, r_tc1, r_tc2, r_ost, r_sqj])
        self.stage(r_hl + persist + grp_res)
        R.op("dve", "memset", [], [r_pre], ap=pre, constant=0.0)
        fm_dst = [xsT[:, 0, :], xsT[:, 1, :], BT, CT]

        def gweights(g):
            return self.wload_multi([w_in[:, g * 256:(g + 1) * 256], w_in[:, 2048 + g * 256:2048 + (g + 1) * 256],
                                     w_in[:, 4096 + g * 128:4096 + (g + 1) * 128],
                                     w_in[:, 5120 + g * 128:5120 + (g + 1) * 128]], 8)

        def make_xw(c, d, g, k):
            h0 = d * 32 + g * 4
            R.op("dve", "tensor_tensor", [r_xb[c], r_tokq], [r_xw[k]], out=xw[k],
                 in0=xb_tok[:, c, 0:256].rearrange("p (h q) -> p h q", h=4),
                 in1=bc(wS[:, c, h0:h0 + 4].unsqueeze(2), [128, 4, 64]), op=ALU.mult)

        def state_update(c, d, g, kx=None):
            h0 = d * 32 + g * 4
            k = d if kx is None else kx
            if kx is None:
                make_xw(c, d, g, k)
            ps, rps = self.bank()
            R.mm(ps[:, 0:256], xb_tok[:, c, 256:384], xw[k].rearrange("p h q -> p (h q)"), True, True,
                 [r_xb[c], r_xw[k]], [rps])
            R.op("dve", "tensor_tensor", [r_S[d], r_dec], [r_S[d]], out=Sst[d], in0=Sst[d],
                 in1=bc(dec[:, c, h0:h0 + 4].unsqueeze(2), [128, 4, 64]), op=ALU.mult)
            R.op("dve", "tensor_tensor", [r_S[d], rps], [r_S[d]], out=Sst[d],
                 in0=ps[:, 0:256].rearrange("p (h q) -> p h q", h=4), in1=Sst[d], op=ALU.add)

        def state_init(d, g):
            if u == 0:
                R.op("dve", "memset", [], [r_S[d]], ap=Sst[d], constant=0.0)
            else:
                src = dr["st_ssd"][d, g * 4:(g + 1) * 4].rearrange("(blk h2) p n -> (h2 p) blk n", blk=2)
                R.dma("sp", ost, src, [], [r_ost])
                ps, rps = self.bank()
                for blk in range(2):
                    R.tr(ps[:, blk * 128:(blk + 1) * 128], ost[:, blk, :], identf[:], [r_ost, rc], [rps])
                R.op("dve", "tensor_copy", [rps], [r_S[d]], out=Sst[d].rearrange("p h q -> p (h q)"), in_=ps[:, 0:256])

        def state_out(seq, d, g):
            ps, rps = self.bank()
            S2 = Sst[d].rearrange("p h q -> p (h q)")
            for blk in range(2):
                R.tr(ps[:, blk * 128:(blk + 1) * 128], S2[:, blk * 128:(blk + 1) * 128], identf[:], [r_S[d], rc], [rps])
            R.op("dve", "tensor_copy", [rps], [r_ost], out=ost.rearrange("p b n -> p (b n)"), in_=ps[:, 0:256])
            dst = dr["o_ssd"][seq, d, g * 4:(g + 1) * 4].rearrange("(blk h2) p n -> (h2 p) blk n", blk=2)
            R.dma("sp", dst, ost, [r_ost], [])

        wnext = gweights(0)
        for g in range(8):
            W, rW = wnext
            if g < 7:
                wnext = gweights(g + 1)
            for ci, (col0, cch) in enumerate([(256, g * 2), (384, g * 2 + 1), (512, 16 + g), (640, 24 + g)]):
                for tt in range(2):
                    ps, rps = self.bank()
                    for kc in range(8):
                        R.mm(ps[:], W[:, kc, col0:col0 + 128], hT[:, kc, tt * 512:(tt + 1) * 512], kc == 0, kc == 7,
                             [rW, r_hl[tt]], [rps])
                    if u == 0:
                        self.copy("act", pre[:, 2 * tt:2 * tt + 2, 2:2 + L],
                                  ps[:].rearrange("p (s t) -> p s t", s=2), [rps], [r_pre])
                    else:
                        self.copy("act", pre[:, 0, 2 + tt * 512:2 + (tt + 1) * 512], ps[:], [rps], [r_pre])
                acc3 = acc.rearrange("p (s t) -> p s t", s=nseq)
                cw = pc["ssd_conv_w"]
                R.op("dve", "tensor_scalar", [r_pre, self.r_prm], [r_acc], out=acc3, in0=pre[:, :, 0:L],
                     scalar1=self.prm[:, cw + cch:cw + cch + 1],
                     scalar2=self.prm[:, pc["ssd_conv_b"] + cch:pc["ssd_conv_b"] + cch + 1], op0=ALU.mult, op1=ALU.add)
                for k in range(1, 5):
                    R.op("dve", "scalar_tensor_tensor", [r_pre, self.r_prm, r_acc], [r_acc], out=acc3,
                         in0=pre[:, :, k:k + L], scalar=self.prm[:, cw + k * 32 + cch:cw + k * 32 + cch + 1],
                         in1=acc3, op0=ALU.mult, op1=ALU.add)
                R.op("act", "activation", [r_acc], [r_fm[ci]], out=fm_dst[ci], in_=acc, func=AF.Silu)
            for c in range(8):
                cs = slice(c * 128, (c + 1) * 128)
                ps, rps = self.bank()
                psb = ps[:].bitcast(BF16)
                R.tr(psb[:, 0:128], xsT[:, 0, cs], identb[:], [r_fm[0], rc], [rps])
                R.tr(psb[:, 128:256], xsT[:, 1, cs], identb[:], [r_fm[1], rc], [rps])
                R.tr(psb[:, 256:384], BT[:, cs], identb[:], [r_fm[2], rc], [rps])
                self.copy("act", xb_tok[:, c, :], psb[:, 0:384], [rps], [r_xb[c]])
            for seq in range(nseq):
                state_init(1, g)
                for c in reversed(range(seq * cps, (seq + 1) * cps)):
                    R.op("act", "activation", [r_S[1]], [r_Tbf[c]], out=Tbf[:, c, :],
                         in_=Sst[1].rearrange("p h q -> p (h q)"), func=AF.Identity)
                    if u == 0 or c > seq * cps:
                        state_update(c, 1, g)
                if u == 0:
                    state_out(seq, 1, g)
            def front(c):
                cs = slice(c * 128, (c + 1) * 128)
                pp = c % 2
                psz, rpsz = self.bank()
                for kc in range(8):
                    R.mm(psz[:, 0:256], hT[:, kc, cs], W[:, kc, 0:256], kc == 0, kc == 7, [rW, r_hl[c // 4]], [rpsz])
                R.op("act", "activation", [rpsz], [r_zs2[pp]], out=zs[pp], in_=psz[:, 0:256], func=AF.Tanh, scale=0.5)
                R.op("dve", "scalar_tensor_tensor", [rpsz, r_zs2[pp]], [r_zs2[pp]], out=zs[pp], in0=zs[pp], scalar=1.0,
                     in1=psz[:, 0:256], op0=ALU.add, op1=ALU.mult)
                pcb, rpcb = self.bank()
                R.mm(pcb[:, 0:128], BT[:, cs], CT[:, cs], True, True, [r_fm[2], r_fm[3]], [rpcb])
                R.op("act", "activation", [rpcb], [r_CB[pp]], out=CBm[pp], in_=pcb[:, 0:128], func=AF.Identity)
                R.op("dve", "tensor_tensor", [r_xb[c], rc], [r_xD2[pp]], out=xD[pp],
                     in0=xb_tok[:, c, 0:256].rearrange("p (h q) -> p h q", h=4),
                     in1=bc(self.dbc[:, g * 4:g * 4 + 4].unsqueeze(2), [128, 4, 64]), op=ALU.mult)
                for d in range(2):
                    h0 = d * 32 + g * 4
                    R.op("pool", "tensor_tensor", [rc, r_tokq], [r_X4[d]], out=X4[d],
                         in0=bc(identf[:].unsqueeze(1), [128, 4, 128]),
                         in1=bc(E_tok[:, c, h0:h0 + 4].unsqueeze(2), [128, 4, 128]), op=ALU.mult)
                prs = []
                for d in range(2):
                    pr, rpr = self.bank()
                    msk = self.mge if d == 0 else self.mle
                    R.mm(pr[:, 0:512], msk[:], X4[d].rearrange("p h t -> p (h t)"), True, True, [r_X4[d], rc], [rpr])
                    prs.append((pr, rpr))
                for d in range(2):
                    h0 = d * 32 + g * 4
                    pr, rpr = prs[d]
                    for hh in range(4):
                        R.op("act", "activation", [rpr, r_tokq], [r_L4[d]], out=L4[d][:, hh, :],
                             in_=pr[:, hh * 128:(hh + 1) * 128], func=AF.Exp, bias=lnEb[:, c, h0 + hh:h0 + hh + 1])

            def needs_upd(c):
                return u == 0 or c < (c // cps + 1) * cps - 1

            def front2(c):
                pp = c % 2
                for d in range(2):
                    R.op("dve", "tensor_tensor", [r_L4[d], r_CB[pp]], [r_M4[pp][d]], out=M4[pp][d], in0=L4[d],
                         in1=bc(CBm[pp].unsqueeze(1), [128, 4, 128]), op=ALU.mult)
                if needs_upd(c):
                    make_xw(c, 0, g, pp)

            def back(seq, c):
                cs = slice(c * 128, (c + 1) * 128)
                pp = c % 2
                if c == seq * cps:
                    state_init(0, g)
                R.op("act", "activation", [r_S[0]], [r_Sbf], out=Sbf, in_=Sst[0].rearrange("p h q -> p (h q)"),
                     func=AF.Identity)
                py, rpy = self.bank(hold=True)
                R.mm(py[:, 0:256], identb[:], xD[pp].rearrange("p h q -> p (h q)"), True, False, [r_xD2[pp], rc], [rpy])
                for d in range(2):
                    for hh in range(4):
                        R.mm(py[:, hh * 64:(hh + 1) * 64], M4[pp][d][:, hh, :], xb_tok[:, c, hh * 64:(hh + 1) * 64],
                             False, d == 1 and hh == 3, [r_M4[pp][d], r_xb[c]], [rpy])
                pz, rpz = self.bank()
                R.mm(pz[:, 0:256], CT[:, cs], Sbf, True, True, [r_fm[3], r_Sbf], [rpz])
                R.mm(pz[:, 256:512], CT[:, cs], Tbf[:, c, :], True, True, [r_fm[3], r_Tbf[c]], [rpz])
                h0 = g * 4
                R.op("dve", "tensor_tensor", [rpz, r_tokq], [r_tc1], out=tc1,
                     in0=pz[:, 0:256].rearrange("p (h q) -> p h q", h=4),
                     in1=bc(dO[:, c, h0:h0 + 4].unsqueeze(2), [128, 4, 64]), op=ALU.mult)
                R.op("dve", "tensor_tensor", [rpz, r_tokq], [r_tc2], out=tc2,
                     in0=pz[:, 256:512].rearrange("p (h q) -> p h q", h=4),
                     in1=bc(dO[:, c, 32 + h0:32 + h0 + 4].unsqueeze(2), [128, 4, 64]), op=ALU.mult)
                R.op("dve", "tensor_tensor", [r_tc1, r_tc2], [r_tc1], out=tc1, in0=tc1, in1=tc2, op=ALU.add)
                R.op("dve", "tensor_tensor", [rpy, r_tc1], [r_tc1], out=tc1,
                     in0=py[:, 0:256].rearrange("p (h q) -> p h q", h=4), in1=tc1, op=ALU.add)
                self.release(py)
                ygs = yg[:, c, g * 256:(g + 1) * 256]
                R.op("dve", "tensor_tensor", [r_tc1, r_zs2[pp]], [r_yg[c]], out=ygs, in0=tc1.rearrange("p h q -> p (h q)"),
                     in1=zs[pp], op=ALU.mult)
                R.op("act", "activation", [r_yg[c]], [r_sqj, r_ss], out=sqj, in_=ygs, func=AF.Square,
                     accum_out=ss[:, c, g:g + 1])
                if needs_upd(c):
                    state_update(c, 0, g, kx=pp)
                if u == 0 and c == (seq + 1) * cps - 1:
                    state_out(seq, 0, g)

            front(0)
            front2(0)
            for c in range(8):
                if c + 1 < 8:
                    front(c + 1)
                back(c // cps, c)
                if c + 1 < 8:
                    front2(c + 1)

        A.off = mark
        sst = A([128, 8], F32)
        ynT = A([128, 16, 512], BF16)
        r_sst, r_yn = Res("sst"), Res("ynT")
        self.stage(persist + [r_sst, r_yn])
        R.op("dve", "tensor_reduce", [r_ss], [r_sst], out=sst, in_=ss, axis=AX.X, op=ALU.add)
        R.op("act", "activation", [r_sst], [r_sst], out=sst, in_=sst, func=AF.Ln, scale=1.0 / 2048, bias=4.0 * EPS)
        R.op("act", "activation", [r_sst], [r_sst], out=sst, in_=sst, func=AF.Exp, scale=-0.5)
        for c in range(8):
            R.op("dve", "tensor_scalar", [r_yg[c], r_sst], [r_yg[c]], out=yg[:, c, :], in0=yg[:, c, :],
                 scalar1=sst[:, c:c + 1], scalar2=None, op0=ALU.mult)
        w_out = dr["ssd_w_out"][0]
        wA, rwA = self.wload(w_out[0:1024, :], 8, 1024)
        wB, rwB = self.wload(w_out[1024:2048, :], 8, 1024)
        gn = pc["ssd_norm_g"]
        for half in range(2):
            for cc in range(16):
                ps, rps = self.bank()
                psb = ps[:].bitcast(BF16)
                for q in range(4):
                    c = half * 4 + q
                    R.tr(psb[:, q * 128:(q + 1) * 128], yg[:, c, cc * 128:(cc + 1) * 128], identb[:], [r_yg[c], rc], [rps])
                R.op("act", "activation", [rps, self.r_prm], [r_yn], out=ynT[:, cc, :], in_=psb[:, 0:512], func=AF.Identity,
                     scale=self.prm[:, gn + cc:gn + cc + 1])
            tt = u * 2 + half
            ts = slice(tt * 512, (tt + 1) * 512)
            for dc in range(8):
                ps, rps = self.bank()
                for cc in range(16):
                    wv, rw = (wA, rwA) if cc < 8 else (wB, rwB)
                    R.mm(ps[:], wv[:, cc % 8, dc * 128:(dc + 1) * 128], ynT[:, cc, :], cc == 0, cc == 15, [rw, r_yn], [rps])
                R.op("dve", "scalar_tensor_tensor", [rps, self.r_mod, self.r_xT[dc][tt]], [self.r_xT[dc][tt]],
                     out=self.xT[:, dc, ts], in0=ps[:], scalar=self.mod[:, 16 + dc, u:u + 1],
                     in1=self.xT[:, dc, ts], op0=ALU.mult, op1=ALU.add)

    def attn_core(self, parts, part_res, vl, v_res, blocks, scale, negm, negm_res, pT, r_pT, fin, v_res_fn=None):
        R = self.R
        for (q0, nq, kts) in blocks:
            po, r_po = self.bank(hold=True)
            psm, r_psm = self.bank(hold=True)
            n = len(kts)

            def score(kt):
                pss, r_pss = self.bank()
                for pi, (kT, qT) in enumerate(parts):
                    R.mm(pss[:, 0:nq], kT[:, kt * 128:(kt + 1) * 128], qT[:, q0:q0 + nq], pi == 0, pi == len(parts) - 1,
                         part_res, [r_pss])
                return pss, r_pss

            nxt = score(kts[0])
            for idx, kt in enumerate(kts):
                pss, r_pss = nxt
                if idx + 1 < n:
                    nxt = score(kts[idx + 1])
                slot = self.pT_i % pT.shape[1]
                self.pT_i += 1
                R.op("act", "activation", [r_pss, negm_res], [r_pT[slot]], out=pT[:, slot, 0:nq], in_=pss[:, 0:nq],
                     func=AF.Exp, scale=scale, bias=negm)
                vr = v_res_fn(kt) if v_res_fn is not None else v_res
                R.mm(po[:, 0:nq], vl(kt), pT[:, slot, 0:nq], idx == 0, idx == n - 1, [vr, r_pT[slot]], [r_po])
                R.mm(psm[:, 0:nq], self.onesb[:], pT[:, slot, 0:nq], idx == 0, idx == n - 1, [self.r_const, r_pT[slot]],
                     [r_psm])
            self.attn_flush()
            self.pending_fin = (fin, q0, nq, po, r_po, psm, r_psm)

    def attn_flush(self):
        if getattr(self, "pending_fin", None) is not None:
            fin, q0, nq, po, r_po, psm, r_psm = self.pending_fin
            self.pending_fin = None
            fin(q0, nq, po, r_po, psm, r_psm)
            self.release(po)
            self.release(psm)

    def sq_bound(self, parts, part_res, ncols, out_col, tmp_sq, r_sq, red, r_red, p0=0):
        R = self.R
        ntile = (ncols + 511) // 512
        first = True
        for t in range(ntile):
            c0 = t * 512
            w = min(512, ncols - c0)
            ps, rps = self.bank()
            for pi, p in enumerate(parts):
                K = p.shape[0]
                R.op("act", "activation", part_res, [r_sq], out=tmp_sq[p0:p0 + K, 0:w], in_=p[:, c0:c0 + w], func=AF.Square)
                R.mm(ps[:, 0:w], self.onesb[p0:p0 + K, :], tmp_sq[p0:p0 + K, 0:w], pi == 0, pi == len(parts) - 1,
                     [r_sq, self.r_const], [rps])
            if first:
                R.op("dve", "tensor_reduce", [rps], [r_red], out=red[:, out_col:out_col + 1], in_=ps[:, 0:w], axis=AX.X,
                     op=ALU.max)
                first = False
            else:
                R.op("dve", "tensor_reduce", [rps], [r_red], out=red[:, 7:8], in_=ps[:, 0:w], axis=AX.X, op=ALU.max)
                R.op("dve", "tensor_tensor", [r_red], [r_red], out=red[:, out_col:out_col + 1],
                     in0=red[:, out_col:out_col + 1], in1=red[:, 7:8], op=ALU.max)

    def fnorm(self, src32, nch, w, g_col, tmp_sq, r_sq, rstd, r_rstd, lnt):
        R = self.R
        R.op("act", "activation", [self.r_f32], [r_sq], out=tmp_sq[:, 0:nch, 0:w], in_=src32[:, 0:nch, 0:w], func=AF.Square)
        ps, rps = self.bank()
        for m in range(nch):
            R.mm(ps[:, 0:w], self.onesb[:], tmp_sq[:, m, 0:w], m == 0, m == nch - 1, [r_sq, self.r_const], [rps])
        R.op("act", "activation", [rps], [r_rstd], out=lnt[:, 0:w], in_=ps[:, 0:w], func=AF.Ln, scale=1.0 / (nch * 128),
             bias=EPS)
        R.op("act", "activation", [r_rstd], [r_rstd], out=rstd[:, 0:w], in_=lnt[:, 0:w], func=AF.Exp, scale=-0.5)

    def mla(self, i, u):
        R = self.R
        dr = self.dram
        pc = self.pcol
        hT, r_h = self.hT, self.r_h
        r_hl = [r_h[u * 2], r_h[u * 2 + 1]]
        nseq, L = (NPS, LP) if u == 0 else (1, LS)
        koff = 0 if u == 0 else 256
        Tk = TU + koff
        identf, identb = self.identf, self.identb
        rc = self.r_const
        scale = 96.0 ** -0.5

        A = self._Alloc(self, 16384)
        cqn = A([128, 4, TU], BF16)
        ckvk = A([128, 2, 1280], BF16)
        krk = A([128, 1280], BF16)
        oT = A([128, 8, TU], BF16)
        WinS = A([128, 8, 32], BF16)
        mark = A.off
        r_cqn, r_ckvk, r_krk, r_WinS = Res("cqn"), Res("ckvk"), Res("krk"), Res("WinS")
        r_oT = [[Res("oT%d_%d" % (c, t)) for t in range(2)] for c in range(8)]
        persist = [r_cqn, r_ckvk, r_krk] + [r for rr in r_oT for r in rr]

        f32 = A([128, 4, 512], F32)
        sq = A([128, 4, 512], BF16)
        lnt = A([128, 512], F32)
        rstd = A([128, 512], F32)
        ck32 = A([128, 2, 512], F32)
        krf = A([32, 512], F32)
        kt1 = A([32, 512], F32)
        kt2 = A([32, 512], F32)
        rope = A([32, 2, TU], F32)
        ost = A([128, 256], F32)
        ost2 = A([128, 32], F32)
        cst = A([128, 2, 256], F32)
        cst2 = A([128, 2, 32], F32)
        self.r_f32 = Res("f32")
        r_sq, r_rstd, r_ck32, r_krf, r_kt, r_rope = Res("sq"), Res("rstd"), Res("ck32"), Res("krf"), Res("kt"), Res("rope")
        r_ost, r_ost2, r_cst = Res("ost"), Res("ost2"), Res("cst")
        self.stage(r_hl + persist + [r_WinS, self.r_f32, r_sq, r_rstd, r_ck32, r_krf, r_kt, r_rope, r_ost, r_ost2, r_cst])
        R.op("dve", "memset", [], [r_krk], ap=krk, constant=0.0)
        Win, rWin = self.wload(dr["mla_w_in"][0], 8, 800)
        R.op("dve", "tensor_copy", [rWin], [r_WinS], out=WinS[:, :, 0:16], in_=Win[:, :, 784:800])
        R.op("dve", "tensor_copy", [rWin], [r_WinS], out=WinS[:, :, 16:32], in_=Win[:, :, 768:784])
        if u == 1:
            R.dma("sp", rope, dr["rope_cs"].rearrange("a r t -> r a t"), [], [r_rope])
            R.dma("sp", cst, dr["c_ckv"].rearrange("(a p) f -> p a f", p=128), [], [r_cst])
            R.dma("sp", cst2, dr["c_kr"].rearrange("(a p) f -> p a f", p=128), [], [r_cst])
            for a in range(2):
                ps, rps = self.bank()
                for m in range(2):
                    R.tr(ps[:, m * 128:(m + 1) * 128], cst[:, a, m * 128:(m + 1) * 128], identf[:], [r_cst, rc], [rps])
                R.op("dve", "tensor_copy", [rps], [r_ckvk], out=ckvk[:, :, a * 128:(a + 1) * 128],
                     in_=ps[:, 0:256].rearrange("p (m t) -> p m t", m=2))
                ps, rps = self.bank()
                R.tr(ps[0:32, 0:128], cst2[:, a, :], identf[:], [r_cst, rc], [rps])
                R.op("dve", "tensor_copy", [rps], [r_krk], out=krk[0:32, a * 128:(a + 1) * 128], in_=ps[0:32, 0:128])
        gq, gkv = pc["mla_q_norm_g"], pc["mla_kv_norm_g"]
        for tt in range(2):
            ts = slice(tt * 512, (tt + 1) * 512)
            for m in range(4):
                ps, rps = self.bank()
                for kc in range(8):
                    R.mm(ps[:], Win[:, kc, m * 128:(m + 1) * 128], hT[:, kc, ts], kc == 0, kc == 7, [rWin, r_hl[tt]], [rps])
                self.copy(self.evac_engine(), f32[:, m, :], ps[:], [rps], [self.r_f32])
            self.fnorm(f32, 4, 512, gq, sq, r_sq, rstd, r_rstd, lnt)
            for m in range(4):
                R.op("dve", "scalar_tensor_tensor", [self.r_f32, r_rstd, self.r_prm], [r_cqn], out=cqn[:, m, ts],
                     in0=f32[:, m, :], scalar=self.prm[:, gq + m:gq + m + 1], in1=rstd, op0=ALU.mult, op1=ALU.mult)
            for m in range(2):
                ps, rps = self.bank()
                for kc in range(8):
                    R.mm(ps[:], Win[:, kc, 512 + m * 128:512 + (m + 1) * 128], hT[:, kc, ts], kc == 0, kc == 7,
                         [rWin, r_hl[tt]], [rps])
                self.copy(self.evac_engine(), f32[:, m, :], ps[:], [rps], [self.r_f32])
            self.fnorm(f32, 2, 512, gkv, sq, r_sq, rstd, r_rstd, lnt)
            for m in range(2):
                R.op("dve", "scalar_tensor_tensor", [self.r_f32, r_rstd, self.r_prm], [r_ck32], out=ck32[:, m, :],
                     in0=f32[:, m, :], scalar=self.prm[:, gkv + m:gkv + m + 1], in1=rstd, op0=ALU.mult, op1=ALU.mult)
            R.op("act", "activation", [r_ck32], [r_ckvk], out=ckvk[:, :, koff + tt * 512:koff + (tt + 1) * 512], in_=ck32,
                 func=AF.Identity)
            ps, rps = self.bank()
            for kc in range(8):
                R.mm(ps[0:32, :], Win[:, kc, 768:800], hT[:, kc, ts], kc == 0, kc == 7, [rWin, r_hl[tt]], [rps])
            R.op("dve", "tensor_copy", [rps], [r_krf], out=krf, in_=ps[0:32, :])
            kdst = krk[0:32, koff + tt * 512:koff + (tt + 1) * 512]
            if u == 0:
                R.op("act", "activation", [r_krf], [r_krk], out=kdst, in_=krf, func=AF.Identity)
            else:
                ps2, rps2 = self.bank()
                for kc in range(8):
                    R.mm(ps2[0:32, :], WinS[:, kc, :], hT[:, kc, ts], kc == 0, kc == 7, [r_WinS, r_hl[tt]], [rps2])
                R.op("dve", "tensor_tensor", [r_krf, r_rope], [r_kt], out=kt1, in0=krf, in1=rope[:, 0, ts], op=ALU.mult)
                R.op("dve", "tensor_tensor", [rps2, r_rope], [r_kt], out=kt2, in0=ps2[0:32, :], in1=rope[:, 1, ts],
                     op=ALU.mult)
                R.op("dve", "tensor_tensor", [r_kt], [r_krk], out=kdst, in0=kt1, in1=kt2, op=ALU.add)
            if u == 0:
                for q in range(4):
                    tok0 = tt * 512 + q * 128
                    seq, t0 = tok0 // LP, tok0 % LP
                    ps, rps = self.bank()
                    for m in range(2):
                        R.tr(ps[:, m * 128:(m + 1) * 128], ck32[:, m, q * 128:(q + 1) * 128], identf[:], [r_ck32, rc], [rps])
                    R.op("dve", "tensor_copy", [rps], [r_ost], out=ost, in_=ps[:, 0:256])
                    R.dma("sp", dr["o_ckv"][seq, t0:t0 + 128, :], ost, [r_ost], [])
                    ps, rps = self.bank()
                    R.tr(ps[:, 0:32], krf[:, q * 128:(q + 1) * 128], identf[0:32, 0:32], [r_krf, rc], [rps])
                    R.op("dve", "tensor_copy", [rps], [r_ost2], out=ost2, in_=ps[:, 0:32])
                    R.dma("sp", dr["o_kr"][seq, t0:t0 + 128, :], ost2, [r_ost2], [])

        A.off = mark
        WuqS = A([128, 4, 16, 32], BF16)
        qn_ = [A([128, TU], BF16) for _ in range(2)]
        qr_ = [A([128, TU], BF16) for _ in range(2)]
        qa_ = [A([32, TU], F32) for _ in range(2)]
        qb_ = [A([32, TU], F32) for _ in range(2)]
        kn_l = [A([128, 1280], BF16) for _ in range(2)]
        sqt_ = [A([64, 512], BF16) for _ in range(2)]
        red_ = [A([128, 8], F32) for _ in range(2)]
        negm = A([128, 2], F32)
        vpair_ = [A([128, 10, 128], BF16) for _ in range(2)]
        NS = 4
        pT = A([128, NS, 512], BF16)
        rec = A([128, 512], F32)
        rope2 = A([32, 2, TU], F32)
        r_WuqS, r_rec, r_rope2 = Res("WuqS"), Res("rec"), Res("rope2")
        r_qn_ = [Res("qn0"), Res("qn1")]
        r_qr_ = [Res("qr0"), Res("qr1")]
        r_qab_ = [Res("qab0"), Res("qab1")]
        r_kn_ = [Res("kn0"), Res("kn1")]
        r_sqt_ = [Res("sqt0"), Res("sqt1")]
        r_red_ = [Res("red0"), Res("red1")]
        r_negm_ = [Res("negm0"), Res("negm1")]
        r_vp_ = [Res("vp0"), Res("vp1")]
        r_pT = [Res("pT%d" % k) for k in range(NS)]
        self.stage(persist + [r_WuqS, r_rec, r_rope2] + r_qn_ + r_qr_ + r_qab_ + r_kn_ + r_sqt_ + r_red_ + r_negm_ + r_vp_ + r_pT)
        if u == 1:
            R.dma("sp", rope2, dr["rope_cs"].rearrange("a r t -> r a t"), [], [r_rope2])
        for k in range(2):
            R.op("dve", "memset", [], [r_qn_[k]], ap=qn_[k], constant=0.0)
            R.op("dve", "memset", [], [r_qr_[k]], ap=qr_[k], constant=0.0)
            R.op("dve", "memset", [], [r_kn_[k]], ap=kn_l[k], constant=0.0)
        Wuq, rWuq = self.wload(dr["mla_w_uq"][0], 4, 1536)
        Wukv, rWukv = self.wload(dr["mla_w_ukv"][0], 2, 2048)
        Wuq4 = Wuq.rearrange("p k (h d) -> p k h d", d=96)
        for kc in range(4):
            R.op("dve", "tensor_copy", [rWuq], [r_WuqS], out=WuqS[:, kc, :, 0:16], in_=Wuq4[:, kc, :, 80:96])
            R.op("dve", "tensor_copy", [rWuq], [r_WuqS], out=WuqS[:, kc, :, 16:32], in_=Wuq4[:, kc, :, 64:80])
        ktiles = [(0, 512), (512, 512), (1024, Tk - 1024)] if Tk > 1024 else [(0, 512), (512, 512)]
        if u == 1:
            blocks = [(0, 512, list(range(10))), (512, 512, list(range(10)))]
        else:
            blocks = [(s * 256, 256, [2 * s, 2 * s + 1]) for s in range(4)]
        Wv = Wukv.rearrange("p k (h two d) -> p k h two d", two=2, d=64)
        nkt = Tk // 128

        def prep_pair(a):
            vpair, r_vp = vpair_[a % 2], r_vp_[a % 2]
            for k0 in range(0, nkt, 4):
                ps, rps = self.bank()
                nq4 = min(4, nkt - k0)
                for q in range(nq4):
                    kt = k0 + q
                    for kc in range(2):
                        R.mm(ps[:, q * 128:(q + 1) * 128].rearrange("p (h d) -> p h d", h=2),
                             ckvk[:, kc, kt * 128:(kt + 1) * 128], Wv[:, kc, 2 * a:2 * a + 2, 1, :], kc == 0, kc == 1,
                             [r_ckvk, rWukv], [rps])
                self.copy(self.evac_engine(), vpair[:, k0:k0 + nq4, :],
                          ps[:, 0:nq4 * 128].rearrange("p (q d) -> p q d", d=128), [rps], [r_vp])

        def prep(h):
            hp = h % 2
            qn, qr, qa, qb, kn, sqt, red = qn_[hp], qr_[hp], qa_[hp], qb_[hp], kn_l[hp], sqt_[hp], red_[hp]
            r_qn, r_qr, r_qab, r_kn, r_sqt, r_red, r_negm = (r_qn_[hp], r_qr_[hp], r_qab_[hp], r_kn_[hp], r_sqt_[hp],
                                                             r_red_[hp], r_negm_[hp])
            for (c0, w) in ktiles:
                ps, rps = self.bank()
                for kc in range(2):
                    R.mm(ps[0:64, 0:w], Wukv[:, kc, h * 128:h * 128 + 64], ckvk[:, kc, c0:c0 + w], kc == 0, kc == 1,
                         [rWukv, r_ckvk], [rps])
                self.copy(self.evac_engine(), kn[0:64, c0:c0 + w], ps[0:64, 0:w], [rps], [r_kn])
            for tt in range(2):
                ts = slice(tt * 512, (tt + 1) * 512)
                ps, rps = self.bank()
                for kc in range(4):
                    R.mm(ps[0:64, :], Wuq[:, kc, h * 96:h * 96 + 64], cqn[:, kc, ts], kc == 0, kc == 3, [rWuq, r_cqn], [rps])
                self.copy(self.evac_engine(), qn[0:64, ts], ps[0:64, :], [rps], [r_qn])
                ps, rps = self.bank()
                for kc in range(4):
                    R.mm(ps[0:32, :], Wuq[:, kc, h * 96 + 64:h * 96 + 96], cqn[:, kc, ts], kc == 0, kc == 3, [rWuq, r_cqn],
                         [rps])
                if u == 0:
                    self.copy(self.evac_engine(), qr[0:32, ts], ps[0:32, :], [rps], [r_qr])
                else:
                    ps2, rps2 = self.bank()
                    for kc in range(4):
                        R.mm(ps2[0:32, :], WuqS[:, kc, h, :], cqn[:, kc, ts], kc == 0, kc == 3, [r_WuqS, r_cqn], [rps2])
                    R.op("dve", "tensor_tensor", [rps, r_rope2], [r_qab], out=qa[:, ts], in0=ps[0:32, :],
                         in1=rope2[:, 0, ts], op=ALU.mult)
                    R.op("dve", "tensor_tensor", [rps2, r_rope2], [r_qab], out=qb[:, ts], in0=ps2[0:32, :],
                         in1=rope2[:, 1, ts], op=ALU.mult)
                    R.op("dve", "tensor_tensor", [r_qab], [r_qr], out=qr[0:32, ts], in0=qa[:, ts], in1=qb[:, ts], op=ALU.add)
            self.sq_bound([qn[0:64, :], qr[0:32, :]], [r_qn, r_qr], TU, 0, sqt, r_sqt, red, r_red)
            self.sq_bound([kn[0:64, 0:Tk], krk[0:32, 0:Tk]], [r_kn, r_krk], Tk, 1, sqt, r_sqt, red, r_red)
            R.op("dve", "tensor_tensor", [r_red], [r_red], out=red[:, 2:3], in0=red[:, 0:1], in1=red[:, 1:2], op=ALU.add)
            R.op("dve", "tensor_scalar", [r_red], [r_negm], out=negm[:, hp:hp + 1], in0=red[:, 2:3],
                 scalar1=-0.5 * scale, scalar2=None, op0=ALU.mult)

        def attn(h):
            a, hp = h // 2, h % 2
            rows = slice(hp * 64, (hp + 1) * 64)
            vpair, r_vp = vpair_[a % 2], r_vp_[a % 2]
            qn, qr, kn = qn_[hp], qr_[hp], kn_l[hp]

            def fin(q0, nq, po, r_po, psm, r_psm):
                R.op("dve", "reciprocal", [r_psm], [r_rec], out=rec[rows, 0:nq], in_=psm[rows, 0:nq])
                R.op("dve", "tensor_tensor", [r_po, r_rec], [r_oT[a][q0 // 512]], out=oT[rows, a, q0:q0 + nq],
                     in0=po[rows, 0:nq], in1=rec[rows, 0:nq], op=ALU.mult)

            self.attn_core([(kn, qn), (krk, qr)], [r_kn_[hp], r_qn_[hp], r_krk, r_qr_[hp]], lambda kt: vpair[:, kt, :], r_vp,
                           blocks, scale, negm[:, hp:hp + 1], r_negm_[hp], pT, r_pT, fin)

        prep_pair(0)
        prep(0)
        for h in range(16):
            if h + 1 < 16:
                if (h + 1) % 2 == 0:
                    prep_pair((h + 1) // 2)
                prep(h + 1)
            attn(h)
        self.attn_flush()

        Wo, rWo = self.wload(dr["mla_w_o"][0], 8, 1024)
        for half in range(2):
            tt = u * 2 + half
            ts = slice(tt * 512, (tt + 1) * 512)
            ls = slice(half * 512, (half + 1) * 512)
            for dc in range(8):
                ps, rps = self.bank()
                for cc in range(8):
                    R.mm(ps[:], Wo[:, cc, dc * 128:(dc + 1) * 128], oT[:, cc, ls], cc == 0, cc == 7, [rWo, r_oT[cc][half]],
                         [rps])
                R.op("dve", "scalar_tensor_tensor", [rps, self.r_mod, self.r_xT[dc][tt]], [self.r_xT[dc][tt]],
                     out=self.xT[:, dc, ts], in0=ps[:], scalar=self.mod[:, 16 + dc, u:u + 1],
                     in1=self.xT[:, dc, ts], op0=ALU.mult, op1=ALU.add)

    def mlstm(self, i, u):
        R = self.R
        nc = self.nc
        dr = self.dram
        pc = self.pcol
        hT, r_h = self.hT, self.r_h
        r_hl = [r_h[u * 2], r_h[u * 2 + 1]]
        nseq, L = (NPS, LP) if u == 0 else (1, LS)
        cps = L // 128
        identf, identb, onesf = self.identf, self.identb, self.onesf
        rc = self.r_const
        bc = lambda ap, shape: ap.to_broadcast(shape)
        w_up = dr["mlstm_w_up"][0]
        if not hasattr(self, "zsp"):
            self.zsp = nc.dram_tensor("zsp", [128, 16, TU], BF16).ap()
            self.r_zsp = Res("zsp")
        zsp, r_zsp = self.zsp, self.r_zsp

        A = self._Alloc(self, 0)
        hTv = A([128, 8, TU], BF16)
        xmT = A([128, 16, TU], BF16)
        xcT = A([128, 16, TU], BF16)
        e_tok = A([128, 8, 16], F32)
        emc_tok = A([128, 8, 16], F32)
        iwb = A([128, 8, 16], F32)
        mark = A.off
        r_xm = [Res("xm%d" % c) for c in range(16)]
        r_xc = [Res("xc%d" % c) for c in range(16)]
        r_tok = Res("mtok")
        persist = r_xm + r_xc + [r_tok]

        pre_ = [A([128, nseq, L + 4], F32) for _ in range(2)]
        acc = A([128, TU], F32)
        zst = [A([128, TU], BF16) for _ in range(2)]
        r_pre_, r_acc, r_g = [Res("pre0"), Res("pre1")], Res("acc"), Res("gates")
        r_zst = [Res("zst0"), Res("zst1")]
        self.stage(r_hl + persist + r_pre_ + [r_acc] + r_zst)
        for k in range(2):
            R.op("dve", "memset", [], [r_pre_[k]], ap=pre_[k], constant=0.0)
        cw, cb = pc["mlstm_conv_w"], pc["mlstm_conv_b"]
        for blk in range(2):
            W, rW = self.wload(w_up[:, blk * 1024:(blk + 1) * 1024], 8, 1024)
            for cl in range(8):
                cc = blk * 8 + cl
                pre, r_pre = pre_[cc % 2], r_pre_[cc % 2]
                for tt in range(2):
                    ps, rps = self.bank()
                    for kc in range(8):
                        R.mm(ps[:], W[:, kc, cl * 128:(cl + 1) * 128], hT[:, kc, tt * 512:(tt + 1) * 512], kc == 0, kc == 7,
                             [rW, r_hl[tt]], [rps])
                    if u == 0:
                        R.op("act", "activation", [rps], [r_pre], out=pre[:, 2 * tt:2 * tt + 2, 2:2 + L],
                             in_=ps[:].rearrange("p (s t) -> p s t", s=2), func=AF.Identity)
                    else:
                        R.op("act", "activation", [rps], [r_pre], out=pre[:, 0, 2 + tt * 512:2 + (tt + 1) * 512], in_=ps[:],
                             func=AF.Identity)
                    R.op("dve", "tensor_copy", [rps], [r_xm[cc]], out=xmT[:, cc, tt * 512:(tt + 1) * 512], in_=ps[:])
                acc3 = acc.rearrange("p (s t) -> p s t", s=nseq)
                R.op("dve", "tensor_scalar", [r_pre, self.r_prm], [r_acc], out=acc3, in0=pre[:, :, 0:L],
                     scalar1=self.prm[:, cw + cc:cw + cc + 1], scalar2=self.prm[:, cb + cc:cb + cc + 1], op0=ALU.mult,
                     op1=ALU.add)
                for k in range(1, 5):
                    R.op("dve", "scalar_tensor_tensor", [r_pre, self.r_prm, r_acc], [r_acc], out=acc3,
                         in0=pre[:, :, k:k + L], scalar=self.prm[:, cw + k * 16 + cc:cw + k * 16 + cc + 1], in1=acc3,
                         op0=ALU.mult, op1=ALU.add)
                R.op("act", "activation", [r_acc], [r_xc[cc]], out=xcT[:, cc, :], in_=acc, func=AF.Silu)
        for blk in range(2):
            W, rW = self.wload(w_up[:, 2048 + blk * 1024:2048 + (blk + 1) * 1024], 8, 1024)
            for cl in range(8):
                cc = blk * 8 + cl
                k = cc % 2
                for tt in range(2):
                    ps, rps = self.bank()
                    for kc in range(8):
                        R.mm(ps[:], W[:, kc, cl * 128:(cl + 1) * 128], hT[:, kc, tt * 512:(tt + 1) * 512], kc == 0, kc == 7,
                             [rW, r_hl[tt]], [rps])
                    R.op("act", "activation", [rps], [r_zst[k]], out=zst[k][:, tt * 512:(tt + 1) * 512], in_=ps[:], func=AF.Silu)
                R.dma("sp", zsp[:, cc, :], zst[k], [r_zst[k]], [r_zsp])
        A.off = mark
        Wg40 = A([128, 8, 2, 40], BF16)
        gb = A([40, 2], F32)
        GI = A([40, TU], F32)
        X = A([40, TU], F32)
        T1 = A([40, TU], F32)
        Bt = A([40, TU], F32)
        cm = Bt
        Mt = A([40, 8], F32)
        mint = A([40, 9], F32)
        amax = A([40, 8], F32)
        iw = A([40, 8], F32)
        mcur = A([40, 1], F32)
        mfin = A([40, 4], F32)
        xd = A([40, 8, 40], F32)
        self.stage(r_hl + persist + [r_g])
        RG = [r_g]
        Wg, rWg = self.wload(w_up[:, 4096:4128], 8, 32)
        R.op("dve", "memset", [], RG, ap=Wg40, constant=0.0)
        for (f, dst0, src0) in ((0, 0, 0), (0, 32, 16), (1, 0, 8), (1, 32, 24)):
            R.op("dve", "tensor_copy", [rWg] + RG, RG, out=Wg40[:, :, f, dst0:dst0 + 8], in_=Wg[:, :, src0:src0 + 8])
        R.op("dve", "memset", RG, RG, ap=gb, constant=0.0)
        gbd = dr["mlstm_gate_b"][0]
        for d in range(2):
            for f in range(2):
                R.dma("sp", gb[d * 32:d * 32 + 8, f:f + 1], gbd[d, f].rearrange("(h o) -> h o", o=1), RG, RG)
        for f, dst in ((0, GI), (1, X)):
            for tt in range(2):
                ps, rps = self.bank()
                for kc in range(8):
                    R.mm(ps[0:40, :], Wg40[:, kc, f, :], hT[:, kc, tt * 512:(tt + 1) * 512], kc == 0, kc == 7,
                         RG + [r_hl[tt]], [rps])
                R.op("dve", "tensor_scalar", [rps] + RG, RG, out=dst[:, tt * 512:(tt + 1) * 512], in0=ps[0:40, :],
                     scalar1=gb[:, f:f + 1], scalar2=None, op0=ALU.add)
        R.op("act", "activation", RG, RG, out=T1, in_=X, func=AF.Abs)
        R.op("act", "activation", RG, RG, out=T1, in_=T1, func=AF.Exp, scale=-1.0)
        R.op("act", "activation", RG, RG, out=T1, in_=T1, func=AF.Ln, bias=1.0)
        R.op("dve", "scalar_tensor_tensor", RG, RG, out=X, in0=X, scalar=0.0, in1=T1, op0=ALU.min, op1=ALU.subtract)
        cm3 = cm.rearrange("p (c t) -> p c t", t=128)
        R.op("dve", "memset", RG, RG, ap=cm, constant=1.0)
        R.op("dve", "memset", RG, RG, ap=cm3[:, :, 0:1], constant=0.0)
        R.op("dve", "tensor_tensor_scan", RG, RG, out=T1, data0=cm, data1=X, initial=0.0, op0=ALU.mult, op1=ALU.add)
        cum3 = T1.rearrange("p (c t) -> p c t", t=128)
        tot = cum3[:, :, 127:128]
        Bt3 = Bt.rearrange("p (c t) -> p c t", t=128)
        R.op("dve", "tensor_copy", RG, RG, out=Bt, in_=T1)
        R.op("dve", "tensor_tensor", RG, RG, out=Bt3[32:40], in0=bc(tot[32:40], [8, 8, 128]), in1=cum3[32:40], op=ALU.subtract)
        R.op("dve", "tensor_tensor", RG, RG, out=Bt[32:40, :], in0=Bt[32:40, :], in1=X[32:40, :], op=ALU.add)
        R.op("dve", "tensor_tensor", RG, RG, out=GI, in0=GI, in1=Bt, op=ALU.subtract)
        R.op("dve", "tensor_reduce", RG, RG, out=amax, in_=GI.rearrange("p (c t) -> p c t", t=128), axis=AX.X, op=ALU.max)
        R.op("dve", "memset", RG, RG, ap=mfin, constant=0.0)
        for d, rows in ((0, slice(0, 8)), (1, slice(32, 40))):
            for seq in range(nseq):
                if u == 0:
                    R.op("dve", "memset", RG, RG, ap=mcur[rows, :], constant=0.0)
                else:
                    R.dma("sp", mcur[rows, :], dr["st_m"][d].rearrange("(h o) -> h o", o=1), RG, RG)
                cl = list(range(seq * cps, (seq + 1) * cps))
                if d == 1:
                    cl = cl[::-1]
                for c in cl:
                    R.op("dve", "tensor_copy", RG, RG, out=mint[rows, c:c + 1], in_=mcur[rows, :])
                    R.op("dve", "tensor_tensor", RG, RG, out=Mt[rows, c:c + 1], in0=mcur[rows, :], in1=amax[rows, c:c + 1],
                         op=ALU.max)
                    R.op("dve", "tensor_tensor", RG, RG, out=mcur[rows, :], in0=Mt[rows, c:c + 1], in1=tot[rows, c, :],
                         op=ALU.add)
                R.op("dve", "tensor_copy", RG, RG, out=mfin[rows, seq:seq + 1], in_=mcur[rows, :])
                if u == 0:
                    R.dma("sp", dr["o_m"][seq, d].rearrange("(h o) -> h o", o=1), mfin[rows, seq:seq + 1], RG, [])
        R.op("dve", "memset", RG, RG, ap=amax, constant=0.0)
        R.op("dve", "tensor_tensor", RG, RG, out=iw[0:8, :], in0=mint[0:8, 0:8], in1=Mt[0:8, :], op=ALU.subtract)
        R.op("dve", "tensor_tensor", RG, RG, out=iw[32:40, :], in0=mint[32:40, 0:8], in1=Mt[32:40, :], op=ALU.subtract)
        R.op("dve", "memset", RG, RG, ap=xd, constant=0.0)
        for rows in (slice(0, 8), slice(32, 40)):
            R.op("act", "activation", RG, RG, out=iw[rows, :], in_=iw[rows, :], func=AF.Exp)
            R.op("dve", "tensor_scalar", RG, RG, out=amax[rows, :], in0=Mt[rows, :], scalar1=-1.0, scalar2=None, op0=ALU.mult)
            R.op("dve", "tensor_tensor", RG + [rc], RG, out=xd[rows], in0=bc(identf[rows, 0:40].unsqueeze(1), [8, 8, 40]),
                 in1=bc(iw[rows, :].unsqueeze(2), [8, 8, 40]), op=ALU.mult)
        ps, rps = self.bank()
        R.mm(ps[:, 0:320], onesf[0:40, :], xd.rearrange("p c h -> p (c h)"), True, True, RG + [rc], [rps])
        ps3 = ps[:, 0:320].rearrange("p (c h) -> p c h", h=40)
        R.op("dve", "tensor_copy", [rps], [r_tok], out=iwb[:, :, 0:8], in_=ps3[:, :, 0:8])
        R.op("dve", "tensor_copy", [rps], [r_tok], out=iwb[:, :, 8:16], in_=ps3[:, :, 32:40])
        for rows in (slice(0, 8), slice(32, 40)):
            for c in range(8):
                cs = slice(c * 128, (c + 1) * 128)
                R.op("act", "activation", RG, RG, out=GI[rows, cs], in_=GI[rows, cs], func=AF.Exp, bias=amax[rows, c:c + 1])
                R.op("act", "activation", RG, RG, out=Bt[rows, cs], in_=Bt[rows, cs], func=AF.Exp, scale=-1.0,
                     bias=amax[rows, c:c + 1])
        for c in range(8):
            cs = slice(c * 128, (c + 1) * 128)
            ps, rps = self.bank()
            R.tr(ps[:, 0:40], GI[:, cs], identf[0:40, 0:40], RG + [rc], [rps])
            R.tr(ps[:, 64:104], Bt[:, cs], identf[0:40, 0:40], RG + [rc], [rps])
            R.op("dve", "tensor_copy", [rps], [r_tok], out=e_tok[:, c, 0:8], in_=ps[:, 0:8])
            R.op("dve", "tensor_copy", [rps], [r_tok], out=e_tok[:, c, 8:16], in_=ps[:, 32:40])
            R.op("dve", "tensor_copy", [rps], [r_tok], out=emc_tok[:, c, 0:8], in_=ps[:, 64:72])
            R.op("dve", "tensor_copy", [rps], [r_tok], out=emc_tok[:, c, 8:16], in_=ps[:, 96:104])

        A.off = 16384 + 2 * 32768 + 3 * 512
        assert A.off == mark
        A.off = 0
        yTq = A([128, 4, TU], BF16)
        zsq = A([128, 4, TU], BF16)
        assert A.off <= 16384
        A.off = mark
        qT = A([128, TU], BF16)
        kT = A([128, TU], BF16)
        v_tok = A([128, 8, 257], BF16)
        k_tok = A([128, 8, 128], BF16)
        Cpb = A([128, 8, 257], BF16)
        _c0, _c1, _c2 = A([128, 257], F32), A([128, 257], F32), A([128, 257], F32)
        Cst2 = [[_c0, _c1], [_c2, _c1]]
        Cst = list(Cst2[0])
        Cp = [A([128, 257], BF16) for _ in range(2)]
        sT = [[A([128, 128], BF16) for _ in range(2)] for _ in range(2)]
        kw = A([128, 128], BF16)
        _hs = A([128, 256], F32)
        hs_ = [_hs, _hs]
        hn_ = [A([128, 256], BF16) for _ in range(2)]
        sml_ = [A([128, 8], F32) for _ in range(2)]
        u1 = A([128, 128], F32)
        r_yTq = [[Res("yTq%d_%d" % (c, t)) for t in range(2)] for c in range(4)]
        r_zsq, r_qT, r_kT = Res("zsq"), Res("qT"), Res("kT")
        r_vt = [Res("vt%d" % c) for c in range(8)]
        r_kt = [Res("kt%d" % c) for c in range(8)]
        r_Cpb = [Res("Cpb%d" % c) for c in range(8)]
        _r0, _r1, _r2 = Res("Cst00"), Res("Cst01"), Res("Cst10")
        r_Cst2 = [[_r0, _r1], [_r2, _r1]]
        r_Cst = list(r_Cst2[0])
        r_kw, r_u1 = Res("kw"), Res("u1")
        _rhs = Res("hs")
        r_hs_ = [_rhs, _rhs]
        r_hn_ = [Res("hn0"), Res("hn1")]
        r_sml_ = [Res("sml0"), Res("sml1")]
        r_Cp = [Res("Cp0"), Res("Cp1")]
        r_sT = [[Res("sT00"), Res("sT01")], [Res("sT10"), Res("sT11")]]
        self.stage(persist + [r for rr in r_yTq for r in rr] + [r_zsq, r_qT, r_kT] + r_vt + r_kt + r_Cpb + [_r0, _r1, _r2] +
                   r_Cp + [r_kw, r_u1, _rhs] + r_hn_ + r_sml_ + r_sT[0] + r_sT[1])
        R.op("dve", "memset", [], r_vt, ap=v_tok[:, :, 256:257], constant=1.0)
        gcol, scol = pc["mlstm_norm_g"], pc["mlstm_skip"]
        kscale = 128.0 ** -0.5

        def st_sel(seq, d):
            Cst[d] = Cst2[seq % 2][d]
            r_Cst[d] = r_Cst2[seq % 2][d]

        def st_init(d, h):
            if u == 0:
                R.op("dve", "memset", [], [r_Cst[d]], ap=Cst[d], constant=0.0)
            else:
                R.dma("sp", Cst[d][:, 0:256], dr["st_C"][d, h], [], [r_Cst[d]])
                R.dma("sp", Cst[d][:, 256:257], dr["st_n"][d, h].rearrange("(p o) -> p o", o=1), [], [r_Cst[d]])

        def st_out(seq, d, h):
            R.dma("sp", dr["o_C"][seq, d, h], Cst[d][:, 0:256], [r_Cst[d]], [])
            R.dma("sp", dr["o_n"][seq, d, h].rearrange("(p o) -> p o", o=1), Cst[d][:, 256:257], [r_Cst[d]], [])

        def st_decay(c, d, h):
            col = d * 8 + h
            R.op("dve", "tensor_scalar", [r_Cst[d], r_tok], [r_Cst[d]], out=Cst[d], in0=Cst[d],
                 scalar1=iwb[:, c, col:col + 1], scalar2=None, op0=ALU.mult)

        def st_update(c, d, h):
            col = d * 8 + h
            R.op("dve", "tensor_scalar", [r_kt[c], r_tok], [r_kw], out=kw, in0=k_tok[:, c, :],
                 scalar1=e_tok[:, c, col:col + 1], scalar2=None, op0=ALU.mult)
            ps, rps = self.bank()
            R.mm(ps[:, 0:257], kw, v_tok[:, c, :], True, True, [r_kw, r_vt[c]], [rps])
            R.op("dve", "tensor_tensor", [rps, r_Cst[d]], [r_Cst[d]], out=Cst[d], in0=ps[:, 0:257], in1=Cst[d], op=ALU.add)

        for pair in range(4):
            R.dma("sp", zsq, zsp[:, pair * 4:(pair + 1) * 4, :], [r_zsp], [r_zsq])
            for hp in range(2):
                h = pair * 2 + hp
                W, rW = self.wload_multi([dr["mlstm_w_q"][0][:, h * 128:(h + 1) * 128],
                                          dr["mlstm_w_k"][0][:, h * 128:(h + 1) * 128],
                                          dr["mlstm_w_v"][0][:, h * 256:(h + 1) * 256]], 16)
                for tt in range(2):
                    ts = slice(tt * 512, (tt + 1) * 512)
                    ps, rps = self.bank()
                    for kc in range(16):
                        R.mm(ps[:], W[:, kc, 0:128], xcT[:, kc, ts], kc == 0, kc == 15, [rW, r_xc[kc]], [rps])
                    self.copy(self.evac_engine(), qT[:, ts], ps[:], [rps], [r_qT])
                    ps, rps = self.bank()
                    for kc in range(16):
                        R.mm(ps[:], W[:, kc, 128:256], xcT[:, kc, ts], kc == 0, kc == 15, [rW, r_xc[kc]], [rps])
                    R.op("act", "activation", [rps], [r_kT], out=kT[:, ts], in_=ps[:], func=AF.Identity, scale=kscale)
                for c in range(8):
                    cs = slice(c * 128, (c + 1) * 128)
                    ps, rps = self.bank()
                    for kc in range(16):
                        R.mm(ps[:, 0:256], xmT[:, kc, cs], W[:, kc, 256:512], kc == 0, kc == 15, [rW, r_xm[kc]], [rps])
                    self.copy(self.evac_engine(), v_tok[:, c, 0:256], ps[:, 0:256], [rps], [r_vt[c]])
                ps, rps = self.bank()
                psb = ps[:].bitcast(BF16)
                for c in range(8):
                    R.tr(psb[:, c * 128:(c + 1) * 128], kT[:, c * 128:(c + 1) * 128], identb[:], [r_kT, rc], [rps])
                R.op("dve", "tensor_copy", [rps], r_kt, out=k_tok.rearrange("p c d -> p (c d)"), in_=psb[:, 0:1024])
                for seq in range(nseq):
                    st_sel(seq, 1)
                    st_init(1, h)
                    for c in reversed(range(seq * cps, (seq + 1) * cps)):
                        st_decay(c, 1, h)
                        R.op("act", "activation", [r_Cst[1]], [r_Cpb[c]], out=Cpb[:, c, :], in_=Cst[1], func=AF.Identity)
                        if u == 0 or c > seq * cps:
                            st_update(c, 1, h)
                    if u == 0:
                        st_out(seq, 1, h)
                def F(c):
                    cs = slice(c * 128, (c + 1) * 128)
                    pp = c % 2
                    pss, rpss = self.bank()
                    R.mm(pss[:, 0:128], kT[:, cs], qT[:, cs], True, True, [r_kT, r_qT], [rpss])
                    for d, msk in ((0, self.mle), (1, self.mge)):
                        col = d * 8 + h
                        R.op("dve", "scalar_tensor_tensor", [rpss, r_tok, rc], [r_sT[pp][d]], out=sT[pp][d], in0=pss[:, 0:128],
                             scalar=e_tok[:, c, col:col + 1], in1=msk[:], op0=ALU.mult, op1=ALU.mult)

                def Astep(c):
                    seq = c // cps
                    pp = c % 2
                    if c == seq * cps:
                        st_sel(seq, 0)
                        st_init(0, h)
                    st_decay(c, 0, h)
                    R.op("act", "activation", [r_Cst[0]], [r_Cp[pp]], out=Cp[pp], in_=Cst[0], func=AF.Identity)
                    if u == 0 or c < (seq + 1) * cps - 1:
                        st_update(c, 0, h)
                    if u == 0 and c == (seq + 1) * cps - 1:
                        st_out(seq, 0, h)

                def B1(c):
                    cs = slice(c * 128, (c + 1) * 128)
                    pp = c % 2
                    hs, hn, sml, sqj = hs_[pp], hn_[pp], sml_[pp], hn_[pp]
                    r_hs, r_hn, r_sml = r_hs_[pp], r_hn_[pp], r_sml_[pp]
                    pn = []
                    for d in range(2):
                        p_, rp_ = self.bank(hold=True)
                        R.mm(p_[:, 0:257], sT[pp][d], v_tok[:, c, :], True, False, [r_sT[pp][d], r_vt[c]], [rp_])
                        if d == 0:
                            R.mm(p_[:, 0:257], qT[:, cs], Cp[pp], False, True, [r_qT, r_Cp[pp]], [rp_])
                        else:
                            R.mm(p_[:, 0:257], qT[:, cs], Cpb[:, c, :], False, True, [r_qT, r_Cpb[c]], [rp_])
                        pn.append((p_, rp_))
                    for d in range(2):
                        col = d * 8 + h
                        p_, rp_ = pn[d]
                        R.op("act", "activation", [rp_], [r_sml], out=sml[:, d:d + 1], in_=p_[:, 256:257], func=AF.Abs)
                        R.op("dve", "tensor_tensor", [r_sml, r_tok], [r_sml], out=sml[:, d:d + 1], in0=sml[:, d:d + 1],
                             in1=emc_tok[:, c, col:col + 1], op=ALU.max)
                        R.op("dve", "reciprocal", [r_sml], [r_sml], out=sml[:, 2 + d:3 + d], in_=sml[:, d:d + 1])
                    R.op("dve", "tensor_scalar", [pn[0][1], r_sml], [r_hs], out=hs, in0=pn[0][0][:, 0:256],
                         scalar1=sml[:, 2:3], scalar2=None, op0=ALU.mult)
                    R.op("dve", "scalar_tensor_tensor", [pn[1][1], r_sml, r_hs], [r_hs], out=hs, in0=pn[1][0][:, 0:256],
                         scalar=sml[:, 3:4], in1=hs, op0=ALU.mult, op1=ALU.add)
                    self.release(pn[0][0])
                    self.release(pn[1][0])
                    R.op("act", "activation", [r_hs], [r_hn, r_sml], out=sqj, in_=hs, func=AF.Square, accum_out=sml[:, 4:5])
                    R.op("act", "activation", [r_sml], [r_sml], out=sml[:, 5:6], in_=sml[:, 4:5], func=AF.Ln, scale=1.0 / 256,
                         bias=EPS)
                    R.op("act", "activation", [r_sml], [r_sml], out=sml[:, 5:6], in_=sml[:, 5:6], func=AF.Exp, scale=-0.5)
                    R.op("dve", "tensor_scalar", [r_hs, r_sml], [r_hn], out=hn, in0=hs, scalar1=sml[:, 5:6], scalar2=None,
                         op0=ALU.mult)

                def B2(c):
                    pp = c % 2
                    hn, r_hn = hn_[pp], r_hn_[pp]
                    q4 = c % 4
                    if q4 == 0:
                        self.pt_cur = self.bank(hold=True)
                    pt, rpt = self.pt_cur
                    ptb = pt[:].bitcast(BF16)
                    for c2 in range(2):
                        R.tr(ptb[:, (c2 * 4 + q4) * 128:(c2 * 4 + q4 + 1) * 128], hn[:, c2 * 128:(c2 + 1) * 128], identb[:],
                             [r_hn, rc], [rpt])
                    if q4 == 3:
                        tl = c // 4
                        for c2 in range(2):
                            ch = 2 * h + c2
                            lc = 2 * hp + c2
                            for hf in range(4):
                                t2 = slice(tl * 512 + hf * 128, tl * 512 + (hf + 1) * 128)
                                R.op("dve", "tensor_scalar", [rpt, self.r_prm], [r_u1], out=u1,
                                     in0=ptb[:, c2 * 512 + hf * 128:c2 * 512 + (hf + 1) * 128],
                                     scalar1=self.prm[:, gcol + ch:gcol + ch + 1], scalar2=None, op0=ALU.mult)
                                R.op("dve", "scalar_tensor_tensor", [r_xc[ch], self.r_prm, r_u1], [r_u1], out=u1,
                                     in0=xcT[:, ch, t2], scalar=self.prm[:, scol + ch:scol + ch + 1], in1=u1, op0=ALU.mult,
                                     op1=ALU.add)
                                R.op("dve", "tensor_tensor", [r_u1, r_zsq], [r_yTq[lc][tl]], out=yTq[:, lc, t2], in0=u1,
                                     in1=zsq[:, lc, t2], op=ALU.mult)
                        self.release(pt)

                F(0)
                Astep(0)
                for c in range(8):
                    if c + 1 < 8:
                        F(c + 1)
                        Astep(c + 1)
                    B1(c)
                    if c > 0:
                        B2(c - 1)
                B2(7)
            Wd, rWd = self.wload(dr["mlstm_w_down"][0][pair * 512:(pair + 1) * 512, :], 4, 1024)
            for half in range(2):
                tt = u * 2 + half
                ts = slice(tt * 512, (tt + 1) * 512)
                ls = slice(half * 512, (half + 1) * 512)
                for dc in range(8):
                    ps, rps = self.bank()
                    for lc in range(4):
                        R.mm(ps[:], Wd[:, lc, dc * 128:(dc + 1) * 128], yTq[:, lc, ls], lc == 0, lc == 3,
                             [rWd, r_yTq[lc][half]], [rps])
                    R.op("dve", "scalar_tensor_tensor", [rps, self.r_mod, self.r_xT[dc][tt]], [self.r_xT[dc][tt]],
                         out=self.xT[:, dc, ts], in0=ps[:], scalar=self.mod[:, 16 + dc, u:u + 1],
                         in1=self.xT[:, dc, ts], op0=ALU.mult, op1=ALU.add)

    def diff(self, i, u):
        R = self.R
        dr = self.dram
        pc = self.pcol
        hT, r_h = self.hT, self.r_h
        r_hl = [r_h[u * 2], r_h[u * 2 + 1]]
        koff = 0 if u == 0 else 256
        Tk = TU + koff
        nkt = Tk // 128
        identf, identb, onesf = self.identf, self.identb, self.onesf
        rc = self.r_const
        scale = 64.0 ** -0.5
        lam_init = 0.8 - 0.6 * math.exp(-0.3 * i)
        wqkv = dr["diff_w_qkv"][0]

        A = self._Alloc(self, 16384)
        qT = A([128, 8, TU], BF16)
        kT = A([128, 8, 1280], BF16)
        v_tok = A([128, 10, 1024], BF16)
        lamt = A([128, 4], F32)
        gs = A([128, 1], F32)
        mark = A.off
        r_qT = [Res("qT%d" % m) for m in range(8)]
        r_kT = [Res("kT%d" % m) for m in range(8)]
        r_v = [Res("v%d" % k) for k in range(10)]
        r_lam = Res("lam")
        persist = r_qT + r_kT + r_v + [r_lam]

        rope = A([128, 2, TU], F32)
        qraw = A([128, 512], BF16)
        tA = A([128, 512], F32)
        tB = A([128, 512], F32)
        kvst = [A([128, 1024], F32) for _ in range(2)]
        perm = A([128, 128], BF16)
        lqk = A([64, 4], F32)
        r_rope, r_qraw, r_tA, r_tB, r_perm, r_lqk = Res("rope"), Res("qraw"), Res("tA"), Res("tB"), Res("perm"), Res("lqk")
        r_kvst = [Res("kvst0"), Res("kvst1")]
        self.stage(r_hl + persist + [r_rope, r_qraw, r_tA, r_tB, r_perm, r_lqk] + r_kvst)
        for k, nm in enumerate(["diff_lq1", "diff_lk1", "diff_lq2", "diff_lk2"]):
            R.dma("sp", lqk[:, k:k + 1], dr[nm][0].rearrange("(d o) -> d o", o=1), [], [r_lqk])
        R.op("dve", "tensor_tensor", [r_lqk], [r_lqk], out=lqk[:, 0:1], in0=lqk[:, 0:1], in1=lqk[:, 1:2], op=ALU.mult)
        R.op("dve", "tensor_tensor", [r_lqk], [r_lqk], out=lqk[:, 1:2], in0=lqk[:, 2:3], in1=lqk[:, 3:4], op=ALU.mult)
        ps, rps = self.bank()
        R.mm(ps[:, 0:2], onesf[0:64, :], lqk[:, 0:2], True, True, [r_lqk, rc], [rps])
        R.op("act", "activation", [rps], [r_lam], out=lamt[:, 0:2], in_=ps[:, 0:2], func=AF.Exp)
        R.op("dve", "tensor_tensor", [r_lam], [r_lam], out=lamt[:, 2:3], in0=lamt[:, 1:2], in1=lamt[:, 0:1], op=ALU.subtract)
        R.op("dve", "tensor_scalar", [r_lam], [r_lam], out=lamt[:, 2:3], in0=lamt[:, 2:3], scalar1=-lam_init, scalar2=None,
             op0=ALU.add)
        R.op("dve", "tensor_scalar", [self.r_prm], [r_lam], out=gs,
             in0=self.prm[:, pc["diff_subln_g"]:pc["diff_subln_g"] + 1], scalar1=1.0 - lam_init, scalar2=None,
             op0=ALU.mult)
        if u == 1:
            R.dma("sp", rope, dr["rope128"].rearrange("a r t -> r a t"), [], [r_rope])
            for (d0, s0) in ((0, 32), (32, 0), (64, 96), (96, 64)):
                R.op("dve", "tensor_copy", [rc], [r_perm], out=perm[:, d0:d0 + 32], in_=identb[:, s0:s0 + 32])
            ck = dr["c_dk"].rearrange("(a p) h d -> p a (h d)", p=128)
            cv = dr["c_dv"].rearrange("(a p) h d -> p a (h d)", p=128)
            for a in range(2):
                R.dma("sp", kvst[0], ck[:, a, :], [], [r_kvst[0]])
                for mh in range(2):
                    ps, rps = self.bank()
                    for q in range(4):
                        m = mh * 4 + q
                        R.tr(ps[:, q * 128:(q + 1) * 128], kvst[0][:, m * 128:(m + 1) * 128], identf[:], [r_kvst[0], rc],
                             [rps])
                    self.copy(self.evac_engine(), kT[:, mh * 4:(mh + 1) * 4, a * 128:(a + 1) * 128],
                              ps[:].rearrange("p (q t) -> p q t", q=4), [rps], r_kT[mh * 4:(mh + 1) * 4])
                R.dma("sp", kvst[1], cv[:, a, :], [], [r_kvst[1]])
                R.op("dve", "tensor_copy", [r_kvst[1]], [r_v[a]], out=v_tok[:, a, :], in_=kvst[1])

        def fm_proj(W, rW, dst, r_dst, col_off):
            for m in range(8):
                for tt in range(2):
                    ts = slice(tt * 512, (tt + 1) * 512)
                    ps, rps = self.bank()
                    for kc in range(8):
                        R.mm(ps[:], W[:, kc, m * 128:(m + 1) * 128], hT[:, kc, ts], kc == 0, kc == 7, [rW, r_hl[tt]], [rps])
                    od = dst[:, m, col_off + tt * 512:col_off + (tt + 1) * 512]
                    if u == 0:
                        self.copy(self.evac_engine(), od, ps[:], [rps], [r_dst[m]])
                    else:
                        R.op("act", "activation", [rps], [r_qraw], out=qraw, in_=ps[:], func=AF.Identity)
                        R.op("dve", "tensor_tensor", [rps, r_rope], [r_tA], out=tA, in0=ps[:], in1=rope[:, 0, ts], op=ALU.mult)
                        ps2, rps2 = self.bank()
                        R.mm(ps2[:], perm, qraw, True, True, [r_perm, r_qraw], [rps2])
                        R.op("dve", "tensor_tensor", [rps2, r_rope], [r_tB], out=tB, in0=ps2[:], in1=rope[:, 1, ts],
                             op=ALU.mult)
                        R.op("dve", "tensor_tensor", [r_tA, r_tB], [r_dst[m]], out=od, in0=tA, in1=tB, op=ALU.add)

        def tm_proj(W, rW, out_name, to_v):
            for c in range(8):
                cs = slice(c * 128, (c + 1) * 128)
                kk = c % 2
                for half in range(2):
                    ps, rps = self.bank()
                    for kc in range(8):
                        R.mm(ps[:], hT[:, kc, cs], W[:, kc, half * 512:(half + 1) * 512], kc == 0, kc == 7,
                             [rW, r_hl[c // 4]], [rps])
                    if to_v:
                        R.op("act", "activation", [rps], [r_v[koff // 128 + c]],
                             out=v_tok[:, koff // 128 + c, half * 512:(half + 1) * 512], in_=ps[:], func=AF.Identity)
                    if u == 0:
                        R.op("dve", "tensor_copy", [rps], [r_kvst[kk]], out=kvst[kk][:, half * 512:(half + 1) * 512], in_=ps[:])
                if u == 0:
                    seq, t0 = (c * 128) // LP, (c * 128) % LP
                    R.dma("sp", dr[out_name][seq, t0:t0 + 128].rearrange("t h d -> t (h d)"), kvst[kk], [r_kvst[kk]], [])

        Wq, rWq = self.wload(wqkv[:, 0:1024], 8, 1024)
        fm_proj(Wq, rWq, qT, r_qT, 0)
        Wk, rWk = self.wload(wqkv[:, 1024:2048], 8, 1024)
        fm_proj(Wk, rWk, kT, r_kT, koff)
        if u == 0:
            tm_proj(Wk, rWk, "o_dk", False)
        Wv, rWv = self.wload(wqkv[:, 2048:3072], 8, 1024)
        tm_proj(Wv, rWv, "o_dv", True)

        A.off = mark
        oT = self.av(0, [128, 8, TU], BF16)
        NS = 4
        pT = A([128, NS, 512], BF16)
        qz = [[A([128, TU], BF16) for _ in range(2)] for _ in range(2)]
        r_qz = [[Res("qz%d%d" % (a_, b_)) for b_ in range(2)] for a_ in range(2)]
        a1 = A([128, 512], F32)
        rec = A([128, 512], F32)
        o32 = A([128, 512], F32)
        sqo = A([128, 512], BF16)
        rstd = A([128, 512], F32)
        lnt = A([128, 512], F32)
        sqt = A([128, 512], BF16)
        red = A([128, 8], F32)
        negm = A([128, 2], F32)
        r_oT = [[Res("oT%d_%d" % (c, t)) for t in range(2)] for c in range(8)]
        r_pT = [Res("pT%d" % k) for k in range(NS)]
        r_a1, r_rec, r_o32, r_sqo, r_rstd, r_sqt, r_red, r_negm = (Res("a1"), Res("rec"), Res("o32"), Res("sqo"),
                                                                   Res("rstd"), Res("sqt"), Res("red"), Res("negm"))
        self.stage(persist + [r for rr in r_oT for r in rr] + r_pT + r_qz[0] + r_qz[1] +
                   [r_a1, r_rec, r_o32, r_sqo, r_rstd, r_sqt, r_red, r_negm])
        for a_ in range(2):
            for b_ in range(2):
                R.op("dve", "memset", [], [r_qz[a_][b_]], ap=qz[a_][b_], constant=0.0)
        if u == 1:
            blocks = [(0, 512, list(range(10))), (512, 512, list(range(10)))]
        else:
            blocks = [(s * 256, 256, [2 * s, 2 * s + 1]) for s in range(4)]
        for h in range(8):
            for br in range(2):
                rows = slice(br * 64, (br + 1) * 64)
                self.sq_bound([qT[rows, h, :]], [r_qT[h]], TU, 0, sqt, r_sqt, red, r_red, p0=br * 64)
                self.sq_bound([kT[rows, h, 0:Tk]], [r_kT[h]], Tk, 1, sqt, r_sqt, red, r_red, p0=br * 64)
                R.op("dve", "tensor_tensor", [r_red], [r_red], out=red[:, 2:3], in0=red[:, 0:1], in1=red[:, 1:2], op=ALU.add)
                R.op("dve", "tensor_scalar", [r_red], [r_negm], out=negm[:, br:br + 1], in0=red[:, 2:3],
                     scalar1=-0.5 * scale, scalar2=None, op0=ALU.mult)

            def fin1(q0, nq, po, r_po, psm, r_psm):
                R.op("dve", "reciprocal", [r_psm], [r_rec], out=rec[:, 0:nq], in_=psm[:, 0:nq])
                R.op("dve", "tensor_tensor", [r_po, r_rec], [r_a1], out=a1[:, 0:nq], in0=po[:, 0:nq], in1=rec[:, 0:nq],
                     op=ALU.mult)

            def fin2(q0, nq, po, r_po, psm, r_psm, h=h):
                R.op("dve", "reciprocal", [r_psm], [r_rec], out=rec[:, 0:nq], in_=psm[:, 0:nq])
                R.op("dve", "tensor_tensor", [r_po, r_rec], [r_o32], out=o32[:, 0:nq], in0=po[:, 0:nq], in1=rec[:, 0:nq],
                     op=ALU.mult)
                R.op("dve", "scalar_tensor_tensor", [r_o32, r_a1, r_lam], [r_o32], out=o32[:, 0:nq], in0=o32[:, 0:nq],
                     scalar=lamt[:, 2:3], in1=a1[:, 0:nq], op0=ALU.mult, op1=ALU.add)
                R.op("act", "activation", [r_o32], [r_sqo], out=sqo[:, 0:nq], in_=o32[:, 0:nq], func=AF.Square)
                ps, rps = self.bank()
                R.mm(ps[:, 0:nq], self.onesb[:], sqo[:, 0:nq], True, True, [r_sqo, rc], [rps])
                R.op("act", "activation", [rps], [r_rstd], out=lnt[:, 0:nq], in_=ps[:, 0:nq], func=AF.Ln, scale=1.0 / 128,
                     bias=EPS)
                R.op("act", "activation", [r_rstd], [r_rstd], out=rstd[:, 0:nq], in_=lnt[:, 0:nq], func=AF.Exp, scale=-0.5)
                R.op("dve", "scalar_tensor_tensor", [r_o32, r_rstd, r_lam], [r_oT[h][q0 // 512]], out=oT[:, h, q0:q0 + nq],
                     in0=o32[:, 0:nq], scalar=gs[:, 0:1], in1=rstd[:, 0:nq], op0=ALU.mult, op1=ALU.mult)

            hq = h % 2
            for br in range(2):
                rows = slice(br * 64, (br + 1) * 64)
                self.copy(self.evac_engine(), qz[hq][br][rows, :], qT[rows, h, :], [r_qT[h]], [r_qz[hq][br]])
            for blk in blocks:
                for br, fin in ((0, fin1), (1, fin2)):
                    self.attn_core([(kT[:, h, :], qz[hq][br])], [r_kT[h], r_qz[hq][br]],
                                   lambda kt, h=h: v_tok[:, kt, h * 128:(h + 1) * 128], None, [blk], scale,
                                   negm[:, br:br + 1], r_negm, pT, r_pT, fin, v_res_fn=lambda kt: r_v[kt])

        self.attn_flush()

        Wo, rWo = self.wload(dr["diff_w_o"][0], 8, 1024)
        for half in range(2):
            tt = u * 2 + half
            ts = slice(tt * 512, (tt + 1) * 512)
            ls = slice(half * 512, (half + 1) * 512)
            for dc in range(8):
                ps, rps = self.bank()
                for cc in range(8):
                    R.mm(ps[:], Wo[:, cc, dc * 128:(dc + 1) * 128], oT[:, cc, ls], cc == 0, cc == 7, [rWo, r_oT[cc][half]],
                         [rps])
                R.op("dve", "scalar_tensor_tensor", [rps, self.r_mod, self.r_xT[dc][tt]], [self.r_xT[dc][tt]],
                     out=self.xT[:, dc, ts], in0=ps[:], scalar=self.mod[:, 16 + dc, u:u + 1],
                     in1=self.xT[:, dc, ts], op0=ALU.mult, op1=ALU.add)


def rope_tables(d):
    half = d // 2
    nf = half // 2
    t = np.arange(LS)
    pos_r = (t // 64).astype(np.float32)
    pos_c = (t % 64).astype(np.float32)
    inv = (10000.0 ** (-np.arange(nf, dtype=np.float32) / nf)).astype(np.float32)
    ang = np.concatenate([pos_r[:, None] * inv, pos_c[:, None] * inv], axis=-1).astype(np.float32)
    cos = np.cos(ang).astype(np.float32).T
    sin = np.sin(ang).astype(np.float32).T
    out = np.zeros((2, d, LS), np.float32)
    out[0, :half] = cos
    out[0, half:] = cos
    out[1, :half] = -sin
    out[1, half:] = sin
    return out


def make_in_maps(inputs):
    f = lambda a: np.ascontiguousarray(np.asarray(a, dtype=np.float32))
    shared = {}
    for name, shape in IN_SPECS[12:]:
        if name in ("rope_cs", "rope128"):
            continue
        shared[name] = f(inputs[name]).reshape(shape)
    shared["rope_cs"] = rope_tables(32)
    r64 = rope_tables(64)
    shared["rope128"] = np.ascontiguousarray(np.concatenate([r64, r64], axis=1))
    shared["c_ctx"] = f(inputs["c_ctx"])
    maps = []
    for c in range(N_CORES):
        s = c % 2
        m = dict(shared)
        m["xp"] = f(inputs["x_prompt"][c * NPS:(c + 1) * NPS])
        m["xs"] = f(inputs["x_sample"][s])
        m["st_ssd"] = f(inputs["state_ssd"][s, 0])
        m["c_ckv"] = f(inputs["cache_mla_ckv"][s, 0])
        m["c_kr"] = f(inputs["cache_mla_krope"][s, 0])
        m["st_C"] = f(inputs["state_mlstm_C"][s, 0])
        m["st_n"] = f(inputs["state_mlstm_n"][s, 0])
        m["st_m"] = f(inputs["state_mlstm_m"][s, 0])
        m["c_dk"] = f(inputs["cache_diff_k"][s, 0])
        m["c_dv"] = f(inputs["cache_diff_v"][s, 0])
        m["c_s"] = f(inputs["c"][s])
        maps.append(m)
    return maps


def assemble(results):
    cat = lambda k: np.concatenate([r[k] for r in results], axis=0)
    y_prompt = cat("yp")
    y_sample = np.stack([results[0]["ys"], results[1]["ys"]], axis=0)
    exp1 = lambda a: a[:, None]
    return (y_prompt, y_sample, exp1(cat("o_ssd")), exp1(cat("o_ckv")), exp1(cat("o_kr")),
            exp1(cat("o_C")), exp1(cat("o_n")), exp1(cat("o_m")), exp1(cat("o_dk")), exp1(cat("o_dv")))


_CACHE = {}


def kernel(**inputs):
    if "nc" not in _CACHE:
        b = Builder()
        _CACHE["nc"] = b.build()
    nc = _CACHE["nc"]
    maps = make_in_maps(inputs)
    res = run_bass_kernel_spmd(nc, maps, core_ids=list(range(N_CORES)))
    return assemble(res.results)
```

```python
import math
from contextlib import ExitStack

import numpy as np
import concourse.bass as bass
import concourse.mybir as mybir
from concourse.bass_utils import run_bass_kernel_spmd

F32 = mybir.dt.float32
BF16 = mybir.dt.bfloat16
ALU = mybir.AluOpType
AF = mybir.ActivationFunctionType
AX = mybir.AxisListType

ENGS = ("pe", "act", "dve", "pool", "sp")

D = 1024
NPS = 4
LP = 256
LS = 1024
TU = 1024
TOK = 2048
DFF = 4096
EPS = 1e-6
N_CORES = 8
SSD_BIG = 20000.0


class Res:
    __slots__ = ("name", "lw", "rd", "excl")

    def __init__(self, name="", excl=False):
        self.name = name
        self.lw = None
        self.rd = {}
        self.excl = excl


class Op:
    __slots__ = ("fn", "waits", "sig", "lane", "lane_k")

    def __init__(self, fn):
        self.fn = fn
        self.waits = []
        self.sig = False
        self.lane = None
        self.lane_k = 0


class Rec:
    def __init__(self, nc, n_lanes_sp=8, n_lanes_pool=4, n_lanes_act=2):
        self.nc = nc
        self.ops = {e: [] for e in ENGS}
        self.waited = {e: {} for e in ENGS}
        self.lane_cnt = {}
        self.lanes = {"sp": [("dma", "sp", i) for i in range(n_lanes_sp)],
                      "pool": [("dma", "pool", i) for i in range(n_lanes_pool)],
                      "act": [("dma", "act", i) for i in range(n_lanes_act)]}
        self.lane_rr = {"sp": 0, "pool": 0, "act": 0}
        for q in self.lanes:
            for l in self.lanes[q]:
                self.lane_cnt[l] = 0

    def emit(self, eng, fn, reads=(), writes=(), dma=False):
        deps = {}

        def add(tok):
            if tok is None:
                return
            key, idx = tok
            if deps.get(key, -1) < idx:
                deps[key] = idx

        for r in reads:
            add(r.lw)
            if r.excl:
                for k, i in r.rd.items():
                    if k != ("eng", eng):
                        add((k, i))
        for w in writes:
            add(w.lw)
            for k, i in w.rd.items():
                add((k, i))
        op = Op(fn)
        if dma:
            lanes = self.lanes[eng]
            lane = lanes[self.lane_rr[eng] % len(lanes)]
            self.lane_rr[eng] += 1
            k = self.lane_cnt[lane]
            if k > 0:
                add((lane, k))
            self.lane_cnt[lane] = k + 1
            op.lane = lane
            op.lane_k = k + 1
            tok = (lane, k + 1)
        else:
            tok = (("eng", eng), len(self.ops[eng]))
        wd = self.waited[eng]
        for key, idx in deps.items():
            if key == ("eng", "pe") and eng == "pe" and not dma:
                continue
            if wd.get(key, -1) >= idx:
                continue
            wd[key] = idx
            op.waits.append((key, idx))
            if key[0] == "eng":
                self.ops[key[1]][idx].sig = True
        for r in reads:
            if r.rd.get(tok[0], -1) < tok[1]:
                r.rd[tok[0]] = tok[1]
        for w in writes:
            w.lw = tok
            w.rd = {}
        self.ops[eng].append(op)
        return tok

    def op(self, eng, method, reads, writes, **kw):
        self.emit(eng, lambda e: getattr(e, method)(**kw), reads, writes)

    def mm(self, out, lhsT, rhs, start, stop, reads, writes):
        self.emit("pe", lambda e: e.matmul(out, lhsT, rhs, start=start, stop=stop), reads, writes)

    def tr(self, out, in_, ident, reads, writes):
        self.emit("pe", lambda e: e.transpose(out, in_, ident), reads, writes)

    def dma(self, q, out, in_, reads, writes, **kw):
        self.emit(q, lambda e: e.dma_start(out=out, in_=in_, **kw), reads, writes, dma=True)

    def replay(self, stack):
        nc = self.nc
        sems = {}
        for e in ENGS:
            sems[("eng", e)] = stack.enter_context(nc.semaphore("s_" + e))
        for q in self.lanes:
            for l in self.lanes[q]:
                if self.lane_cnt[l] > 0:
                    sems[l] = stack.enter_context(nc.semaphore("l_%s%d" % (l[1], l[2])))
        rank = {}
        for e in ENGS:
            c = 0
            r = []
            for op in self.ops[e]:
                if op.sig:
                    c += 1
                r.append(c)
            rank[e] = r

        def val(key, idx):
            if key[0] == "eng":
                return rank[key[1]][idx]
            return 16 * idx

        block = stack.enter_context(nc.Block())
        ops = self.ops
        lanes = self.lanes
        lane_cnt = self.lane_cnt

        def run(eng_name, eng):
            for op in ops[eng_name]:
                for key, idx in op.waits:
                    eng.wait_ge(sems[key], val(key, idx))
                inst = op.fn(eng)
                if op.lane is not None:
                    inst.then_inc(sems[op.lane], 16)
                elif op.sig:
                    inst.then_inc(sems[("eng", eng_name)], 1)
            if eng_name in lanes:
                for l in lanes[eng_name]:
                    if lane_cnt[l] > 0:
                        eng.wait_ge(sems[l], 16 * lane_cnt[l])

        @block.tensor
        def _(e):
            run("pe", e)

        @block.scalar
        def _(e):
            run("act", e)

        @block.vector
        def _(e):
            run("dve", e)

        @block.gpsimd
        def _(e):
            run("pool", e)

        @block.sync
        def _(e):
            run("sp", e)

    def stats(self):
        return {e: (len(self.ops[e]), sum(1 for o in self.ops[e] if o.sig),
                    sum(len(o.waits) for o in self.ops[e])) for e in ENGS}


IN_SPECS = [
    ("xp", [NPS, LP, D]), ("xs", [LS, D]),
    ("st_ssd", [2, 32, 64, 128]), ("c_ckv", [256, 256]), ("c_kr", [256, 32]),
    ("st_C", [2, 8, 128, 256]), ("st_n", [2, 8, 128]), ("st_m", [2, 8]),
    ("c_dk", [256, 8, 128]), ("c_dv", [256, 8, 128]),
    ("c_s", [D]), ("c_ctx", [D]),
    ("norm1_g", [4, D]), ("norm2_g", [4, D]), ("ada_w", [4, D, 6 * D]), ("ada_b", [4, 6 * D]),
    ("mlp_w1", [4, D, DFF]), ("mlp_w2", [4, DFF, D]), ("final_g", [D]),
    ("ssd_w_in", [1, D, 6208]), ("ssd_conv_w", [1, 5, 4096]), ("ssd_conv_b", [1, 4096]),
    ("ssd_dt_bias", [1, 2, 32]), ("ssd_A_log", [1, 2, 32]), ("ssd_D", [1, 32]),
    ("ssd_norm_g", [1, 2048]), ("ssd_w_out", [1, 2048, D]),
    ("mla_w_in", [1, D, 800]), ("mla_q_norm_g", [1, 512]), ("mla_kv_norm_g", [1, 256]),
    ("mla_w_uq", [1, 512, 1536]), ("mla_w_ukv", [1, 256, 2048]), ("mla_w_o", [1, 1024, D]),
    ("mlstm_w_up", [1, D, 4128]), ("mlstm_conv_w", [1, 5, 2048]), ("mlstm_conv_b", [1, 2048]),
    ("mlstm_gate_b", [1, 2, 2, 8]), ("mlstm_w_q", [1, 2048, 1024]), ("mlstm_w_k", [1, 2048, 1024]),
    ("mlstm_w_v", [1, 2048, 2048]), ("mlstm_skip", [1, 2048]), ("mlstm_norm_g", [1, 2048]),
    ("mlstm_w_down", [1, 2048, D]),
    ("diff_w_qkv", [1, D, 3 * D]), ("diff_lq1", [1, 64]), ("diff_lk1", [1, 64]),
    ("diff_lq2", [1, 64]), ("diff_lk2", [1, 64]), ("diff_subln_g", [1, 128]), ("diff_w_o", [1, D, D]),
    ("rope_cs", [2, 32, LS]), ("rope128", [2, 128, LS]),
]
OUT_SPECS = [
    ("yp", [NPS, LP, D]), ("ys", [LS, D]),
    ("o_ssd", [NPS, 2, 32, 64, 128]), ("o_ckv", [NPS, LP, 256]), ("o_kr", [NPS, LP, 32]),
    ("o_C", [NPS, 2, 8, 128, 256]), ("o_n", [NPS, 2, 8, 128]), ("o_m", [NPS, 2, 8]),
    ("o_dk", [NPS, LP, 8, 128]), ("o_dv", [NPS, LP, 8, 128]),
]

ARENA_BYTES = 104 * 1024


class Builder:
    def __init__(self, depth=4, mixers=(0, 1, 2, 3), dbg=()):
        self.depth = depth
        self.mixers = set(mixers)
        self.dbg = dbg
        self.nc = bass.Bass("TRN2", target_bir_lowering=False)
        self.R = Rec(self.nc)
        self.dram = {}
        self.dbg_out = {}

    def sb(self, name, shape, dt):
        return self.st.enter_context(self.nc.sbuf_tensor(name, shape, dt))

    def av(self, off, shape, dt):
        esz = 2 if dt == BF16 else 4
        n = 1
        for s in shape[1:]:
            n *= s
        nb = n * esz
        assert off % 4 == 0 and off + nb <= ARENA_BYTES, (off, nb)
        ap = self.arena[0:shape[0], off // 2: (off + nb) // 2]
        if dt != BF16:
            ap = ap.bitcast(dt)
        if len(shape) == 2:
            return ap
        names = " ".join("d%d" % i for i in range(1, len(shape)))
        kw = {"d%d" % i: shape[i] for i in range(1, len(shape))}
        return ap.rearrange("p (%s) -> p %s" % (names, names), **kw)

    def bank(self, hold=False):
        while True:
            b = self.bank_i % 8
            self.bank_i += 1
            if b not in self.held:
                break
        if hold:
            self.held.add(b)
        return self.ps[b], self.ps_res[b]

    def release(self, ps):
        for b in range(8):
            if self.ps[b] is ps:
                self.held.discard(b)

    def stage(self, new_res):
        R = self.R
        allr = list(self.arena_live) + list(new_res)
        d = self.dummy
        R.emit("dve", lambda e: e.memset(d[:, 0:1], 0.0), [], allr + [self.r_dummy])
        self.arena_live = list(new_res)

    def evac_engine(self):
        self.ev_i += 1
        return "act" if self.ev_i % 2 == 0 else "dve"

    def copy(self, eng, out, in_, reads, writes):
        if eng == "act":
            self.R.op("act", "activation", reads, writes, out=out, in_=in_, func=AF.Identity)
        else:
            self.R.op(eng, "tensor_copy", reads, writes, out=out, in_=in_)

    def wload(self, src_ap, kc_n, ncols):
        assert kc_n * ncols <= 8192
        i = self.wb_i % len(self.wb)
        self.wb_i += 1
        view = self.wb[i][:, 0:kc_n * ncols].rearrange("p (kc n) -> p kc n", kc=kc_n)
        self.R.dma("pool", view, src_ap.rearrange("(kc p) n -> p kc n", p=128), [], [self.wb_res[i]])
        return view, self.wb_res[i]

    def debug_dump(self, name, ap, res, shape, dt=F32):
        if name not in self.dbg:
            return
        t = self.nc.dram_tensor("dbg_" + name, list(shape), dt, kind="ExternalOutput").ap()
        self.dbg_out[name] = list(shape)
        self.R.dma("sp", t, ap, res, [])

    def build(self):
        nc = self.nc
        R = self.R
        for name, shape in IN_SPECS:
            self.dram[name] = nc.dram_tensor(name, list(shape), F32, kind="ExternalInput").ap()
        for name, shape in OUT_SPECS:
            self.dram[name] = nc.dram_tensor(name, list(shape), F32, kind="ExternalOutput").ap()
        with ExitStack() as st:
            self.st = st
            self.xT = self.sb("xT", [128, 8, TOK], F32)
            self.r_xT = [[Res("xT%d_%d" % (dc, tt)) for tt in range(4)] for dc in range(8)]
            self.wb = [self.sb("wb%d" % i, [128, 8192], BF16) for i in range(2)]
            self.wb_res = [Res("wb%d" % i) for i in range(2)]
            self.wb_i = 0
            self.arena = self.sb("arena", [128, ARENA_BYTES // 2], BF16)
            self.arena_live = []
            self.identf = self.sb("identf", [128, 128], F32)
            self.identb = self.sb("identb", [128, 128], BF16)
            self.onesb = self.sb("onesb", [128, 128], BF16)
            self.onesf = self.sb("onesf", [128, 128], F32)
            self.prm = self.sb("prm", [128, 640], F32)
            self.mod = self.sb("mod", [128, 48, 2], F32)
            self.scT = self.sb("scT", [128, 8, 2], BF16)
            self.dummy = self.sb("sdummy", [128, 2], F32)
            self.mle = self.sb("mle", [128, 128], F32)
            self.mge = self.sb("mge", [128, 128], F32)
            self.sm64 = self.sb("sm64", [64, 4], F32)
            self.dbc = self.sb("dbc", [128, 32], F32)
            self.r_const = Res("const")
            self.r_prm = Res("prm")
            self.r_mod = Res("mod")
            self.r_scT = Res("scT")
            self.r_dummy = Res("dummy")
            self.ps = [st.enter_context(nc.psum_tensor("ps%d" % i, [128, 512], F32)) for i in range(8)]
            self.ps_res = [Res("ps%d" % i, excl=True) for i in range(8)]
            self.bank_i = 0
            self.held = set()
            self.pT_i = 0
            self.pending_fin = None
            self.ev_i = 0

            self.setup()
            self.debug_dump("xT0", self.xT[:], [r for rr in self.r_xT for r in rr], [128, 8, TOK])
            self.debug_dump("prm", self.prm[:], [self.r_prm], [128, 640])
            self.debug_dump("scT", self.scT[:], [self.r_scT], [128, 8, 2], BF16)
            for i in range(self.depth):
                self.adaln(i)
                if i == 0:
                    self.debug_dump("mod0", self.mod[:], [self.r_mod], [128, 48, 2])
                kind = i % 4
                if kind in self.mixers:
                    for u in range(2):
                        self.norm_mod(i, 0, [u * 2, u * 2 + 1], local=True)
                        [self.ssd, self.mla, self.mlstm, self.diff][kind](i, u)
                self.norm_mod(i, 1, [0, 1, 2, 3])
                if i == 0:
                    self.debug_dump("hT0", self.hT, self.r_h, [128, 8, TOK], BF16)
                self.mlp(i)
                if i == 0:
                    self.debug_dump("xT1", self.xT[:], [r for rr in self.r_xT for r in rr], [128, 8, TOK])
            self.final()
            self.stats = R.stats()
            R.replay(st)
        return nc

    def setup(self):
        R = self.R
        nc = self.nc
        dr = self.dram
        identf, identb, onesb, onesf = self.identf, self.identb, self.onesb, self.onesf
        rc = self.r_const
        R.emit("pool", lambda e: e.memset(identf[:], 0.0), [], [rc])
        R.emit("pool", lambda e: e.affine_select(out=identf[:], in_=identf[:], pattern=[[-1, 128]],
                                                 compare_op=ALU.not_equal, fill=1.0, base=0,
                                                 channel_multiplier=1), [rc], [rc])
        R.emit("dve", lambda e: e.tensor_copy(out=identb[:], in_=identf[:]), [rc], [rc])
        R.emit("dve", lambda e: e.memset(onesb[:], 1.0), [rc], [rc])
        R.emit("dve", lambda e: e.memset(onesf[:], 1.0), [rc], [rc])
        R.op("pool", "affine_select", [rc], [rc], out=self.mle[:], in_=onesf[:], pattern=[[1, 128]],
             compare_op=ALU.is_ge, fill=0.0, base=0, channel_multiplier=-1)
        R.op("pool", "affine_select", [rc], [rc], out=self.mge[:], in_=onesf[:], pattern=[[-1, 128]],
             compare_op=ALU.is_ge, fill=0.0, base=0, channel_multiplier=1)
        R.dma("sp", self.sm64[:, 0:1], dr["ssd_dt_bias"][0].rearrange("d (h o) -> (d h) o", o=1), [], [rc])
        R.dma("sp", self.sm64[:, 1:2], dr["ssd_A_log"][0].rearrange("d (h o) -> (d h) o", o=1), [], [rc])
        R.dma("sp", self.dbc[:], dr["ssd_D"][0].partition_broadcast(128), [], [rc])
        R.op("act", "activation", [rc], [rc], out=self.sm64[:, 2:3], in_=self.sm64[:, 1:2], func=AF.Exp)
        R.op("dve", "tensor_scalar", [rc], [rc], out=self.sm64[:, 2:3], in0=self.sm64[:, 2:3], scalar1=-1.0,
             scalar2=None, op0=ALU.mult)
        self.stage([])

        rows = [
            ("c_ctx", dr["c_ctx"].rearrange("(c p) -> c p", p=128)),
            ("c_s", dr["c_s"].rearrange("(c p) -> c p", p=128)),
            ("norm1_g", dr["norm1_g"].rearrange("l (c p) -> (l c) p", p=128)),
            ("norm2_g", dr["norm2_g"].rearrange("l (c p) -> (l c) p", p=128)),
            ("final_g", dr["final_g"].rearrange("(c p) -> c p", p=128)),
            ("ada_b", dr["ada_b"].rearrange("l (c p) -> (l c) p", p=128)),
            ("ssd_conv_w", dr["ssd_conv_w"][0].rearrange("k (c p) -> (k c) p", p=128)),
            ("ssd_conv_b", dr["ssd_conv_b"][0].rearrange("(c p) -> c p", p=128)),
            ("ssd_norm_g", dr["ssd_norm_g"][0].rearrange("(c p) -> c p", p=128)),
            ("mla_q_norm_g", dr["mla_q_norm_g"][0].rearrange("(c p) -> c p", p=128)),
            ("mla_kv_norm_g", dr["mla_kv_norm_g"][0].rearrange("(c p) -> c p", p=128)),
            ("mlstm_conv_w", dr["mlstm_conv_w"][0].rearrange("k (c p) -> (k c) p", p=128)),
            ("mlstm_conv_b", dr["mlstm_conv_b"][0].rearrange("(c p) -> c p", p=128)),
            ("mlstm_skip", dr["mlstm_skip"][0].rearrange("(c p) -> c p", p=128)),
            ("mlstm_norm_g", dr["mlstm_norm_g"][0].rearrange("(c p) -> c p", p=128)),
            ("diff_subln_g", dr["diff_subln_g"][0].rearrange("(c p) -> c p", p=128)),
        ]
        self.pcol = {}
        off = 0
        for name, ap in rows:
            self.pcol[name] = off
            off += ap.shape[0]
        total = off
        assert total <= 640
        ntile = (total + 127) // 128
        stg = [self.av(i * 512, [128, 128], F32) for i in range(ntile)]
        r_stg = [Res("stg%d" % i) for i in range(ntile)]
        self.stage(r_stg)
        for i in range(ntile):
            R.emit("dve", lambda e, i=i: e.memset(stg[i], 0.0), [], [r_stg[i]])
        for name, ap in rows:
            o = self.pcol[name]
            n = ap.shape[0]
            s = 0
            while s < n:
                t = (o + s) // 128
                p0 = (o + s) % 128
                m = min(n - s, 128 - p0)
                R.dma("sp", stg[t][p0:p0 + m, :], ap[s:s + m, :], [], [r_stg[t]])
                s += m
        for i in range(ntile):
            ps, rps = self.bank()
            R.tr(ps[:, 0:128], stg[i], identf[:], [r_stg[i], rc], [rps])
            w = min(128, total - i * 128)
            R.emit("dve", lambda e, i=i, ps=ps, w=w: e.tensor_copy(out=self.prm[:, i * 128:i * 128 + w],
                                                                  in_=ps[:, 0:w]), [rps], [self.r_prm])
        pc = self.pcol
        for j, nm in enumerate(["c_ctx", "c_s"]):
            R.emit("act", lambda e, j=j, nm=nm: e.activation(out=self.scT[:, :, j],
                                                            in_=self.prm[:, pc[nm]:pc[nm] + 8], func=AF.Silu),
                   [self.r_prm], [self.r_scT])

        nst = 4
        xst = [self.av(4096 + i * 4096, [128, D], F32) for i in range(nst)]
        r_xst = [Res("xst%d" % i) for i in range(nst)]
        self.stage(r_xst)
        for tt in range(16):
            if tt < 8:
                src = dr["xp"][tt // 2, (tt % 2) * 128:(tt % 2 + 1) * 128, :]
            else:
                src = dr["xs"][(tt - 8) * 128:(tt - 7) * 128, :]
            s = tt % nst
            R.dma("sp", xst[s], src, [], [r_xst[s]])
            for half in range(2):
                ps, rps = self.bank()
                for j in range(4):
                    dc = half * 4 + j
                    R.tr(ps[:, j * 128:(j + 1) * 128], xst[s][:, dc * 128:(dc + 1) * 128], identf[:],
                         [r_xst[s], rc], [rps])
                self.copy(self.evac_engine(), self.xT[:, half * 4:(half + 1) * 4, tt * 128:(tt + 1) * 128],
                          ps[:].rearrange("p (j t) -> p j t", j=4), [rps],
                          [self.r_xT[half * 4 + j][tt // 4] for j in range(4)])

    def adaln(self, i):
        R = self.R
        dr = self.dram
        pc = self.pcol
        ps, rps = self.bank(hold=True)
        for blk in range(6):
            wv, rw = self.wload(dr["ada_w"][i][:, blk * 1024:(blk + 1) * 1024], 8, 1024)
            for f in range(8):
                fc = blk * 8 + f
                for kc in range(8):
                    R.mm(ps[:, fc * 2:fc * 2 + 2], wv[:, kc, f * 128:(f + 1) * 128], self.scT[:, kc, :],
                         kc == 0, kc == 7, [rw, self.r_scT], [rps])
        mod = self.mod
        ab = self.prm[:, pc["ada_b"] + i * 48: pc["ada_b"] + (i + 1) * 48]
        R.emit("dve", lambda e: e.tensor_tensor(out=mod[:], in0=ps[:, 0:96].rearrange("p (c u) -> p c u", u=2),
                                                in1=ab.unsqueeze(2).to_broadcast([128, 48, 2]), op=ALU.add),
               [rps, self.r_prm], [self.r_mod])
        self.release(ps)
        for (sc0, gname) in ((8, "norm1_g"), (32, "norm2_g")):
            g = self.prm[:, pc[gname] + i * 8: pc[gname] + (i + 1) * 8]
            R.emit("dve", lambda e, sc0=sc0: e.tensor_scalar(out=mod[:, sc0:sc0 + 8, :], in0=mod[:, sc0:sc0 + 8, :],
                                                             scalar1=1.0, scalar2=None, op0=ALU.add),
                   [self.r_mod], [self.r_mod])
            R.emit("dve", lambda e, sc0=sc0, g=g: e.tensor_tensor(out=mod[:, sc0:sc0 + 8, :],
                                                                  in0=mod[:, sc0:sc0 + 8, :],
                                                                  in1=g.unsqueeze(2).to_broadcast([128, 8, 2]),
                                                                  op=ALU.mult),
                   [self.r_mod, self.r_prm], [self.r_mod])

    def rstd_tile(self, tt, sq, r_sq, lnt, rstd, r_tmp):
        R = self.R
        ts = slice(tt * 512, (tt + 1) * 512)
        R.op("act", "activation", [self.r_xT[dc][tt] for dc in range(8)], [r_sq],
             out=sq, in_=self.xT[:, :, ts], func=AF.Square)
        ps, rps = self.bank()
        for dc in range(8):
            R.mm(ps[:], self.onesb[:], sq[:, dc, :], dc == 0, dc == 7, [r_sq, self.r_const], [rps])
        R.op("act", "activation", [rps, self.r_const], [r_tmp],
             out=lnt, in_=ps[:], func=AF.Ln, scale=1.0 / D, bias=EPS)
        R.op("act", "activation", [r_tmp], [r_tmp], out=rstd, in_=lnt, func=AF.Exp, scale=-0.5)

    def norm_mod(self, i, which, tts, local=False):
        R = self.R
        if local:
            hT = self.av(0, [128, 8, TU], BF16)
        else:
            hT = self.av(0, [128, 8, TOK], BF16)
        r_h = [Res("hT%d" % tt) for tt in range(4)]
        base = 65536
        sq = [self.av(base + k * 8192, [128, 8, 512], BF16) for k in range(2)]
        lnt = [self.av(base + 16384 + k * 2048, [128, 512], F32) for k in range(2)]
        rstd = [self.av(base + 20480 + k * 2048, [128, 512], F32) for k in range(2)]
        tmp = [self.av(base + 24576 + k * 2048, [128, 512], F32) for k in range(2)]
        r_sq = [Res("sq0"), Res("sq1")]
        r_tmp = [Res("nt0"), Res("nt1")]
        r_t2 = [Res("t2a"), Res("t2b")]
        self.stage(r_h + r_sq + r_tmp + r_t2)
        self.hT, self.r_h = hT, r_h
        sh0 = 0 if which == 0 else 24
        sc0 = 8 if which == 0 else 32

        def stats(j):
            k = j % 2
            self.rstd_tile(tts[j], sq[k], r_sq[k], lnt[k], rstd[k], r_tmp[k])

        stats(0)
        for j, tt in enumerate(tts):
            if j + 1 < len(tts):
                stats(j + 1)
            kk = j % 2
            u = tt // 2
            ts = slice(tt * 512, (tt + 1) * 512)
            for dc in range(8):
                k = dc % 2
                R.op("dve", "tensor_tensor", [self.r_xT[dc][tt], r_tmp[kk]], [r_t2[k]],
                     out=tmp[k], in0=self.xT[:, dc, ts], in1=rstd[kk], op=ALU.mult)
                ots = slice((tt % 2) * 512, (tt % 2 + 1) * 512) if local else ts
                R.op("act", "activation", [r_t2[k], self.r_mod], [r_h[tt]],
                     out=hT[:, dc, ots], in_=tmp[k], func=AF.Identity,
                     scale=self.mod[:, sc0 + dc, u:u + 1], bias=self.mod[:, sh0 + dc, u:u + 1])

    def mlp(self, i):
        R = self.R
        dr = self.dram
        hT, r_h = self.hT, self.r_h
        hid = self.av(32768, [128, 8, TOK], BF16)
        r_hid = [[Res("hid%d_%d" % (f, tt)) for tt in range(4)] for f in range(8)]
        rl = [self.av(65536 + k * 2048, [128, 512], F32) for k in range(2)]
        r_rl = [Res("rl0"), Res("rl1")]
        self.stage(r_h + [r for rr in r_hid for r in rr] + r_rl)
        k = 0
        for fg in range(4):
            w1, rw1 = self.wload(dr["mlp_w1"][i][:, fg * 1024:(fg + 1) * 1024], 8, 1024)
            for f in range(8):
                for tt in range(4):
                    ts = slice(tt * 512, (tt + 1) * 512)
                    ps, rps = self.bank()
                    for kc in range(8):
                        R.mm(ps[:], w1[:, kc, f * 128:(f + 1) * 128], hT[:, kc, ts], kc == 0, kc == 7,
                             [rw1, r_h[tt]], [rps])
                    kk = k % 2
                    k += 1
                    R.op("act", "activation", [rps], [r_rl[kk]], out=rl[kk], in_=ps[:], func=AF.Relu)
                    R.op("dve", "tensor_tensor", [r_rl[kk]], [r_hid[f][tt]],
                         out=hid[:, f, ts], in0=rl[kk], in1=rl[kk], op=ALU.mult)
            w2, rw2 = self.wload(dr["mlp_w2"][i][fg * 1024:(fg + 1) * 1024, :], 8, 1024)
            for tt in range(4):
                u = tt // 2
                ts = slice(tt * 512, (tt + 1) * 512)
                for dc in range(8):
                    ps, rps = self.bank()
                    for f in range(8):
                        R.mm(ps[:], w2[:, f, dc * 128:(dc + 1) * 128], hid[:, f, ts], f == 0, f == 7,
                             [rw2, r_hid[f][tt]], [rps])
                    R.op("dve", "scalar_tensor_tensor", [rps, self.r_mod, self.r_xT[dc][tt]], [self.r_xT[dc][tt]],
                         out=self.xT[:, dc, ts], in0=ps[:], scalar=self.mod[:, 40 + dc, u:u + 1],
                         in1=self.xT[:, dc, ts], op0=ALU.mult, op1=ALU.add)

    def final(self):
        R = self.R
        dr = self.dram
        pc = self.pcol
        sq = self.av(65536, [128, 8, 512], BF16)
        lnt = self.av(65536 + 8192, [128, 512], F32)
        rstd = self.av(65536 + 8192 + 2048, [128, 512], F32)
        yT = self.av(0, [128, 8, 512], F32)
        yo = [self.av(16384 + k * 4096, [128, D], F32) for k in range(4)]
        r_sq, r_tmp = Res("sq"), Res("nt")
        r_yT = [Res("yT%d" % dc) for dc in range(8)]
        r_yo = [Res("yo%d" % k) for k in range(4)]
        self.stage([r_sq, r_tmp] + r_yT + r_yo)
        ko = 0
        for tt in range(4):
            ts = slice(tt * 512, (tt + 1) * 512)
            self.rstd_tile(tt, sq, r_sq, lnt, rstd, r_tmp)
            for dc in range(8):
                g = self.prm[:, pc["final_g"] + dc: pc["final_g"] + dc + 1]
                R.op("dve", "scalar_tensor_tensor", [self.r_xT[dc][tt], r_tmp, self.r_prm], [r_yT[dc]],
                     out=yT[:, dc, :], in0=self.xT[:, dc, ts], scalar=g, in1=rstd, op0=ALU.mult, op1=ALU.mult)
            for n in range(4):
                k = ko % 4
                ko += 1
                for half in range(2):
                    ps, rps = self.bank()
                    for j in range(4):
                        dc = half * 4 + j
                        R.tr(ps[:, j * 128:(j + 1) * 128], yT[:, dc, n * 128:(n + 1) * 128], self.identf[:],
                             [r_yT[dc], self.r_const], [rps])
                    self.copy(self.evac_engine(), yo[k][:, half * 512:(half + 1) * 512], ps[:], [rps], [r_yo[k]])
                tok0 = tt * 512 + n * 128
                if tok0 < TU:
                    dst = dr["yp"][tok0 // LP, (tok0 % LP):(tok0 % LP) + 128, :]
                else:
                    dst = dr["ys"][tok0 - TU: tok0 - TU + 128, :]
                R.dma("sp", dst, yo[k], [r_yo[k]], [])

    class _Alloc:
        def __init__(self, b, off):
            self.b = b
            self.off = off

        def __call__(self, shape, dt):
            esz = 2 if dt == BF16 else 4
            n = 1
            for x in shape[1:]:
                n *= x
            nb = (n * esz + 3) // 4 * 4
            v = self.b.av(self.off, shape, dt)
            self.off += nb
            return v

    def wload_multi(self, parts, kc_n):
        tot = sum(p.shape[1] for p in parts)
        assert kc_n * tot <= 8192
        i = self.wb_i % len(self.wb)
        self.wb_i += 1
        view = self.wb[i][:, 0:kc_n * tot].rearrange("p (kc n) -> p kc n", kc=kc_n)
        c0 = 0
        for p in parts:
            n = p.shape[1]
            self.R.dma("pool", view[:, :, c0:c0 + n], p.rearrange("(kc p) n -> p kc n", p=128), [], [self.wb_res[i]])
            c0 += n
        return view, self.wb_res[i]

    def ssd(self, i, u):
        R = self.R
        dr = self.dram
        pc = self.pcol
        hT, r_h = self.hT, self.r_h
        r_hl = [r_h[u * 2], r_h[u * 2 + 1]]
        nseq, L = (NPS, LP) if u == 0 else (1, LS)
        cps = L // 128
        w_in = dr["ssd_w_in"][0]
        identf, identb, onesf = self.identf, self.identb, self.onesf
        rc = self.r_const
        bc = lambda ap, shape: ap.to_broadcast(shape)

        A = self._Alloc(self, 16384)
        E_tok = A([128, 8, 64], F32)
        lnEb = A([128, 8, 64], F32)
        dO = A([128, 8, 64], F32)
        wS = A([128, 8, 64], F32)
        dec = A([128, 8, 64], F32)
        ss = A([128, 8, 8], F32)
        yg = A([128, 8, 2048], BF16)
        mark = A.off
        r_tokq, r_dec, r_ss = Res("tokq"), Res("dec"), Res("ss")
        r_yg = [Res("yg%d" % c) for c in range(8)]
        persist = [r_tokq, r_dec, r_ss] + r_yg

        dtT = A([64, 1024], F32)
        xx = A([64, 1024], F32)
        t1 = A([64, 1024], F32)
        t2 = A([64, 1024], F32)
        ET = A([64, 1024], F32)
        cm = A([64, 1024], F32)
        xd = A([64, 8, 64], F32)
        r_a = Res("ssdA")
        self.stage(r_hl + persist + [r_a])
        wdt, rwdt = self.wload(w_in[:, 6144:6208], 8, 64)
        for tt in range(2):
            ps, rps = self.bank()
            for kc in range(8):
                R.mm(ps[0:64, :], wdt[:, kc, :], hT[:, kc, tt * 512:(tt + 1) * 512], kc == 0, kc == 7,
                     [rwdt, r_hl[tt]], [rps])
            R.op("dve", "tensor_scalar", [rps, rc], [r_a], out=xx[:, tt * 512:(tt + 1) * 512], in0=ps[0:64, :],
                 scalar1=self.sm64[:, 0:1], scalar2=None, op0=ALU.add)
        RA = [r_a]
        R.op("act", "activation", RA, RA, out=t1, in_=xx, func=AF.Abs)
        R.op("act", "activation", RA, RA, out=t1, in_=t1, func=AF.Exp, scale=-1.0)
        R.op("act", "activation", RA, RA, out=t1, in_=t1, func=AF.Ln, bias=1.0)
        R.op("dve", "scalar_tensor_tensor", RA, RA, out=dtT, in0=xx, scalar=0.0, in1=t1, op0=ALU.max, op1=ALU.add)
        R.op("act", "activation", RA, RA, out=t2, in_=dtT, func=AF.Ln)
        R.op("dve", "tensor_scalar", RA + [rc], RA, out=xx, in0=dtT, scalar1=self.sm64[:, 2:3], scalar2=None,
             op0=ALU.mult)
        cm3 = cm.rearrange("p (c t) -> p c t", t=128)
        R.op("dve", "memset", [], RA, ap=cm, constant=1.0)
        R.op("dve", "memset", RA, RA, ap=cm3[:, :, 0:1], constant=0.0)
        R.op("dve", "tensor_tensor_scan", RA, RA, out=t1, data0=cm, data1=xx, initial=0.0, op0=ALU.mult, op1=ALU.add)
        cum3 = t1.rearrange("p (c t) -> p c t", t=128)
        tot = cum3[:, :, 127:128]
        ET3 = ET.rearrange("p (c t) -> p c t", t=128)
        xx3 = xx.rearrange("p (c t) -> p c t", t=128)
        R.op("dve", "tensor_copy", RA, RA, out=ET[0:32, :], in_=t1[0:32, :])
        R.op("dve", "tensor_tensor", RA, RA, out=ET3[32:64], in0=bc(tot[32:64], [32, 8, 128]), in1=cum3[32:64],
             op=ALU.subtract)
        R.op("dve", "tensor_tensor", RA, RA, out=ET[32:64, :], in0=ET[32:64, :], in1=xx[32:64, :], op=ALU.add)
        R.op("dve", "tensor_tensor", RA + [rc], RA, out=xd, in0=bc(identf[0:64, 0:64].unsqueeze(1), [64, 8, 64]),
             in1=bc(tot, [64, 8, 64]), op=ALU.mult)
        ps, rps = self.bank()
        R.mm(ps[:, 0:512], onesf[0:64, :], xd.rearrange("p c h -> p (c h)"), True, True, RA + [rc], [rps])
        R.op("act", "activation", [rps], [r_dec], out=dec.rearrange("p c h -> p (c h)"), in_=ps[:, 0:512], func=AF.Exp)
        cmv = cm.rearrange("p (c t) -> p c t", t=128)
        R.op("dve", "tensor_tensor", RA, RA, out=cmv, in0=bc(tot, [64, 8, 128]), in1=ET3, op=ALU.subtract)
        R.op("dve", "tensor_tensor", RA, RA, out=cm, in0=cm, in1=t2, op=ALU.add)
        R.op("dve", "tensor_tensor", RA, RA, out=xx, in0=t2, in1=ET, op=ALU.subtract)
        for c in range(8):
            cs = slice(c * 128, (c + 1) * 128)
            ps, rps = self.bank()
            R.tr(ps[:, 0:64], ET[:, cs], identf[0:64, 0:64], RA + [rc], [rps])
            R.tr(ps[:, 64:128], xx[:, cs], identf[0:64, 0:64], RA + [rc], [rps])
            R.tr(ps[:, 128:192], cm[:, cs], identf[0:64, 0:64], RA + [rc], [rps])
            R.op("dve", "tensor_scalar", [rps], [r_tokq], out=E_tok[:, c, :], in0=ps[:, 0:64], scalar1=SSD_BIG, scalar2=None,
                 op0=ALU.add)
            R.op("dve", "tensor_scalar", [rps], [r_tokq], out=lnEb[:, c, :], in0=ps[:, 64:128], scalar1=-SSD_BIG,
                 scalar2=None, op0=ALU.add)
            R.op("act", "activation", [rps], [r_tokq], out=dO[:, c, :], in_=ps[:, 0:64], func=AF.Exp)
            R.op("act", "activation", [rps], [r_tokq], out=wS[:, c, :], in_=ps[:, 128:192], func=AF.Exp)
        R.op("dve", "memset", [], [r_ss], ap=ss, constant=0.0)

        A.off = mark
        pre = A([128, nseq, L + 4], F32)
        acc = A([128, 1024], F32)
        xsT = A([128, 2, 1024], BF16)
        BT = A([128, 1024], BF16)
        CT = A([128, 1024], BF16)
        xb_tok = A([128, 8, 384], BF16)
        Tbf = A([128, 8, 256], BF16)
        CBm = [A([128, 128], F32) for _ in range(2)]
        X4 = [A([128, 4, 128], F32), xsT[:, 0, :].bitcast(F32).rearrange("p (h t) -> p h t", h=4)]
        L4 = [A([128, 4, 128], F32), xsT[:, 1, :].bitcast(F32).rearrange("p (h t) -> p h t", h=4)]
        M4 = [[A([128, 4, 128], BF16) for _ in range(2)] for _ in range(2)]
        xw = [A([128, 4, 64], BF16) for _ in range(2)]
        Sst = [A([128, 4, 64], F32) for _ in range(2)]
        Sbf = A([128, 256], BF16)
        zs = [A([128, 256], BF16) for _ in range(2)]
        xD = [A([128, 4, 64], BF16) for _ in range(2)]
        tc1 = A([128, 4, 64], F32)
        tc2 = A([128, 4, 64], F32)
        ost = A([128, 2, 128], F32)
        sqj = A([128, 256], BF16)
        r_pre, r_acc = Res("pre"), Res("acc")
        r_fm = [Res("xsT0"), Res("xsT1"), Res("BT"), Res("CT")]
        r_xb = [Res("xb%d" % c) for c in range(8)]
        r_Tbf = [Res("Tbf%d" % c) for c in range(8)]
        r_CB = [Res("CB0"), Res("CB1")]
        r_X4 = [Res("X4"), r_fm[0]]
        r_L4 = [Res("L4"), r_fm[1]]
        r_zs2 = [Res("zs0"), Res("zs1")]
        r_xD2 = [Res("xD0"), Res("xD1")]
        r_M4 = [[Res("M4a0"), Res("M4b0")], [Res("M4a1"), Res("M4b1")]]
        r_xw = [Res("xwa"), Res("xwb")]
        r_S = [Res("Sf"), Res("Sb")]
        r_Sbf, r_zs, r_xD, r_tc1, r_tc2, r_ost, r_sqj = (Res("Sbf"), Res("zs"), Res("xD"), Res("tc1"), Res("tc2"),
                                                         Res("ost"), Res("sqj"))
        grp_res = ([r_pre, r_acc] + r_fm + r_xb + r_Tbf + r_CB + [r_X4[0], r_L4[0]] + r_M4[0] + r_M4[1] + r_xw + r_S + r_zs2 + r_xD2 +
                   [r_Sbf, r_zs, r_xD, r_tc1, r_tc2, r_ost, r_sqj])
        self.stage(r_hl + persist + grp_res)
        R.op("dve", "memset", [], [r_pre], ap=pre, constant=0.0)
        fm_dst = [xsT[:, 0, :], xsT[:, 1, :], BT, CT]

        def gweights(g):
            return self.wload_multi([w_in[:, g * 256:(g + 1) * 256], w_in[:, 2048 + g * 256:2048 + (g + 1) * 256],
                                     w_in[:, 4096 + g * 128:4096 + (g + 1) * 128],
                                     w_in[:, 5120 + g * 128:5120 + (g + 1) * 128]], 8)

        def make_xw(c, d, g, k):
            h0 = d * 32 + g * 4
            R.op("dve", "tensor_tensor", [r_xb[c], r_tokq], [r_xw[k]], out=xw[k],
                 in0=xb_tok[:, c, 0:256].rearrange("p (h q) -> p h q", h=4),
                 in1=bc(wS[:, c, h0:h0 + 4].unsqueeze(2), [128, 4, 64]), op=ALU.mult)

        def state_update(c, d, g, kx=None):
            h0 = d * 32 + g * 4
            k = d if kx is None else kx
            if kx is None:
                make_xw(c, d, g, k)
            ps, rps = self.bank()
            R.mm(ps[:, 0:256], xb_tok[:, c, 256:384], xw[k].rearrange("p h q -> p (h q)"), True, True,
                 [r_xb[c], r_xw[k]], [rps])
            R.op("dve", "tensor_tensor", [r_S[d], r_dec], [r_S[d]], out=Sst[d], in0=Sst[d],
                 in1=bc(dec[:, c, h0:h0 + 4].unsqueeze(2), [128, 4, 64]), op=ALU.mult)
            R.op("dve", "tensor_tensor", [r_S[d], rps], [r_S[d]], out=Sst[d],
                 in0=ps[:, 0:256].rearrange("p (h q) -> p h q", h=4), in1=Sst[d], op=ALU.add)

        def state_init(d, g):
            if u == 0:
                R.op("dve", "memset", [], [r_S[d]], ap=Sst[d], constant=0.0)
            else:
                src = dr["st_ssd"][d, g * 4:(g + 1) * 4].rearrange("(blk h2) p n -> (h2 p) blk n", blk=2)
                R.dma("sp", ost, src, [], [r_ost])
                ps, rps = self.bank()
                for blk in range(2):
                    R.tr(ps[:, blk * 128:(blk + 1) * 128], ost[:, blk, :], identf[:], [r_ost, rc], [rps])
                R.op("dve", "tensor_copy", [rps], [r_S[d]], out=Sst[d].rearrange("p h q -> p (h q)"), in_=ps[:, 0:256])

        def state_out(seq, d, g):
            ps, rps = self.bank()
            S2 = Sst[d].rearrange("p h q -> p (h q)")
            for blk in range(2):
                R.tr(ps[:, blk * 128:(blk + 1) * 128], S2[:, blk * 128:(blk + 1) * 128], identf[:], [r_S[d], rc], [rps])
            R.op("dve", "tensor_copy", [rps], [r_ost], out=ost.rearrange("p b n -> p (b n)"), in_=ps[:, 0:256])
            dst = dr["o_ssd"][seq, d, g * 4:(g + 1) * 4].rearrange("(blk h2) p n -> (h2 p) blk n", blk=2)
            R.dma("sp", dst, ost, [r_ost], [])

        wnext = gweights(0)
        for g in range(8):
            W, rW = wnext
            if g < 7:
                wnext = gweights(g + 1)
            for ci, (col0, cch) in enumerate([(256, g * 2), (384, g * 2 + 1), (512, 16 + g), (640, 24 + g)]):
                for tt in range(2):
                    ps, rps = self.bank()
                    for kc in range(8):
                        R.mm(ps[:], W[:, kc, col0:col0 + 128], hT[:, kc, tt * 512:(tt + 1) * 512], kc == 0, kc == 7,
                             [rW, r_hl[tt]], [rps])
                    if u == 0:
                        self.copy("act", pre[:, 2 * tt:2 * tt + 2, 2:2 + L],
                                  ps[:].rearrange("p (s t) -> p s t", s=2), [rps], [r_pre])
                    else:
                        self.copy("act", pre[:, 0, 2 + tt * 512:2 + (tt + 1) * 512], ps[:], [rps], [r_pre])
                acc3 = acc.rearrange("p (s t) -> p s t", s=nseq)
                cw = pc["ssd_conv_w"]
                R.op("dve", "tensor_scalar", [r_pre, self.r_prm], [r_acc], out=acc3, in0=pre[:, :, 0:L],
                     scalar1=self.prm[:, cw + cch:cw + cch + 1],
                     scalar2=self.prm[:, pc["ssd_conv_b"] + cch:pc["ssd_conv_b"] + cch + 1], op0=ALU.mult, op1=ALU.add)
                for k in range(1, 5):
                    R.op("dve", "scalar_tensor_tensor", [r_pre, self.r_prm, r_acc], [r_acc], out=acc3,
                         in0=pre[:, :, k:k + L], scalar=self.prm[:, cw + k * 32 + cch:cw + k * 32 + cch + 1],
                         in1=acc3, op0=ALU.mult, op1=ALU.add)
                R.op("act", "activation", [r_acc], [r_fm[ci]], out=fm_dst[ci], in_=acc, func=AF.Silu)
            for c in range(8):
                cs = slice(c * 128, (c + 1) * 128)
                ps, rps = self.bank()
                psb = ps[:].bitcast(BF16)
                R.tr(psb[:, 0:128], xsT[:, 0, cs], identb[:], [r_fm[0], rc], [rps])
                R.tr(psb[:, 128:256], xsT[:, 1, cs], identb[:], [r_fm[1], rc], [rps])
                R.tr(psb[:, 256:384], BT[:, cs], identb[:], [r_fm[2], rc], [rps])
                self.copy("act", xb_tok[:, c, :], psb[:, 0:384], [rps], [r_xb[c]])
            for seq in range(nseq):
                state_init(1, g)
                for c in reversed(range(seq * cps, (seq + 1) * cps)):
                    R.op("act", "activation", [r_S[1]], [r_Tbf[c]], out=Tbf[:, c, :],
                         in_=Sst[1].rearrange("p h q -> p (h q)"), func=AF.Identity)
                    if u == 0 or c > seq * cps:
                        state_update(c, 1, g)
                if u == 0:
                    state_out(seq, 1, g)
            def front(c):
                cs = slice(c * 128, (c + 1) * 128)
                pp = c % 2
                psz, rpsz = self.bank()
                for kc in range(8):
                    R.mm(psz[:, 0:256], hT[:, kc, cs], W[:, kc, 0:256], kc == 0, kc == 7, [rW, r_hl[c // 4]], [rpsz])
                R.op("act", "activation", [rpsz], [r_zs2[pp]], out=zs[pp], in_=psz[:, 0:256], func=AF.Tanh, scale=0.5)
                R.op("dve", "scalar_tensor_tensor", [rpsz, r_zs2[pp]], [r_zs2[pp]], out=zs[pp], in0=zs[pp], scalar=1.0,
                     in1=psz[:, 0:256], op0=ALU.add, op1=ALU.mult)
                pcb, rpcb = self.bank()
                R.mm(pcb[:, 0:128], BT[:, cs], CT[:, cs], True, True, [r_fm[2], r_fm[3]], [rpcb])
                R.op("act", "activation", [rpcb], [r_CB[pp]], out=CBm[pp], in_=pcb[:, 0:128], func=AF.Identity)
                R.op("dve", "tensor_tensor", [r_xb[c], rc], [r_xD2[pp]], out=xD[pp],
                     in0=xb_tok[:, c, 0:256].rearrange("p (h q) -> p h q", h=4),
                     in1=bc(self.dbc[:, g * 4:g * 4 + 4].unsqueeze(2), [128, 4, 64]), op=ALU.mult)
                for d in range(2):
                    h0 = d * 32 + g * 4
                    R.op("pool", "tensor_tensor", [rc, r_tokq], [r_X4[d]], out=X4[d],
                         in0=bc(identf[:].unsqueeze(1), [128, 4, 128]),
                         in1=bc(E_tok[:, c, h0:h0 + 4].unsqueeze(2), [128, 4, 128]), op=ALU.mult)
                prs = []
                for d in range(2):
                    pr, rpr = self.bank()
                    msk = self.mge if d == 0 else self.mle
                    R.mm(pr[:, 0:512], msk[:], X4[d].rearrange("p h t -> p (h t)"), True, True, [r_X4[d], rc], [rpr])
                    prs.append((pr, rpr))
                for d in range(2):
                    h0 = d * 32 + g * 4
                    pr, rpr = prs[d]
                    for hh in range(4):
                        R.op("act", "activation", [rpr, r_tokq], [r_L4[d]], out=L4[d][:, hh, :],
                             in_=pr[:, hh * 128:(hh + 1) * 128], func=AF.Exp, bias=lnEb[:, c, h0 + hh:h0 + hh + 1])

            def needs_upd(c):
                return u == 0 or c < (c // cps + 1) * cps - 1

            def front2(c):
                pp = c % 2
                for d in range(2):
                    R.op("dve", "tensor_tensor", [r_L4[d], r_CB[pp]], [r_M4[pp][d]], out=M4[pp][d], in0=L4[d],
                         in1=bc(CBm[pp].unsqueeze(1), [128, 4, 128]), op=ALU.mult)
                if needs_upd(c):
                    make_xw(c, 0, g, pp)

            def back(seq, c):
                cs = slice(c * 128, (c + 1) * 128)
                pp = c % 2
                if c == seq * cps:
                    state_init(0, g)
                R.op("act", "activation", [r_S[0]], [r_Sbf], out=Sbf, in_=Sst[0].rearrange("p h q -> p (h q)"),
                     func=AF.Identity)
                py, rpy = self.bank(hold=True)
                R.mm(py[:, 0:256], identb[:], xD[pp].rearrange("p h q -> p (h q)"), True, False, [r_xD2[pp], rc], [rpy])
                for d in range(2):
                    for hh in range(4):
                        R.mm(py[:, hh * 64:(hh + 1) * 64], M4[pp][d][:, hh, :], xb_tok[:, c, hh * 64:(hh + 1) * 64],
                             False, d == 1 and hh == 3, [r_M4[pp][d], r_xb[c]], [rpy])
                pz, rpz = self.bank()
                R.mm(pz[:, 0:256], CT[:, cs], Sbf, True, True, [r_fm[3], r_Sbf], [rpz])
                R.mm(pz[:, 256:512], CT[:, cs], Tbf[:, c, :], True, True, [r_fm[3], r_Tbf[c]], [rpz])
                h0 = g * 4
                R.op("dve", "tensor_tensor", [rpz, r_tokq], [r_tc1], out=tc1,
                     in0=pz[:, 0:256].rearrange("p (h q) -> p h q", h=4),
                     in1=bc(dO[:, c, h0:h0 + 4].unsqueeze(2), [128, 4, 64]), op=ALU.mult)
                R.op("dve", "tensor_tensor", [rpz, r_tokq], [r_tc2], out=tc2,
                     in0=pz[:, 256:512].rearrange("p (h q) -> p h q", h=4),
                     in1=bc(dO[:, c, 32 + h0:32 + h0 + 4].unsqueeze(2), [128, 4, 64]), op=ALU.mult)
                R.op("dve", "tensor_tensor", [r_tc1, r_tc2], [r_tc1], out=tc1, in0=tc1, in1=tc2, op=ALU.add)
                R.op("dve", "tensor_tensor", [rpy, r_tc1], [r_tc1], out=tc1,
                     in0=py[:, 0:256].rearrange("p (h q) -> p h q", h=4), in1=tc1, op=ALU.add)
                self.release(py)
                ygs = yg[:, c, g * 256:(g + 1) * 256]
                R.op("dve", "tensor_tensor", [r_tc1, r_zs2[pp]], [r_yg[c]], out=ygs, in0=tc1.rearrange("p h q -> p (h q)"),
                     in1=zs[pp], op=ALU.mult)
                R.op("act", "activation", [r_yg[c]], [r_sqj, r_ss], out=sqj, in_=ygs, func=AF.Square,
                     accum_out=ss[:, c, g:g + 1])
                if needs_upd(c):
                    state_update(c, 0, g, kx=pp)
                if u == 0 and c == (seq + 1) * cps - 1:
                    state_out(seq, 0, g)

            front(0)
            front2(0)
            for c in range(8):
                if c + 1 < 8:
                    front(c + 1)
                back(c // cps, c)
                if c + 1 < 8:
                    front2(c + 1)

        A.off = mark
        sst = A([128, 8], F32)
        ynT = A([128, 16, 512], BF16)
        r_sst, r_yn = Res("sst"), Res("ynT")
        self.stage(persist + [r_sst, r_yn])
        R.op("dve", "tensor_reduce", [r_ss], [r_sst], out=sst, in_=ss, axis=AX.X, op=ALU.add)
        R.op("act", "activation", [r_sst], [r_sst], out=sst, in_=sst, func=AF.Ln, scale=1.0 / 2048, bias=4.0 * EPS)
        R.op("act", "activation", [r_sst], [r_sst], out=sst, in_=sst, func=AF.Exp, scale=-0.5)
        for c in range(8):
            R.op("dve", "tensor_scalar", [r_yg[c], r_sst], [r_yg[c]], out=yg[:, c, :], in0=yg[:, c, :],
                 scalar1=sst[:, c:c + 1], scalar2=None, op0=ALU.mult)
        w_out = dr["ssd_w_out"][0]
        wA, rwA = self.wload(w_out[0:1024, :], 8, 1024)
        wB, rwB = self.wload(w_out[1024:2048, :], 8, 1024)
        gn = pc["ssd_norm_g"]
        for half in range(2):
            for cc in range(16):
                ps, rps = self.bank()
                psb = ps[:].bitcast(BF16)
                for q in range(4):
                    c = half * 4 + q
                    R.tr(psb[:, q * 128:(q + 1) * 128], yg[:, c, cc * 128:(cc + 1) * 128], identb[:], [r_yg[c], rc], [rps])
                R.op("act", "activation", [rps, self.r_prm], [r_yn], out=ynT[:, cc, :], in_=psb[:, 0:512], func=AF.Identity,
                     scale=self.prm[:, gn + cc:gn + cc + 1])
            tt = u * 2 + half
            ts = slice(tt * 512, (tt + 1) * 512)
            for dc in range(8):
                ps, rps = self.bank()
                for cc in range(16):
                    wv, rw = (wA, rwA) if cc < 8 else (wB, rwB)
                    R.mm(ps[:], wv[:, cc % 8, dc * 128:(dc + 1) * 128], ynT[:, cc, :], cc == 0, cc == 15, [rw, r_yn], [rps])
                R.op("dve", "scalar_tensor_tensor", [rps, self.r_mod, self.r_xT[dc][tt]], [self.r_xT[dc][tt]],
                     out=self.xT[:, dc, ts], in0=ps[:], scalar=self.mod[:, 16 + dc, u:u + 1],
                     in1=self.xT[:, dc, ts], op0=ALU.mult, op1=ALU.add)

    def attn_core(self, parts, part_res, vl, v_res, blocks, scale, negm, negm_res, pT, r_pT, fin, v_res_fn=None):
        R = self.R
        for (q0, nq, kts) in blocks:
            po, r_po = self.bank(hold=True)
            psm, r_psm = self.bank(hold=True)
            n = len(kts)

            def score(kt):
                pss, r_pss = self.bank()
                for pi, (kT, qT) in enumerate(parts):
                    R.mm(pss[:, 0:nq], kT[:, kt * 128:(kt + 1) * 128], qT[:, q0:q0 + nq], pi == 0, pi == len(parts) - 1,
                         part_res, [r_pss])
                return pss, r_pss

            nxt = score(kts[0])
            for idx, kt in enumerate(kts):
                pss, r_pss = nxt
                if idx + 1 < n:
                    nxt = score(kts[idx + 1])
                slot = self.pT_i % pT.shape[1]
                self.pT_i += 1
                R.op("act", "activation", [r_pss, negm_res], [r_pT[slot]], out=pT[:, slot, 0:nq], in_=pss[:, 0:nq],
                     func=AF.Exp, scale=scale, bias=negm)
                vr = v_res_fn(kt) if v_res_fn is not None else v_res
                R.mm(po[:, 0:nq], vl(kt), pT[:, slot, 0:nq], idx == 0, idx == n - 1, [vr, r_pT[slot]], [r_po])
                R.mm(psm[:, 0:nq], self.onesb[:], pT[:, slot, 0:nq], idx == 0, idx == n - 1, [self.r_const, r_pT[slot]],
                     [r_psm])
            self.attn_flush()
            self.pending_fin = (fin, q0, nq, po, r_po, psm, r_psm)

    def attn_flush(self):
        if getattr(self, "pending_fin", None) is not None:
            fin, q0, nq, po, r_po, psm, r_psm = self.pending_fin
            self.pending_fin = None
            fin(q0, nq, po, r_po, psm, r_psm)
            self.release(po)
            self.release(psm)

    def sq_bound(self, parts, part_res, ncols, out_col, tmp_sq, r_sq, red, r_red, p0=0):
        R = self.R
        ntile = (ncols + 511) // 512
        first = True
        for t in range(ntile):
            c0 = t * 512
            w = min(512, ncols - c0)
            ps, rps = self.bank()
            for pi, p in enumerate(parts):
                K = p.shape[0]
                R.op("act", "activation", part_res, [r_sq], out=tmp_sq[p0:p0 + K, 0:w], in_=p[:, c0:c0 + w], func=AF.Square)
                R.mm(ps[:, 0:w], self.onesb[p0:p0 + K, :], tmp_sq[p0:p0 + K, 0:w], pi == 0, pi == len(parts) - 1,
                     [r_sq, self.r_const], [rps])
            if first:
                R.op("dve", "tensor_reduce", [rps], [r_red], out=red[:, out_col:out_col + 1], in_=ps[:, 0:w], axis=AX.X,
                     op=ALU.max)
                first = False
            else:
                R.op("dve", "tensor_reduce", [rps], [r_red], out=red[:, 7:8], in_=ps[:, 0:w], axis=AX.X, op=ALU.max)
                R.op("dve", "tensor_tensor", [r_red], [r_red], out=red[:, out_col:out_col + 1],
                     in0=red[:, out_col:out_col + 1], in1=red[:, 7:8], op=ALU.max)

    def fnorm(self, src32, nch, w, g_col, tmp_sq, r_sq, rstd, r_rstd, lnt):
        R = self.R
        R.op("act", "activation", [self.r_f32], [r_sq], out=tmp_sq[:, 0:nch, 0:w], in_=src32[:, 0:nch, 0:w], func=AF.Square)
        ps, rps = self.bank()
        for m in range(nch):
            R.mm(ps[:, 0:w], self.onesb[:], tmp_sq[:, m, 0:w], m == 0, m == nch - 1, [r_sq, self.r_const], [rps])
        R.op("act", "activation", [rps], [r_rstd], out=lnt[:, 0:w], in_=ps[:, 0:w], func=AF.Ln, scale=1.0 / (nch * 128),
             bias=EPS)
        R.op("act", "activation", [r_rstd], [r_rstd], out=rstd[:, 0:w], in_=lnt[:, 0:w], func=AF.Exp, scale=-0.5)

    def mla(self, i, u):
        R = self.R
        dr = self.dram
        pc = self.pcol
        hT, r_h = self.hT, self.r_h
        r_hl = [r_h[u * 2], r_h[u * 2 + 1]]
        nseq, L = (NPS, LP) if u == 0 else (1, LS)
        koff = 0 if u == 0 else 256
        Tk = TU + koff
        identf, identb = self.identf, self.identb
        rc = self.r_const
        scale = 96.0 ** -0.5

        A = self._Alloc(self, 16384)
        cqn = A([128, 4, TU], BF16)
        ckvk = A([128, 2, 1280], BF16)
        krk = A([128, 1280], BF16)
        oT = A([128, 8, TU], BF16)
        WinS = A([128, 8, 32], BF16)
        mark = A.off
        r_cqn, r_ckvk, r_krk, r_WinS = Res("cqn"), Res("ckvk"), Res("krk"), Res("WinS")
        r_oT = [[Res("oT%d_%d" % (c, t)) for t in range(2)] for c in range(8)]
        persist = [r_cqn, r_ckvk, r_krk] + [r for rr in r_oT for r in rr]

        f32 = A([128, 4, 512], F32)
        sq = A([128, 4, 512], BF16)
        lnt = A([128, 512], F32)
        rstd = A([128, 512], F32)
        ck32 = A([128, 2, 512], F32)
        krf = A([32, 512], F32)
        kt1 = A([32, 512], F32)
        kt2 = A([32, 512], F32)
        rope = A([32, 2, TU], F32)
        ost = A([128, 256], F32)
        ost2 = A([128, 32], F32)
        cst = A([128, 2, 256], F32)
        cst2 = A([128, 2, 32], F32)
        self.r_f32 = Res("f32")
        r_sq, r_rstd, r_ck32, r_krf, r_kt, r_rope = Res("sq"), Res("rstd"), Res("ck32"), Res("krf"), Res("kt"), Res("rope")
        r_ost, r_ost2, r_cst = Res("ost"), Res("ost2"), Res("cst")
        self.stage(r_hl + persist + [r_WinS, self.r_f32, r_sq, r_rstd, r_ck32, r_krf, r_kt, r_rope, r_ost, r_ost2, r_cst])
        R.op("dve", "memset", [], [r_krk], ap=krk, constant=0.0)
        Win, rWin = self.wload(dr["mla_w_in"][0], 8, 800)
        R.op("dve", "tensor_copy", [rWin], [r_WinS], out=WinS[:, :, 0:16], in_=Win[:, :, 784:800])
        R.op("dve", "tensor_copy", [rWin], [r_WinS], out=WinS[:, :, 16:32], in_=Win[:, :, 768:784])
        if u == 1:
            R.dma("sp", rope, dr["rope_cs"].rearrange("a r t -> r a t"), [], [r_rope])
            R.dma("sp", cst, dr["c_ckv"].rearrange("(a p) f -> p a f", p=128), [], [r_cst])
            R.dma("sp", cst2, dr["c_kr"].rearrange("(a p) f -> p a f", p=128), [], [r_cst])
            for a in range(2):
                ps, rps = self.bank()
                for m in range(2):
                    R.tr(ps[:, m * 128:(m + 1) * 128], cst[:, a, m * 128:(m + 1) * 128], identf[:], [r_cst, rc], [rps])
                R.op("dve", "tensor_copy", [rps], [r_ckvk], out=ckvk[:, :, a * 128:(a + 1) * 128],
                     in_=ps[:, 0:256].rearrange("p (m t) -> p m t", m=2))
                ps, rps = self.bank()
                R.tr(ps[0:32, 0:128], cst2[:, a, :], identf[:], [r_cst, rc], [rps])
                R.op("dve", "tensor_copy", [rps], [r_krk], out=krk[0:32, a * 128:(a + 1) * 128], in_=ps[0:32, 0:128])
        gq, gkv = pc["mla_q_norm_g"], pc["mla_kv_norm_g"]
        for tt in range(2):
            ts = slice(tt * 512, (tt + 1) * 512)
            for m in range(4):
                ps, rps = self.bank()
                for kc in range(8):
                    R.mm(ps[:], Win[:, kc, m * 128:(m + 1) * 128], hT[:, kc, ts], kc == 0, kc == 7, [rWin, r_hl[tt]], [rps])
                self.copy(self.evac_engine(), f32[:, m, :], ps[:], [rps], [self.r_f32])
            self.fnorm(f32, 4, 512, gq, sq, r_sq, rstd, r_rstd, lnt)
            for m in range(4):
                R.op("dve", "scalar_tensor_tensor", [self.r_f32, r_rstd, self.r_prm], [r_cqn], out=cqn[:, m, ts],
                     in0=f32[:, m, :], scalar=self.prm[:, gq + m:gq + m + 1], in1=rstd, op0=ALU.mult, op1=ALU.mult)
            for m in range(2):
                ps, rps = self.bank()
                for kc in range(8):
                    R.mm(ps[:], Win[:, kc, 512 + m * 128:512 + (m + 1) * 128], hT[:, kc, ts], kc == 0, kc == 7,
                         [rWin, r_hl[tt]], [rps])
                self.copy(self.evac_engine(), f32[:, m, :], ps[:], [rps], [self.r_f32])
            self.fnorm(f32, 2, 512, gkv, sq, r_sq, rstd, r_rstd, lnt)
            for m in range(2):
                R.op("dve", "scalar_tensor_tensor", [self.r_f32, r_rstd, self.r_prm], [r_ck32], out=ck32[:, m, :],
                     in0=f32[:, m, :], scalar=self.prm[:, gkv + m:gkv + m + 1], in1=rstd, op0=ALU.mult, op1=ALU.mult)
            R.op("act", "activation", [r_ck32], [r_ckvk], out=ckvk[:, :, koff + tt * 512:koff + (tt + 1) * 512], in_=ck32,
                 func=AF.Identity)
            ps, rps = self.bank()
            for kc in range(8):
                R.mm(ps[0:32, :], Win[:, kc, 768:800], hT[:, kc, ts], kc == 0, kc == 7, [rWin, r_hl[tt]], [rps])
            R.op("dve", "tensor_copy", [rps], [r_krf], out=krf, in_=ps[0:32, :])
            kdst = krk[0:32, koff + tt * 512:koff + (tt + 1) * 512]
            if u == 0:
                R.op("act", "activation", [r_krf], [r_krk], out=kdst, in_=krf, func=AF.Identity)
            else:
                ps2, rps2 = self.bank()
                for kc in range(8):
                    R.mm(ps2[0:32, :], WinS[:, kc, :], hT[:, kc, ts], kc == 0, kc == 7, [r_WinS, r_hl[tt]], [rps2])
                R.op("dve", "tensor_tensor", [r_krf, r_rope], [r_kt], out=kt1, in0=krf, in1=rope[:, 0, ts], op=ALU.mult)
                R.op("dve", "tensor_tensor", [rps2, r_rope], [r_kt], out=kt2, in0=ps2[0:32, :], in1=rope[:, 1, ts],
                     op=ALU.mult)
                R.op("dve", "tensor_tensor", [r_kt], [r_krk], out=kdst, in0=kt1, in1=kt2, op=ALU.add)
            if u == 0:
                for q in range(4):
                    tok0 = tt * 512 + q * 128
                    seq, t0 = tok0 // LP, tok0 % LP
                    ps, rps = self.bank()
                    for m in range(2):
                        R.tr(ps[:, m * 128:(m + 1) * 128], ck32[:, m, q * 128:(q + 1) * 128], identf[:], [r_ck32, rc], [rps])
                    R.op("dve", "tensor_copy", [rps], [r_ost], out=ost, in_=ps[:, 0:256])
                    R.dma("sp", dr["o_ckv"][seq, t0:t0 + 128, :], ost, [r_ost], [])
                    ps, rps = self.bank()
                    R.tr(ps[:, 0:32], krf[:, q * 128:(q + 1) * 128], identf[0:32, 0:32], [r_krf, rc], [rps])
                    R.op("dve", "tensor_copy", [rps], [r_ost2], out=ost2, in_=ps[:, 0:32])
                    R.dma("sp", dr["o_kr"][seq, t0:t0 + 128, :], ost2, [r_ost2], [])

        A.off = mark
        WuqS = A([128, 4, 16, 32], BF16)
        qn_ = [A([128, TU], BF16) for _ in range(2)]
        qr_ = [A([128, TU], BF16) for _ in range(2)]
        qa_ = [A([32, TU], F32) for _ in range(2)]
        qb_ = [A([32, TU], F32) for _ in range(2)]
        kn_l = [A([128, 1280], BF16) for _ in range(2)]
        sqt_ = [A([64, 512], BF16) for _ in range(2)]
        red_ = [A([128, 8], F32) for _ in range(2)]
        negm = A([128, 2], F32)
        vpair_ = [A([128, 10, 128], BF16) for _ in range(2)]
        NS = 4
        pT = A([128, NS, 512], BF16)
        rec = A([128, 512], F32)
        rope2 = A([32, 2, TU], F32)
        r_WuqS, r_rec, r_rope2 = Res("WuqS"), Res("rec"), Res("rope2")
        r_qn_ = [Res("qn0"), Res("qn1")]
        r_qr_ = [Res("qr0"), Res("qr1")]
        r_qab_ = [Res("qab0"), Res("qab1")]
        r_kn_ = [Res("kn0"), Res("kn1")]
        r_sqt_ = [Res("sqt0"), Res("sqt1")]
        r_red_ = [Res("red0"), Res("red1")]
        r_negm_ = [Res("negm0"), Res("negm1")]
        r_vp_ = [Res("vp0"), Res("vp1")]
        r_pT = [Res("pT%d" % k) for k in range(NS)]
        self.stage(persist + [r_WuqS, r_rec, r_rope2] + r_qn_ + r_qr_ + r_qab_ + r_kn_ + r_sqt_ + r_red_ + r_negm_ + r_vp_ + r_pT)
        if u == 1:
            R.dma("sp", rope2, dr["rope_cs"].rearrange("a r t -> r a t"), [], [r_rope2])
        for k in range(2):
            R.op("dve", "memset", [], [r_qn_[k]], ap=qn_[k], constant=0.0)
            R.op("dve", "memset", [], [r_qr_[k]], ap=qr_[k], constant=0.0)
            R.op("dve", "memset", [], [r_kn_[k]], ap=kn_l[k], constant=0.0)
        Wuq, rWuq = self.wload(dr["mla_w_uq"][0], 4, 1536)
        Wukv, rWukv = self.wload(dr["mla_w_ukv"][0], 2, 2048)
        Wuq4 = Wuq.rearrange("p k (h d) -> p k h d", d=96)
        for kc in range(4):
            R.op("dve", "tensor_copy", [rWuq], [r_WuqS], out=WuqS[:, kc, :, 0:16], in_=Wuq4[:, kc, :, 80:96])
            R.op("dve", "tensor_copy", [rWuq], [r_WuqS], out=WuqS[:, kc, :, 16:32], in_=Wuq4[:, kc, :, 64:80])
        ktiles = [(0, 512), (512, 512), (1024, Tk - 1024)] if Tk > 1024 else [(0, 512), (512, 512)]
        if u == 1:
            blocks = [(0, 512, list(range(10))), (512, 512, list(range(10)))]
        else:
            blocks = [(s * 256, 256, [2 * s, 2 * s + 1]) for s in range(4)]
        Wv = Wukv.rearrange("p k (h two d) -> p k h two d", two=2, d=64)
        nkt = Tk // 128

        def prep_pair(a):
            vpair, r_vp = vpair_[a % 2], r_vp_[a % 2]
            for k0 in range(0, nkt, 4):
                ps, rps = self.bank()
                nq4 = min(4, nkt - k0)
                for q in range(nq4):
                    kt = k0 + q
                    for kc in range(2):
                        R.mm(ps[:, q * 128:(q + 1) * 128].rearrange("p (h d) -> p h d", h=2),
                             ckvk[:, kc, kt * 128:(kt + 1) * 128], Wv[:, kc, 2 * a:2 * a + 2, 1, :], kc == 0, kc == 1,
                             [r_ckvk, rWukv], [rps])
                self.copy(self.evac_engine(), vpair[:, k0:k0 + nq4, :],
                          ps[:, 0:nq4 * 128].rearrange("p (q d) -> p q d", d=128), [rps], [r_vp])

        def prep(h):
            hp = h % 2
            qn, qr, qa, qb, kn, sqt, red = qn_[hp], qr_[hp], qa_[hp], qb_[hp], kn_l[hp], sqt_[hp], red_[hp]
            r_qn, r_qr, r_qab, r_kn, r_sqt, r_red, r_negm = (r_qn_[hp], r_qr_[hp], r_qab_[hp], r_kn_[hp], r_sqt_[hp],
                                                             r_red_[hp], r_negm_[hp])
            for (c0, w) in ktiles:
                ps, rps = self.bank()
                for kc in range(2):
                    R.mm(ps[0:64, 0:w], Wukv[:, kc, h * 128:h * 128 + 64], ckvk[:, kc, c0:c0 + w], kc == 0, kc == 1,
                         [rWukv, r_ckvk], [rps])
                self.copy(self.evac_engine(), kn[0:64, c0:c0 + w], ps[0:64, 0:w], [rps], [r_kn])
            for tt in range(2):
                ts = slice(tt * 512, (tt + 1) * 512)
                ps, rps = self.bank()
                for kc in range(4):
                    R.mm(ps[0:64, :], Wuq[:, kc, h * 96:h * 96 + 64], cqn[:, kc, ts], kc == 0, kc == 3, [rWuq, r_cqn], [rps])
                self.copy(self.evac_engine(), qn[0:64, ts], ps[0:64, :], [rps], [r_qn])
                ps, rps = self.bank()
                for kc in range(4):
                    R.mm(ps[0:32, :], Wuq[:, kc, h * 96 + 64:h * 96 + 96], cqn[:, kc, ts], kc == 0, kc == 3, [rWuq, r_cqn],
                         [rps])
                if u == 0:
                    self.copy(self.evac_engine(), qr[0:32, ts], ps[0:32, :], [rps], [r_qr])
                else:
                    ps2, rps2 = self.bank()
                    for kc in range(4):
                        R.mm(ps2[0:32, :], WuqS[:, kc, h, :], cqn[:, kc, ts], kc == 0, kc == 3, [r_WuqS, r_cqn], [rps2])
                    R.op("dve", "tensor_tensor", [rps, r_rope2], [r_qab], out=qa[:, ts], in0=ps[0:32, :],
                         in1=rope2[:, 0, ts], op=ALU.mult)
                    R.op("dve", "tensor_tensor", [rps2, r_rope2], [r_qab], out=qb[:, ts], in0=ps2[0:32, :],
                         in1=rope2[:, 1, ts], op=ALU.mult)
                    R.op("dve", "tensor_tensor", [r_qab], [r_qr], out=qr[0:32, ts], in0=qa[:, ts], in1=qb[:, ts], op=ALU.add)
            self.sq_bound([qn[0:64, :], qr[0:32, :]], [r_qn, r_qr], TU, 0, sqt, r_sqt, red, r_red)
            self.sq_bound([kn[0:64, 0:Tk], krk[0:32, 0:Tk]], [r_kn, r_krk], Tk, 1, sqt, r_sqt, red, r_red)
            R.op("dve", "tensor_tensor", [r_red], [r_red], out=red[:, 2:3], in0=red[:, 0:1], in1=red[:, 1:2], op=ALU.add)
            R.op("dve", "tensor_scalar", [r_red], [r_negm], out=negm[:, hp:hp + 1], in0=red[:, 2:3],
                 scalar1=-0.5 * scale, scalar2=None, op0=ALU.mult)

        def attn(h):
            a, hp = h // 2, h % 2
            rows = slice(hp * 64, (hp + 1) * 64)
            vpair, r_vp = vpair_[a % 2], r_vp_[a % 2]
            qn, qr, kn = qn_[hp], qr_[hp], kn_l[hp]

            def fin(q0, nq, po, r_po, psm, r_psm):
                R.op("dve", "reciprocal", [r_psm], [r_rec], out=rec[rows, 0:nq], in_=psm[rows, 0:nq])
                R.op("dve", "tensor_tensor", [r_po, r_rec], [r_oT[a][q0 // 512]], out=oT[rows, a, q0:q0 + nq],
                     in0=po[rows, 0:nq], in1=rec[rows, 0:nq], op=ALU.mult)

            self.attn_core([(kn, qn), (krk, qr)], [r_kn_[hp], r_qn_[hp], r_krk, r_qr_[hp]], lambda kt: vpair[:, kt, :], r_vp,
                           blocks, scale, negm[:, hp:hp + 1], r_negm_[hp], pT, r_pT, fin)

        prep_pair(0)
        prep(0)
        for h in range(16):
            if h + 1 < 16:
                if (h + 1) % 2 == 0:
                    prep_pair((h + 1) // 2)
                prep(h + 1)
            attn(h)
        self.attn_flush()

        Wo, rWo = self.wload(dr["mla_w_o"][0], 8, 1024)
        for half in range(2):
            tt = u * 2 + half
            ts = slice(tt * 512, (tt + 1) * 512)
            ls = slice(half * 512, (half + 1) * 512)
            for dc in range(8):
                ps, rps = self.bank()
                for cc in range(8):
                    R.mm(ps[:], Wo[:, cc, dc * 128:(dc + 1) * 128], oT[:, cc, ls], cc == 0, cc == 7, [rWo, r_oT[cc][half]],
                         [rps])
                R.op("dve", "scalar_tensor_tensor", [rps, self.r_mod, self.r_xT[dc][tt]], [self.r_xT[dc][tt]],
                     out=self.xT[:, dc, ts], in0=ps[:], scalar=self.mod[:, 16 + dc, u:u + 1],
                     in1=self.xT[:, dc, ts], op0=ALU.mult, op1=ALU.add)

    def mlstm(self, i, u):
        R = self.R
        nc = self.nc
        dr = self.dram
        pc = self.pcol
        hT, r_h = self.hT, self.r_h
        r_hl = [r_h[u * 2], r_h[u * 2 + 1]]
        nseq, L = (NPS, LP) if u == 0 else (1, LS)
        cps = L // 128
        identf, identb, onesf = self.identf, self.identb, self.onesf
        rc = self.r_const
        bc = lambda ap, shape: ap.to_broadcast(shape)
        w_up = dr["mlstm_w_up"][0]
        if not hasattr(self, "zsp"):
            self.zsp = nc.dram_tensor("zsp", [128, 16, TU], BF16).ap()
            self.r_zsp = Res("zsp")
        zsp, r_zsp = self.zsp, self.r_zsp

        A = self._Alloc(self, 0)
        hTv = A([128, 8, TU], BF16)
        xmT = A([128, 16, TU], BF16)
        xcT = A([128, 16, TU], BF16)
        e_tok = A([128, 8, 16], F32)
        emc_tok = A([128, 8, 16], F32)
        iwb = A([128, 8, 16], F32)
        mark = A.off
        r_xm = [Res("xm%d" % c) for c in range(16)]
        r_xc = [Res("xc%d" % c) for c in range(16)]
        r_tok = Res("mtok")
        persist = r_xm + r_xc + [r_tok]

        pre_ = [A([128, nseq, L + 4], F32) for _ in range(2)]
        acc = A([128, TU], F32)
        zst = [A([128, TU], BF16) for _ in range(2)]
        r_pre_, r_acc, r_g = [Res("pre0"), Res("pre1")], Res("acc"), Res("gates")
        r_zst = [Res("zst0"), Res("zst1")]
        self.stage(r_hl + persist + r_pre_ + [r_acc] + r_zst)
        for k in range(2):
            R.op("dve", "memset", [], [r_pre_[k]], ap=pre_[k], constant=0.0)
        cw, cb = pc["mlstm_conv_w"], pc["mlstm_conv_b"]
        for blk in range(2):
            W, rW = self.wload(w_up[:, blk * 1024:(blk + 1) * 1024], 8, 1024)
            for cl in range(8):
                cc = blk * 8 + cl
                pre, r_pre = pre_[cc % 2], r_pre_[cc % 2]
                for tt in range(2):
                    ps, rps = self.bank()
                    for kc in range(8):
                        R.mm(ps[:], W[:, kc, cl * 128:(cl + 1) * 128], hT[:, kc, tt * 512:(tt + 1) * 512], kc == 0, kc == 7,
                             [rW, r_hl[tt]], [rps])
                    if u == 0:
                        R.op("act", "activation", [rps], [r_pre], out=pre[:, 2 * tt:2 * tt + 2, 2:2 + L],
                             in_=ps[:].rearrange("p (s t) -> p s t", s=2), func=AF.Identity)
                    else:
                        R.op("act", "activation", [rps], [r_pre], out=pre[:, 0, 2 + tt * 512:2 + (tt + 1) * 512], in_=ps[:],
                             func=AF.Identity)
                    R.op("dve", "tensor_copy", [rps], [r_xm[cc]], out=xmT[:, cc, tt * 512:(tt + 1) * 512], in_=ps[:])
                acc3 = acc.rearrange("p (s t) -> p s t", s=nseq)
                R.op("dve", "tensor_scalar", [r_pre, self.r_prm], [r_acc], out=acc3, in0=pre[:, :, 0:L],
                     scalar1=self.prm[:, cw + cc:cw + cc + 1], scalar2=self.prm[:, cb + cc:cb + cc + 1], op0=ALU.mult,
                     op1=ALU.add)
                for k in range(1, 5):
                    R.op("dve", "scalar_tensor_tensor", [r_pre, self.r_prm, r_acc], [r_acc], out=acc3,
                         in0=pre[:, :, k:k + L], scalar=self.prm[:, cw + k * 16 + cc:cw + k * 16 + cc + 1], in1=acc3,
                         op0=ALU.mult, op1=ALU.add)
                R.op("act", "activation", [r_acc], [r_xc[cc]], out=xcT[:, cc, :], in_=acc, func=AF.Silu)
        for blk in range(2):
            W, rW = self.wload(w_up[:, 2048 + blk * 1024:2048 + (blk + 1) * 1024], 8, 1024)
            for cl in range(8):
                cc = blk * 8 + cl
                k = cc % 2
                for tt in range(2):
                    ps, rps = self.bank()
                    for kc in range(8):
                        R.mm(ps[:], W[:, kc, cl * 128:(cl + 1) * 128], hT[:, kc, tt * 512:(tt + 1) * 512], kc == 0, kc == 7,
                             [rW, r_hl[tt]], [rps])
                    R.op("act", "activation", [rps], [r_zst[k]], out=zst[k][:, tt * 512:(tt + 1) * 512], in_=ps[:], func=AF.Silu)
                R.dma("sp", zsp[:, cc, :], zst[k], [r_zst[k]], [r_zsp])
        A.off = mark
        Wg40 = A([128, 8, 2, 40], BF16)
        gb = A([40, 2], F32)
        GI = A([40, TU], F32)
        X = A([40, TU], F32)
        T1 = A([40, TU], F32)
        Bt = A([40, TU], F32)
        cm = Bt
        Mt = A([40, 8], F32)
        mint = A([40, 9], F32)
        amax = A([40, 8], F32)
        iw = A([40, 8], F32)
        mcur = A([40, 1], F32)
        mfin = A([40, 4], F32)
        xd = A([40, 8, 40], F32)
        self.stage(r_hl + persist + [r_g])
        RG = [r_g]
        Wg, rWg = self.wload(w_up[:, 4096:4128], 8, 32)
        R.op("dve", "memset", [], RG, ap=Wg40, constant=0.0)
        for (f, dst0, src0) in ((0, 0, 0), (0, 32, 16), (1, 0, 8), (1, 32, 24)):
            R.op("dve", "tensor_copy", [rWg] + RG, RG, out=Wg40[:, :, f, dst0:dst0 + 8], in_=Wg[:, :, src0:src0 + 8])
        R.op("dve", "memset", RG, RG, ap=gb, constant=0.0)
        gbd = dr["mlstm_gate_b"][0]
        for d in range(2):
            for f in range(2):
                R.dma("sp", gb[d * 32:d * 32 + 8, f:f + 1], gbd[d, f].rearrange("(h o) -> h o", o=1), RG, RG)
        for f, dst in ((0, GI), (1, X)):
            for tt in range(2):
                ps, rps = self.bank()
                for kc in range(8):
                    R.mm(ps[0:40, :], Wg40[:, kc, f, :], hT[:, kc, tt * 512:(tt + 1) * 512], kc == 0, kc == 7,
                         RG + [r_hl[tt]], [rps])
                R.op("dve", "tensor_scalar", [rps] + RG, RG, out=dst[:, tt * 512:(tt + 1) * 512], in0=ps[0:40, :],
                     scalar1=gb[:, f:f + 1], scalar2=None, op0=ALU.add)
        R.op("act", "activation", RG, RG, out=T1, in_=X, func=AF.Abs)
        R.op("act", "activation", RG, RG, out=T1, in_=T1, func=AF.Exp, scale=-1.0)
        R.op("act", "activation", RG, RG, out=T1, in_=T1, func=AF.Ln, bias=1.0)
        R.op("dve", "scalar_tensor_tensor", RG, RG, out=X, in0=X, scalar=0.0, in1=T1, op0=ALU.min, op1=ALU.subtract)
        cm3 = cm.rearrange("p (c t) -> p c t", t=128)
        R.op("dve", "memset", RG, RG, ap=cm, constant=1.0)
        R.op("dve", "memset", RG, RG, ap=cm3[:, :, 0:1], constant=0.0)
        R.op("dve", "tensor_tensor_scan", RG, RG, out=T1, data0=cm, data1=X, initial=0.0, op0=ALU.mult, op1=ALU.add)
        cum3 = T1.rearrange("p (c t) -> p c t", t=128)
        tot = cum3[:, :, 127:128]
        Bt3 = Bt.rearrange("p (c t) -> p c t", t=128)
        R.op("dve", "tensor_copy", RG, RG, out=Bt, in_=T1)
        R.op("dve", "tensor_tensor", RG, RG, out=Bt3[32:40], in0=bc(tot[32:40], [8, 8, 128]), in1=cum3[32:40], op=ALU.subtract)
        R.op("dve", "tensor_tensor", RG, RG, out=Bt[32:40, :], in0=Bt[32:40, :], in1=X[32:40, :], op=ALU.add)
        R.op("dve", "tensor_tensor", RG, RG, out=GI, in0=GI, in1=Bt, op=ALU.subtract)
        R.op("dve", "tensor_reduce", RG, RG, out=amax, in_=GI.rearrange("p (c t) -> p c t", t=128), axis=AX.X, op=ALU.max)
        R.op("dve", "memset", RG, RG, ap=mfin, constant=0.0)
        for d, rows in ((0, slice(0, 8)), (1, slice(32, 40))):
            for seq in range(nseq):
                if u == 0:
                    R.op("dve", "memset", RG, RG, ap=mcur[rows, :], constant=0.0)
                else:
                    R.dma("sp", mcur[rows, :], dr["st_m"][d].rearrange("(h o) -> h o", o=1), RG, RG)
                cl = list(range(seq * cps, (seq + 1) * cps))
                if d == 1:
                    cl = cl[::-1]
                for c in cl:
                    R.op("dve", "tensor_copy", RG, RG, out=mint[rows, c:c + 1], in_=mcur[rows, :])
                    R.op("dve", "tensor_tensor", RG, RG, out=Mt[rows, c:c + 1], in0=mcur[rows, :], in1=amax[rows, c:c + 1],
                         op=ALU.max)
                    R.op("dve", "tensor_tensor", RG, RG, out=mcur[rows, :], in0=Mt[rows, c:c + 1], in1=tot[rows, c, :],
                         op=ALU.add)
                R.op("dve", "tensor_copy", RG, RG, out=mfin[rows, seq:seq + 1], in_=mcur[rows, :])
                if u == 0:
                    R.dma("sp", dr["o_m"][seq, d].rearrange("(h o) -> h o", o=1), mfin[rows, seq:seq + 1], RG, [])
        R.op("dve", "memset", RG, RG, ap=amax, constant=0.0)
        R.op("dve", "tensor_tensor", RG, RG, out=iw[0:8, :], in0=mint[0:8, 0:8], in1=Mt[0:8, :], op=ALU.subtract)
        R.op("dve", "tensor_tensor", RG, RG, out=iw[32:40, :], in0=mint[32:40, 0:8], in1=Mt[32:40, :], op=ALU.subtract)
        R.op("dve", "memset", RG, RG, ap=xd, constant=0.0)
        for rows in (slice(0, 8), slice(32, 40)):
            R.op("act", "activation", RG, RG, out=iw[rows, :], in_=iw[rows, :], func=AF.Exp)
            R.op("dve", "tensor_scalar", RG, RG, out=amax[rows, :], in0=Mt[rows, :], scalar1=-1.0, scalar2=None, op0=ALU.mult)
            R.op("dve", "tensor_tensor", RG + [rc], RG, out=xd[rows], in0=bc(identf[rows, 0:40].unsqueeze(1), [8, 8, 40]),
                 in1=bc(iw[rows, :].unsqueeze(2), [8, 8, 40]), op=ALU.mult)
        ps, rps = self.bank()
        R.mm(ps[:, 0:320], onesf[0:40, :], xd.rearrange("p c h -> p (c h)"), True, True, RG + [rc], [rps])
        ps3 = ps[:, 0:320].rearrange("p (c h) -> p c h", h=40)
        R.op("dve", "tensor_copy", [rps], [r_tok], out=iwb[:, :, 0:8], in_=ps3[:, :, 0:8])
        R.op("dve", "tensor_copy", [rps], [r_tok], out=iwb[:, :, 8:16], in_=ps3[:, :, 32:40])
        for rows in (slice(0, 8), slice(32, 40)):
            for c in range(8):
                cs = slice(c * 128, (c + 1) * 128)
                R.op("act", "activation", RG, RG, out=GI[rows, cs], in_=GI[rows, cs], func=AF.Exp, bias=amax[rows, c:c + 1])
                R.op("act", "activation", RG, RG, out=Bt[rows, cs], in_=Bt[rows, cs], func=AF.Exp, scale=-1.0,
                     bias=amax[rows, c:c + 1])
        for c in range(8):
            cs = slice(c * 128, (c + 1) * 128)
            ps, rps = self.bank()
            R.tr(ps[:, 0:40], GI[:, cs], identf[0:40, 0:40], RG + [rc], [rps])
            R.tr(ps[:, 64:104], Bt[:, cs], identf[0:40, 0:40], RG + [rc], [rps])
            R.op("dve", "tensor_copy", [rps], [r_tok], out=e_tok[:, c, 0:8], in_=ps[:, 0:8])
            R.op("dve", "tensor_copy", [rps], [r_tok], out=e_tok[:, c, 8:16], in_=ps[:, 32:40])
            R.op("dve", "tensor_copy", [rps], [r_tok], out=emc_tok[:, c, 0:8], in_=ps[:, 64:72])
            R.op("dve", "tensor_copy", [rps], [r_tok], out=emc_tok[:, c, 8:16], in_=ps[:, 96:104])

        A.off = 16384 + 2 * 32768 + 3 * 512
        assert A.off == mark
        A.off = 0
        yTq = A([128, 4, TU], BF16)
        zsq = A([128, 4, TU], BF16)
        assert A.off <= 16384
        A.off = mark
        qT = A([128, TU], BF16)
        kT = A([128, TU], BF16)
        v_tok = A([128, 8, 257], BF16)
        k_tok = A([128, 8, 128], BF16)
        Cpb = A([128, 8, 257], BF16)
        _c0, _c1, _c2 = A([128, 257], F32), A([128, 257], F32), A([128, 257], F32)
        Cst2 = [[_c0, _c1], [_c2, _c1]]
        Cst = list(Cst2[0])
        Cp = [A([128, 257], BF16) for _ in range(2)]
        sT = [[A([128, 128], BF16) for _ in range(2)] for _ in range(2)]
        kw = A([128, 128], BF16)
        _hs = A([128, 256], F32)
        hs_ = [_hs, _hs]
        hn_ = [A([128, 256], BF16) for _ in range(2)]
        sml_ = [A([128, 8], F32) for _ in range(2)]
        u1 = A([128, 128], F32)
        r_yTq = [[Res("yTq%d_%d" % (c, t)) for t in range(2)] for c in range(4)]
        r_zsq, r_qT, r_kT = Res("zsq"), Res("qT"), Res("kT")
        r_vt = [Res("vt%d" % c) for c in range(8)]
        r_kt = [Res("kt%d" % c) for c in range(8)]
        r_Cpb = [Res("Cpb%d" % c) for c in range(8)]
        _r0, _r1, _r2 = Res("Cst00"), Res("Cst01"), Res("Cst10")
        r_Cst2 = [[_r0, _r1], [_r2, _r1]]
        r_Cst = list(r_Cst2[0])
        r_kw, r_u1 = Res("kw"), Res("u1")
        _rhs = Res("hs")
        r_hs_ = [_rhs, _rhs]
        r_hn_ = [Res("hn0"), Res("hn1")]
        r_sml_ = [Res("sml0"), Res("sml1")]
        r_Cp = [Res("Cp0"), Res("Cp1")]
        r_sT = [[Res("sT00"), Res("sT01")], [Res("sT10"), Res("sT11")]]
        self.stage(persist + [r for rr in r_yTq for r in rr] + [r_zsq, r_qT, r_kT] + r_vt + r_kt + r_Cpb + [_r0, _r1, _r2] +
                   r_Cp + [r_kw, r_u1, _rhs] + r_hn_ + r_sml_ + r_sT[0] + r_sT[1])
        R.op("dve", "memset", [], r_vt, ap=v_tok[:, :, 256:257], constant=1.0)
        gcol, scol = pc["mlstm_norm_g"], pc["mlstm_skip"]
        kscale = 128.0 ** -0.5

        def st_sel(seq, d):
            Cst[d] = Cst2[seq % 2][d]
            r_Cst[d] = r_Cst2[seq % 2][d]

        def st_init(d, h):
            if u == 0:
                R.op("dve", "memset", [], [r_Cst[d]], ap=Cst[d], constant=0.0)
            else:
                R.dma("sp", Cst[d][:, 0:256], dr["st_C"][d, h], [], [r_Cst[d]])
                R.dma("sp", Cst[d][:, 256:257], dr["st_n"][d, h].rearrange("(p o) -> p o", o=1), [], [r_Cst[d]])

        def st_out(seq, d, h):
            R.dma("sp", dr["o_C"][seq, d, h], Cst[d][:, 0:256], [r_Cst[d]], [])
            R.dma("sp", dr["o_n"][seq, d, h].rearrange("(p o) -> p o", o=1), Cst[d][:, 256:257], [r_Cst[d]], [])

        def st_decay(c, d, h):
            col = d * 8 + h
            R.op("dve", "tensor_scalar", [r_Cst[d], r_tok], [r_Cst[d]], out=Cst[d], in0=Cst[d],
                 scalar1=iwb[:, c, col:col + 1], scalar2=None, op0=ALU.mult)

        def st_update(c, d, h):
            col = d * 8 + h
            R.op("dve", "tensor_scalar", [r_kt[c], r_tok], [r_kw], out=kw, in0=k_tok[:, c, :],
                 scalar1=e_tok[:, c, col:col + 1], scalar2=None, op0=ALU.mult)
            ps, rps = self.bank()
            R.mm(ps[:, 0:257], kw, v_tok[:, c, :], True, True, [r_kw, r_vt[c]], [rps])
            R.op("dve", "tensor_tensor", [rps, r_Cst[d]], [r_Cst[d]], out=Cst[d], in0=ps[:, 0:257], in1=Cst[d], op=ALU.add)

        for pair in range(4):
            R.dma("sp", zsq, zsp[:, pair * 4:(pair + 1) * 4, :], [r_zsp], [r_zsq])
            for hp in range(2):
                h = pair * 2 + hp
                W, rW = self.wload_multi([dr["mlstm_w_q"][0][:, h * 128:(h + 1) * 128],
                                          dr["mlstm_w_k"][0][:, h * 128:(h + 1) * 128],
                                          dr["mlstm_w_v"][0][:, h * 256:(h + 1) * 256]], 16)
                for tt in range(2):
                    ts = slice(tt * 512, (tt + 1) * 512)
                    ps, rps = self.bank()
                    for kc in range(16):
                        R.mm(ps[:], W[:, kc, 0:128], xcT[:, kc, ts], kc == 0, kc == 15, [rW, r_xc[kc]], [rps])
                    self.copy(self.evac_engine(), qT[:, ts], ps[:], [rps], [r_qT])
                    ps, rps = self.bank()
                    for kc in range(16):
                        R.mm(ps[:], W[:, kc, 128:256], xcT[:, kc, ts], kc == 0, kc == 15, [rW, r_xc[kc]], [rps])
                    R.op("act", "activation", [rps], [r_kT], out=kT[:, ts], in_=ps[:], func=AF.Identity, scale=kscale)
                for c in range(8):
                    cs = slice(c * 128, (c + 1) * 128)
                    ps, rps = self.bank()
                    for kc in range(16):
                        R.mm(ps[:, 0:256], xmT[:, kc, cs], W[:, kc, 256:512], kc == 0, kc == 15, [rW, r_xm[kc]], [rps])
                    self.copy(self.evac_engine(), v_tok[:, c, 0:256], ps[:, 0:256], [rps], [r_vt[c]])
                ps, rps = self.bank()
                psb = ps[:].bitcast(BF16)
                for c in range(8):
                    R.tr(psb[:, c * 128:(c + 1) * 128], kT[:, c * 128:(c + 1) * 128], identb[:], [r_kT, rc], [rps])
                R.op("dve", "tensor_copy", [rps], r_kt, out=k_tok.rearrange("p c d -> p (c d)"), in_=psb[:, 0:1024])
                for seq in range(nseq):
                    st_sel(seq, 1)
                    st_init(1, h)
                    for c in reversed(range(seq * cps, (seq + 1) * cps)):
                        st_decay(c, 1, h)
                        R.op("act", "activation", [r_Cst[1]], [r_Cpb[c]], out=Cpb[:, c, :], in_=Cst[1], func=AF.Identity)
                        if u == 0 or c > seq * cps:
                            st_update(c, 1, h)
                    if u == 0:
                        st_out(seq, 1, h)
                def F(c):
                    cs = slice(c * 128, (c + 1) * 128)
                    pp = c % 2
                    pss, rpss = self.bank()
                    R.mm(pss[:, 0:128], kT[:, cs], qT[:, cs], True, True, [r_kT, r_qT], [rpss])
                    for d, msk in ((0, self.mle), (1, self.mge)):
                        col = d * 8 + h
                        R.op("dve", "scalar_tensor_tensor", [rpss, r_tok, rc], [r_sT[pp][d]], out=sT[pp][d], in0=pss[:, 0:128],
                             scalar=e_tok[:, c, col:col + 1], in1=msk[:], op0=ALU.mult, op1=ALU.mult)

                def Astep(c):
                    seq = c // cps
                    pp = c % 2
                    if c == seq * cps:
                        st_sel(seq, 0)
                        st_init(0, h)
                    st_decay(c, 0, h)
                    R.op("act", "activation", [r_Cst[0]], [r_Cp[pp]], out=Cp[pp], in_=Cst[0], func=AF.Identity)
                    if u == 0 or c < (seq + 1) * cps - 1:
                        st_update(c, 0, h)
                    if u == 0 and c == (seq + 1) * cps - 1:
                        st_out(seq, 0, h)

                def B1(c):
                    cs = slice(c * 128, (c + 1) * 128)
                    pp = c % 2
                    hs, hn, sml, sqj = hs_[pp], hn_[pp], sml_[pp], hn_[pp]
                    r_hs, r_hn, r_sml = r_hs_[pp], r_hn_[pp], r_sml_[pp]
                    pn = []
                    for d in range(2):
                        p_, rp_ = self.bank(hold=True)
                        R.mm(p_[:, 0:257], sT[pp][d], v_tok[:, c, :], True, False, [r_sT[pp][d], r_vt[c]], [rp_])
                        if d == 0:
                            R.mm(p_[:, 0:257], qT[:, cs], Cp[pp], False, True, [r_qT, r_Cp[pp]], [rp_])
                        else:
                            R.mm(p_[:, 0:257], qT[:, cs], Cpb[:, c, :], False, True, [r_qT, r_Cpb[c]], [rp_])
                        pn.append((p_, rp_))
                    for d in range(2):
                        col = d * 8 + h
                        p_, rp_ = pn[d]
                        R.op("act", "activation", [rp_], [r_sml], out=sml[:, d:d + 1], in_=p_[:, 256:257], func=AF.Abs)
                        R.op("dve", "tensor_tensor", [r_sml, r_tok], [r_sml], out=sml[:, d:d + 1], in0=sml[:, d:d + 1],
                             in1=emc_tok[:, c, col:col + 1], op=ALU.max)
                        R.op("dve", "reciprocal", [r_sml], [r_sml], out=sml[:, 2 + d:3 + d], in_=sml[:, d:d + 1])
                    R.op("dve", "tensor_scalar", [pn[0][1], r_sml], [r_hs], out=hs, in0=pn[0][0][:, 0:256],
                         scalar1=sml[:, 2:3], scalar2=None, op0=ALU.mult)
                    R.op("dve", "scalar_tensor_tensor", [pn[1][1], r_sml, r_hs], [r_hs], out=hs, in0=pn[1][0][:, 0:256],
                         scalar=sml[:, 3:4], in1=hs, op0=ALU.mult, op1=ALU.add)
                    self.release(pn[0][0])
                    self.release(pn[1][0])
                    R.op("act", "activation", [r_hs], [r_hn, r_sml], out=sqj, in_=hs, func=AF.Square, accum_out=sml[:, 4:5])
                    R.op("act", "activation", [r_sml], [r_sml], out=sml[:, 5:6], in_=sml[:, 4:5], func=AF.Ln, scale=1.0 / 256,
                         bias=EPS)
                    R.op("act", "activation", [r_sml], [r_sml], out=sml[:, 5:6], in_=sml[:, 5:6], func=AF.Exp, scale=-0.5)
                    R.op("act", "activation", [r_hs, r_sml], [r_hn], out=hn, in_=hs, func=AF.Identity, scale=sml[:, 5:6])

                def B2(c):
                    pp = c % 2
                    hn, r_hn = hn_[pp], r_hn_[pp]
                    q4 = c % 4
                    if q4 == 0:
                        self.pt_cur = self.bank(hold=True)
                    pt, rpt = self.pt_cur
                    ptb = pt[:].bitcast(BF16)
                    for c2 in range(2):
                        R.tr(ptb[:, (c2 * 4 + q4) * 128:(c2 * 4 + q4 + 1) * 128], hn[:, c2 * 128:(c2 + 1) * 128], identb[:],
                             [r_hn, rc], [rpt])
                    if q4 == 3:
                        tl = c // 4
                        for c2 in range(2):
                            ch = 2 * h + c2
                            lc = 2 * hp + c2
                            for hf in range(4):
                                t2 = slice(tl * 512 + hf * 128, tl * 512 + (hf + 1) * 128)
                                R.op("dve", "tensor_scalar", [rpt, self.r_prm], [r_u1], out=u1,
                                     in0=ptb[:, c2 * 512 + hf * 128:c2 * 512 + (hf + 1) * 128],
                                     scalar1=self.prm[:, gcol + ch:gcol + ch + 1], scalar2=None, op0=ALU.mult)
                                R.op("dve", "scalar_tensor_tensor", [r_xc[ch], self.r_prm, r_u1], [r_u1], out=u1,
                                     in0=xcT[:, ch, t2], scalar=self.prm[:, scol + ch:scol + ch + 1], in1=u1, op0=ALU.mult,
                                     op1=ALU.add)
                                R.op("dve", "tensor_tensor", [r_u1, r_zsq], [r_yTq[lc][tl]], out=yTq[:, lc, t2], in0=u1,
                                     in1=zsq[:, lc, t2], op=ALU.mult)
                        self.release(pt)

                F(0)
                Astep(0)
                for c in range(8):
                    if c + 1 < 8:
                        F(c + 1)
                        Astep(c + 1)
                    B1(c)
                    if c > 0:
                        B2(c - 1)
                B2(7)
            Wd, rWd = self.wload(dr["mlstm_w_down"][0][pair * 512:(pair + 1) * 512, :], 4, 1024)
            for half in range(2):
                tt = u * 2 + half
                ts = slice(tt * 512, (tt + 1) * 512)
                ls = slice(half * 512, (half + 1) * 512)
                for dc in range(8):
                    ps, rps = self.bank()
                    for lc in range(4):
                        R.mm(ps[:], Wd[:, lc, dc * 128:(dc + 1) * 128], yTq[:, lc, ls], lc == 0, lc == 3,
                             [rWd, r_yTq[lc][half]], [rps])
                    R.op("dve", "scalar_tensor_tensor", [rps, self.r_mod, self.r_xT[dc][tt]], [self.r_xT[dc][tt]],
                         out=self.xT[:, dc, ts], in0=ps[:], scalar=self.mod[:, 16 + dc, u:u + 1],
                         in1=self.xT[:, dc, ts], op0=ALU.mult, op1=ALU.add)

    def diff(self, i, u):
        R = self.R
        dr = self.dram
        pc = self.pcol
        hT, r_h = self.hT, self.r_h
        r_hl = [r_h[u * 2], r_h[u * 2 + 1]]
        koff = 0 if u == 0 else 256
        Tk = TU + koff
        nkt = Tk // 128
        identf, identb, onesf = self.identf, self.identb, self.onesf
        rc = self.r_const
        scale = 64.0 ** -0.5
        lam_init = 0.8 - 0.6 * math.exp(-0.3 * i)
        wqkv = dr["diff_w_qkv"][0]

        A = self._Alloc(self, 16384)
        qT = A([128, 8, TU], BF16)
        kT = A([128, 8, 1280], BF16)
        v_tok = A([128, 10, 1024], BF16)
        lamt = A([128, 4], F32)
        gs = A([128, 1], F32)
        mark = A.off
        r_qT = [Res("qT%d" % m) for m in range(8)]
        r_kT = [Res("kT%d" % m) for m in range(8)]
        r_v = [Res("v%d" % k) for k in range(10)]
        r_lam = Res("lam")
        persist = r_qT + r_kT + r_v + [r_lam]

        rope = A([128, 2, TU], F32)
        qraw = A([128, 512], BF16)
        tA = A([128, 512], F32)
        tB = A([128, 512], F32)
        kvst = [A([128, 1024], F32) for _ in range(2)]
        perm = A([128, 128], BF16)
        lqk = A([64, 4], F32)
        r_rope, r_qraw, r_tA, r_tB, r_perm, r_lqk = Res("rope"), Res("qraw"), Res("tA"), Res("tB"), Res("perm"), Res("lqk")
        r_kvst = [Res("kvst0"), Res("kvst1")]
        self.stage(r_hl + persist + [r_rope, r_qraw, r_tA, r_tB, r_perm, r_lqk] + r_kvst)
        for k, nm in enumerate(["diff_lq1", "diff_lk1", "diff_lq2", "diff_lk2"]):
            R.dma("sp", lqk[:, k:k + 1], dr[nm][0].rearrange("(d o) -> d o", o=1), [], [r_lqk])
        R.op("dve", "tensor_tensor", [r_lqk], [r_lqk], out=lqk[:, 0:1], in0=lqk[:, 0:1], in1=lqk[:, 1:2], op=ALU.mult)
        R.op("dve", "tensor_tensor", [r_lqk], [r_lqk], out=lqk[:, 1:2], in0=lqk[:, 2:3], in1=lqk[:, 3:4], op=ALU.mult)
        ps, rps = self.bank()
        R.mm(ps[:, 0:2], onesf[0:64, :], lqk[:, 0:2], True, True, [r_lqk, rc], [rps])
        R.op("act", "activation", [rps], [r_lam], out=lamt[:, 0:2], in_=ps[:, 0:2], func=AF.Exp)
        R.op("dve", "tensor_tensor", [r_lam], [r_lam], out=lamt[:, 2:3], in0=lamt[:, 1:2], in1=lamt[:, 0:1], op=ALU.subtract)
        R.op("dve", "tensor_scalar", [r_lam], [r_lam], out=lamt[:, 2:3], in0=lamt[:, 2:3], scalar1=-lam_init, scalar2=None,
             op0=ALU.add)
        R.op("dve", "tensor_scalar", [self.r_prm], [r_lam], out=gs,
             in0=self.prm[:, pc["diff_subln_g"]:pc["diff_subln_g"] + 1], scalar1=1.0 - lam_init, scalar2=None,
             op0=ALU.mult)
        if u == 1:
            R.dma("sp", rope, dr["rope128"].rearrange("a r t -> r a t"), [], [r_rope])
            for (d0, s0) in ((0, 32), (32, 0), (64, 96), (96, 64)):
                R.op("dve", "tensor_copy", [rc], [r_perm], out=perm[:, d0:d0 + 32], in_=identb[:, s0:s0 + 32])
            ck = dr["c_dk"].rearrange("(a p) h d -> p a (h d)", p=128)
            cv = dr["c_dv"].rearrange("(a p) h d -> p a (h d)", p=128)
            for a in range(2):
                R.dma("sp", kvst[0], ck[:, a, :], [], [r_kvst[0]])
                for mh in range(2):
                    ps, rps = self.bank()
                    for q in range(4):
                        m = mh * 4 + q
                        R.tr(ps[:, q * 128:(q + 1) * 128], kvst[0][:, m * 128:(m + 1) * 128], identf[:], [r_kvst[0], rc],
                             [rps])
                    self.copy(self.evac_engine(), kT[:, mh * 4:(mh + 1) * 4, a * 128:(a + 1) * 128],
                              ps[:].rearrange("p (q t) -> p q t", q=4), [rps], r_kT[mh * 4:(mh + 1) * 4])
                R.dma("sp", kvst[1], cv[:, a, :], [], [r_kvst[1]])
                R.op("dve", "tensor_copy", [r_kvst[1]], [r_v[a]], out=v_tok[:, a, :], in_=kvst[1])

        def fm_proj(W, rW, dst, r_dst, col_off):
            for m in range(8):
                for tt in range(2):
                    ts = slice(tt * 512, (tt + 1) * 512)
                    ps, rps = self.bank()
                    for kc in range(8):
                        R.mm(ps[:], W[:, kc, m * 128:(m + 1) * 128], hT[:, kc, ts], kc == 0, kc == 7, [rW, r_hl[tt]], [rps])
                    od = dst[:, m, col_off + tt * 512:col_off + (tt + 1) * 512]
                    if u == 0:
                        self.copy(self.evac_engine(), od, ps[:], [rps], [r_dst[m]])
                    else:
                        R.op("act", "activation", [rps], [r_qraw], out=qraw, in_=ps[:], func=AF.Identity)
                        R.op("dve", "tensor_tensor", [rps, r_rope], [r_tA], out=tA, in0=ps[:], in1=rope[:, 0, ts], op=ALU.mult)
                        ps2, rps2 = self.bank()
                        R.mm(ps2[:], perm, qraw, True, True, [r_perm, r_qraw], [rps2])
                        R.op("dve", "tensor_tensor", [rps2, r_rope], [r_tB], out=tB, in0=ps2[:], in1=rope[:, 1, ts],
                             op=ALU.mult)
                        R.op("dve", "tensor_tensor", [r_tA, r_tB], [r_dst[m]], out=od, in0=tA, in1=tB, op=ALU.add)

        def tm_proj(W, rW, out_name, to_v):
            for c in range(8):
                cs = slice(c * 128, (c + 1) * 128)
                kk = c % 2
                for half in range(2):
                    ps, rps = self.bank()
                    for kc in range(8):
                        R.mm(ps[:], hT[:, kc, cs], W[:, kc, half * 512:(half + 1) * 512], kc == 0, kc == 7,
                             [rW, r_hl[c // 4]], [rps])
                    if to_v:
                        R.op("act", "activation", [rps], [r_v[koff // 128 + c]],
                             out=v_tok[:, koff // 128 + c, half * 512:(half + 1) * 512], in_=ps[:], func=AF.Identity)
                    if u == 0:
                        R.op("dve", "tensor_copy", [rps], [r_kvst[kk]], out=kvst[kk][:, half * 512:(half + 1) * 512], in_=ps[:])
                if u == 0:
                    seq, t0 = (c * 128) // LP, (c * 128) % LP
                    R.dma("sp", dr[out_name][seq, t0:t0 + 128].rearrange("t h d -> t (h d)"), kvst[kk], [r_kvst[kk]], [])

        Wq, rWq = self.wload(wqkv[:, 0:1024], 8, 1024)
        fm_proj(Wq, rWq, qT, r_qT, 0)
        Wk, rWk = self.wload(wqkv[:, 1024:2048], 8, 1024)
        fm_proj(Wk, rWk, kT, r_kT, koff)
        if u == 0:
            tm_proj(Wk, rWk, "o_dk", False)
        Wv, rWv = self.wload(wqkv[:, 2048:3072], 8, 1024)
        tm_proj(Wv, rWv, "o_dv", True)

        A.off = mark
        oT = self.av(0, [128, 8, TU], BF16)
        NS = 4
        pT = A([128, NS, 512], BF16)
        qz = [[A([128, TU], BF16) for _ in range(2)] for _ in range(2)]
        r_qz = [[Res("qz%d%d" % (a_, b_)) for b_ in range(2)] for a_ in range(2)]
        a1 = A([128, 512], F32)
        rec = A([128, 512], F32)
        o32 = A([128, 512], F32)
        sqo = A([128, 512], BF16)
        rstd = A([128, 512], F32)
        lnt = A([128, 512], F32)
        sqt = A([128, 512], BF16)
        red = A([128, 8], F32)
        negm = A([128, 2], F32)
        r_oT = [[Res("oT%d_%d" % (c, t)) for t in range(2)] for c in range(8)]
        r_pT = [Res("pT%d" % k) for k in range(NS)]
        r_a1, r_rec, r_o32, r_sqo, r_rstd, r_sqt, r_red, r_negm = (Res("a1"), Res("rec"), Res("o32"), Res("sqo"),
                                                                   Res("rstd"), Res("sqt"), Res("red"), Res("negm"))
        self.stage(persist + [r for rr in r_oT for r in rr] + r_pT + r_qz[0] + r_qz[1] +
                   [r_a1, r_rec, r_o32, r_sqo, r_rstd, r_sqt, r_red, r_negm])
        for a_ in range(2):
            for b_ in range(2):
                R.op("dve", "memset", [], [r_qz[a_][b_]], ap=qz[a_][b_], constant=0.0)
        if u == 1:
            blocks = [(0, 512, list(range(10))), (512, 512, list(range(10)))]
        else:
            blocks = [(s * 256, 256, [2 * s, 2 * s + 1]) for s in range(4)]
        for h in range(8):
            for br in range(2):
                rows = slice(br * 64, (br + 1) * 64)
                self.sq_bound([qT[rows, h, :]], [r_qT[h]], TU, 0, sqt, r_sqt, red, r_red, p0=br * 64)
                self.sq_bound([kT[rows, h, 0:Tk]], [r_kT[h]], Tk, 1, sqt, r_sqt, red, r_red, p0=br * 64)
                R.op("dve", "tensor_tensor", [r_red], [r_red], out=red[:, 2:3], in0=red[:, 0:1], in1=red[:, 1:2], op=ALU.add)
                R.op("dve", "tensor_scalar", [r_red], [r_negm], out=negm[:, br:br + 1], in0=red[:, 2:3],
                     scalar1=-0.5 * scale, scalar2=None, op0=ALU.mult)

            def fin1(q0, nq, po, r_po, psm, r_psm):
                R.op("dve", "reciprocal", [r_psm], [r_rec], out=rec[:, 0:nq], in_=psm[:, 0:nq])
                R.op("dve", "tensor_tensor", [r_po, r_rec], [r_a1], out=a1[:, 0:nq], in0=po[:, 0:nq], in1=rec[:, 0:nq],
                     op=ALU.mult)

            def fin2(q0, nq, po, r_po, psm, r_psm, h=h):
                R.op("dve", "reciprocal", [r_psm], [r_rec], out=rec[:, 0:nq], in_=psm[:, 0:nq])
                R.op("dve", "tensor_tensor", [r_po, r_rec], [r_o32], out=o32[:, 0:nq], in0=po[:, 0:nq], in1=rec[:, 0:nq],
                     op=ALU.mult)
                R.op("dve", "scalar_tensor_tensor", [r_o32, r_a1, r_lam], [r_o32], out=o32[:, 0:nq], in0=o32[:, 0:nq],
                     scalar=lamt[:, 2:3], in1=a1[:, 0:nq], op0=ALU.mult, op1=ALU.add)
                R.op("act", "activation", [r_o32], [r_sqo], out=sqo[:, 0:nq], in_=o32[:, 0:nq], func=AF.Square)
                ps, rps = self.bank()
                R.mm(ps[:, 0:nq], self.onesb[:], sqo[:, 0:nq], True, True, [r_sqo, rc], [rps])
                R.op("act", "activation", [rps], [r_rstd], out=lnt[:, 0:nq], in_=ps[:, 0:nq], func=AF.Ln, scale=1.0 / 128,
                     bias=EPS)
                R.op("act", "activation", [r_rstd], [r_rstd], out=rstd[:, 0:nq], in_=lnt[:, 0:nq], func=AF.Exp, scale=-0.5)
                R.op("dve", "scalar_tensor_tensor", [r_o32, r_rstd, r_lam], [r_oT[h][q0 // 512]], out=oT[:, h, q0:q0 + nq],
                     in0=o32[:, 0:nq], scalar=gs[:, 0:1], in1=rstd[:, 0:nq], op0=ALU.mult, op1=ALU.mult)

            hq = h % 2
            for br in range(2):
                rows = slice(br * 64, (br + 1) * 64)
                self.copy(self.evac_engine(), qz[hq][br][rows, :], qT[rows, h, :], [r_qT[h]], [r_qz[hq][br]])
            for blk in blocks:
                for br, fin in ((0, fin1), (1, fin2)):
                    self.attn_core([(kT[:, h, :], qz[hq][br])], [r_kT[h], r_qz[hq][br]],
                                   lambda kt, h=h: v_tok[:, kt, h * 128:(h + 1) * 128], None, [blk], scale,
                                   negm[:, br:br + 1], r_negm, pT, r_pT, fin, v_res_fn=lambda kt: r_v[kt])

        self.attn_flush()

        Wo, rWo = self.wload(dr["diff_w_o"][0], 8, 1024)
        for half in range(2):
            tt = u * 2 + half
            ts = slice(tt * 512, (tt + 1) * 512)
            ls = slice(half * 512, (half + 1) * 512)
            for dc in range(8):
                ps, rps = self.bank()
                for cc in range(8):
                    R.mm(ps[:], Wo[:, cc, dc * 128:(dc + 1) * 128], oT[:, cc, ls], cc == 0, cc == 7, [rWo, r_oT[cc][half]],
                         [rps])
                R.op("dve", "scalar_tensor_tensor", [rps, self.r_mod, self.r_xT[dc][tt]], [self.r_xT[dc][tt]],
                     out=self.xT[:, dc, ts], in0=ps[:], scalar=self.mod[:, 16 + dc, u:u + 1],
                     in1=self.xT[:, dc, ts], op0=ALU.mult, op1=ALU.add)


def rope_tables(d):
    half = d // 2
    nf = half // 2
    t = np.arange(LS)
    pos_r = (t // 64).astype(np.float32)
    pos_c = (t % 64).astype(np.float32)
    inv = (10000.0 ** (-np.arange(nf, dtype=np.float32) / nf)).astype(np.float32)
    ang = np.concatenate([pos_r[:, None] * inv, pos_c[:, None] * inv], axis=-1).astype(np.float32)
    cos = np.cos(ang).astype(np.float32).T
    sin = np.sin(ang).astype(np.float32).T
    out = np.zeros((2, d, LS), np.float32)
    out[0, :half] = cos
    out[0, half:] = cos
    out[1, :half] = -sin
    out[1, half:] = sin
    return out


def make_in_maps(inputs):
    f = lambda a: np.ascontiguousarray(np.asarray(a, dtype=np.float32))
    shared = {}
    for name, shape in IN_SPECS[12:]:
        if name in ("rope_cs", "rope128"):
            continue
        shared[name] = f(inputs[name]).reshape(shape)
    shared["rope_cs"] = rope_tables(32)
    r64 = rope_tables(64)
    shared["rope128"] = np.ascontiguousarray(np.concatenate([r64, r64], axis=1))
    shared["c_ctx"] = f(inputs["c_ctx"])
    maps = []
    for c in range(N_CORES):
        s = c % 2
        m = dict(shared)
        m["xp"] = f(inputs["x_prompt"][c * NPS:(c + 1) * NPS])
        m["xs"] = f(inputs["x_sample"][s])
        m["st_ssd"] = f(inputs["state_ssd"][s, 0])
        m["c_ckv"] = f(inputs["cache_mla_ckv"][s, 0])
        m["c_kr"] = f(inputs["cache_mla_krope"][s, 0])
        m["st_C"] = f(inputs["state_mlstm_C"][s, 0])
        m["st_n"] = f(inputs["state_mlstm_n"][s, 0])
        m["st_m"] = f(inputs["state_mlstm_m"][s, 0])
        m["c_dk"] = f(inputs["cache_diff_k"][s, 0])
        m["c_dv"] = f(inputs["cache_diff_v"][s, 0])
        m["c_s"] = f(inputs["c"][s])
        maps.append(m)
    return maps


def assemble(results):
    cat = lambda k: np.concatenate([r[k] for r in results], axis=0)
    y_prompt = cat("yp")
    y_sample = np.stack([results[0]["ys"], results[1]["ys"]], axis=0)
    exp1 = lambda a: a[:, None]
    return (y_prompt, y_sample, exp1(cat("o_ssd")), exp1(cat("o_ckv")), exp1(cat("o_kr")),
            exp1(cat("o_C")), exp1(cat("o_n")), exp1(cat("o_m")), exp1(cat("o_dk")), exp1(cat("o_dv")))


_CACHE = {}


def kernel(**inputs):
    if "nc" not in _CACHE:
        b = Builder()
        _CACHE["nc"] = b.build()
    nc = _CACHE["nc"]
    maps = make_in_maps(inputs)
    res = run_bass_kernel_spmd(nc, maps, core_ids=list(range(N_CORES)))
    return assemble(res.results)
```
